# Optimizing a Trainium2 kernel written in Bass

```python
import jax, jax.numpy as jnp
from jax import lax
import numpy as np

D_MODEL = 1024
BATCH = 8
SEQ = 4096
DEPTH = 4

GRID_W = 64
CTX_LEN = 256
HEAD_DIM = 64
A_HEADS = (D_MODEL // HEAD_DIM) // 2
A_KV_HEADS = 2
A_WIDTH = A_HEADS * HEAD_DIM
B_HEADS = (D_MODEL // HEAD_DIM) // 2
B_WIDTH = B_HEADS * HEAD_DIM
DECAY_LORA = 64
ICLR_LORA = 64
GATE_LORA = 128
C_HEADS = D_MODEL // HEAD_DIM
C_KV_HEADS = 4
C_WIDTH = C_HEADS * HEAD_DIM
WINDOW = 128
Q_BLOCK = 128
D_FF = 2816
N_MOD = 9
ROPE_THETA = 10000.0
EPS = 1e-6
GN_EPS = 64e-5
NEG_BIG = -1e30

A_COLS = A_WIDTH + 2 * A_KV_HEADS * HEAD_DIM
A_SPLITS = (A_WIDTH, A_WIDTH + A_KV_HEADS * HEAD_DIM)
RWKV_COLS = 3 * B_WIDTH + 2 * DECAY_LORA + 2 * ICLR_LORA + GATE_LORA
RWKV_SPLITS = (B_WIDTH, 2 * B_WIDTH, 3 * B_WIDTH, 3 * B_WIDTH + 2 * DECAY_LORA,
               3 * B_WIDTH + 2 * DECAY_LORA + 2 * ICLR_LORA)
EVEN_COLS = A_COLS + RWKV_COLS
ODD_COLS = C_WIDTH + 2 * C_KV_HEADS * HEAD_DIM
ODD_SPLITS = (C_WIDTH, C_WIDTH + C_KV_HEADS * HEAD_DIM)

kernel_name = "hybrid_dit_gqa_rwkv7_swa_macaron"


def rms_norm(x, gain=None):
    xf = x.astype(jnp.float32)
    y = xf * lax.rsqrt(jnp.mean(xf * xf, axis=-1, keepdims=True) + EPS)
    if gain is not None:
        y = y * gain
    return y.astype(x.dtype)


def modulate(h, shift, scale):
    return h * (1.0 + scale) + shift


def swiglu(h, w_in, w_out):
    gate, up = jnp.split(h @ w_in, 2, axis=-1)
    return (jax.nn.silu(gate) * up) @ w_out


def axial_rope_tables(n_tokens):
    rows = n_tokens // GRID_W
    row = jnp.repeat(jnp.arange(rows, dtype=jnp.float32), GRID_W)
    col = jnp.tile(jnp.arange(GRID_W, dtype=jnp.float32), rows)
    n_freq = HEAD_DIM // 4
    inv_freq = ROPE_THETA ** (-jnp.arange(n_freq, dtype=jnp.float32) / n_freq)
    ang = jnp.concatenate([row[:, None] * inv_freq, col[:, None] * inv_freq], axis=-1)
    return jnp.cos(ang), jnp.sin(ang)


def apply_rope(x, cos, sin):
    xp = x.reshape(*x.shape[:-1], HEAD_DIM // 2, 2)
    x0, x1 = xp[..., 0], xp[..., 1]
    c = cos[None, :, None, :].astype(x.dtype)
    s = sin[None, :, None, :].astype(x.dtype)
    return jnp.stack([x0 * c - x1 * s, x0 * s + x1 * c], axis=-1).reshape(x.shape)


def attend(q, k, v):
    s = jnp.einsum('bqhgd,bkhd->bhgqk', q, k).astype(jnp.float32)
    p = jax.nn.softmax(s, axis=-1).astype(v.dtype)
    return jnp.einsum('bhgqk,bkhd->bqhgd', p, v)


def global_gqa(q_l, k_l, v_l, q_c, k_c, v_c, ctx_out):
    Bsz, N = q_l.shape[:2]
    G = A_HEADS // A_KV_HEADS
    scale = HEAD_DIM ** -0.5
    k_all = jnp.concatenate([k_l, k_c], axis=1)
    v_all = jnp.concatenate([v_l, v_c], axis=1)
    qb = (q_l * scale).reshape(Bsz, N // Q_BLOCK, Q_BLOCK, A_KV_HEADS, G, HEAD_DIM).swapaxes(0, 1)
    o_lat = lax.map(lambda qblk: attend(qblk, k_all, v_all), qb)
    o_lat = o_lat.swapaxes(0, 1).reshape(Bsz, N, A_WIDTH)
    o_ctx = None
    if ctx_out:
        L = q_c.shape[1]
        qc = (q_c * scale).reshape(Bsz, L, A_KV_HEADS, G, HEAD_DIM)
        o_ctx = attend(qc, k_c, v_c).reshape(Bsz, L, A_WIDTH)
    return o_lat, o_ctx


def window_gqa(q_l, k_l, v_l, q_c, k_c, v_c, sink, ctx_out):
    Bsz, N = q_l.shape[:2]
    G = C_HEADS // C_KV_HEADS
    scale = HEAD_DIM ** -0.5
    nb = N // Q_BLOCK
    span = Q_BLOCK + 2 * WINDOW
    pad = ((0, 0), (WINDOW, WINDOW), (0, 0), (0, 0))
    k_pad = jnp.pad(k_l, pad)
    v_pad = jnp.pad(v_l, pad)
    qb = (q_l * scale).reshape(Bsz, nb, Q_BLOCK, C_KV_HEADS, G, HEAD_DIM).swapaxes(0, 1)
    sink_hg = sink.reshape(C_KV_HEADS, G).astype(jnp.float32)
    q_off = jnp.arange(Q_BLOCK)
    k_off = jnp.arange(span) - WINDOW
    in_window = jnp.abs(k_off[None, :] - q_off[:, None]) <= WINDOW

    def softmax_with_sink(s):
        sink_col = jnp.broadcast_to(sink_hg[None, :, :, None, None], s.shape[:-1] + (1,))
        return jax.nn.softmax(jnp.concatenate([s, sink_col], axis=-1), axis=-1)[..., :-1]

    def block(args):
        b, qblk = args
        start = b * Q_BLOCK
        kb = lax.dynamic_slice_in_dim(k_pad, start, span, axis=1)
        vb = lax.dynamic_slice_in_dim(v_pad, start, span, axis=1)
        kpos = start - WINDOW + k_off
        valid = in_window & ((kpos >= 0) & (kpos < N))[None, :]
        s_win = jnp.einsum('bqhgd,bkhd->bhgqk', qblk, kb).astype(jnp.float32)
        s_win = jnp.where(valid, s_win, NEG_BIG)
        s_ctx = jnp.einsum('bqhgd,bkhd->bhgqk', qblk, k_c).astype(jnp.float32)
        p = softmax_with_sink(jnp.concatenate([s_win, s_ctx], axis=-1)).astype(vb.dtype)
        return (jnp.einsum('bhgqk,bkhd->bqhgd', p[..., :span], vb)
                + jnp.einsum('bhgqk,bkhd->bqhgd', p[..., span:], v_c))

    o_lat = lax.map(block, (jnp.arange(nb), qb)).swapaxes(0, 1).reshape(Bsz, N, C_WIDTH)
    o_ctx = None
    if ctx_out:
        L = q_c.shape[1]
        qc = (q_c * scale).reshape(Bsz, L, C_KV_HEADS, G, HEAD_DIM)
        s = jnp.einsum('bqhgd,bkhd->bhgqk', qc, k_c).astype(jnp.float32)
        p = softmax_with_sink(s).astype(v_c.dtype)
        o_ctx = jnp.einsum('bhgqk,bkhd->bqhgd', p, v_c).reshape(Bsz, L, C_WIDTH)
    return o_lat, o_ctx


def centred_shift(f, mu):
    prev = jnp.pad(f[:, :-1], ((0, 0), (1, 0), (0, 0)))
    nxt = jnp.pad(f[:, 1:], ((0, 0), (0, 1), (0, 0)))
    return f + mu[0] * (prev - f) + mu[1] * (nxt - f)


def rwkv7_features(f, mu, w0, w2, a0, a2, g2, k_k, k_a):
    Bsz, T, _ = f.shape
    f = centred_shift(f, mu)
    r, k, v, wl, al, gl = jnp.split(f, RWKV_SPLITS, axis=-1)
    heads = lambda t: t.reshape(Bsz, T, B_HEADS, HEAD_DIM)
    dir_heads = lambda t: t.reshape(Bsz, T, 2, B_HEADS, HEAD_DIM)
    kk = heads(k * k_k).astype(jnp.float32)
    kk = kk * lax.rsqrt(jnp.sum(kk * kk, axis=-1, keepdims=True) + EPS)
    w_log = (w0 + jnp.einsum('btdr,drc->btdc', jnp.tanh(wl.reshape(Bsz, T, 2, DECAY_LORA)), w2)).astype(jnp.float32)
    decay = jnp.exp(-jnp.exp(-jax.nn.softplus(-w_log) - 0.5))
    a = jax.nn.sigmoid((a0 + jnp.einsum('btdr,drc->btdc', al.reshape(Bsz, T, 2, ICLR_LORA), a2)).astype(jnp.float32))
    k_dir = k[:, :, None, :].astype(jnp.float32) * (1.0 + (a - 1.0) * k_a)
    g = jax.nn.sigmoid(gl) @ g2
    return (heads(r), heads(k), heads(v), kk, g, dir_heads(decay), dir_heads(a), dir_heads(k_dir))


def wkv7_scan(S0, inputs, reverse):
    xs = tuple(jnp.swapaxes(t.astype(jnp.float32), 0, 1) for t in inputs)

    def step(S, inp):
        r, w, k, v, kk, a = inp
        S = (S * w[:, :, None, :]
             - jnp.einsum('bhvk,bhk->bhv', S, kk)[..., None] * (kk * a)[:, :, None, :]
             + v[..., None] * k[:, :, None, :])
        return S, jnp.einsum('bhvk,bhk->bhv', S, r)

    S, y = lax.scan(step, S0, xs, reverse=reverse)
    return S, jnp.swapaxes(y, 0, 1)


def rwkv7_readout(y, feat, r_k, gn_w, gn_b):
    r, k, v, _, g = feat[:5]
    Bsz, T = y.shape[:2]
    mean = jnp.mean(y, axis=-1, keepdims=True)
    var = jnp.mean(jnp.square(y - mean), axis=-1, keepdims=True)
    yn = ((y - mean) * lax.rsqrt(var + GN_EPS)).reshape(Bsz, T, B_WIDTH) * gn_w + gn_b
    bonus = (jnp.sum(r * k * r_k, axis=-1, keepdims=True) * v).reshape(Bsz, T, B_WIDTH)
    return ((yn + bonus.astype(jnp.float32)) * g.astype(jnp.float32)).astype(g.dtype)


def rwkv7_bidir(f_lat, f_ctx, r_k, gn_w, gn_b, ctx_out):
    Bsz = f_lat[0].shape[0]
    S0 = jnp.zeros((Bsz, B_HEADS, HEAD_DIM, HEAD_DIM), jnp.float32)

    def dir_inputs(f, d):
        r, _, v, kk, _, decay, a, k_dir = f
        return (r, decay[:, :, d], k_dir[:, :, d], v, kk, a[:, :, d])

    S_cf, y_cf = wkv7_scan(S0, dir_inputs(f_ctx, 0), reverse=False)
    S_cb, y_cb = wkv7_scan(S0, dir_inputs(f_ctx, 1), reverse=True)
    _, y_lf = wkv7_scan(S_cf, dir_inputs(f_lat, 0), reverse=False)
    _, y_lb = wkv7_scan(S_cb, dir_inputs(f_lat, 1), reverse=True)
    o_lat = rwkv7_readout(y_lf + y_lb, f_lat, r_k, gn_w, gn_b)
    o_ctx = rwkv7_readout(y_cf + y_cb, f_ctx, r_k, gn_w, gn_b) if ctx_out else None
    return o_lat, o_ctx


def even_mixer(h_lat, h_ctx, cos, sin, w_in, w_out, q_gain, k_gain, mu, w0, w2, a0, a2, g2,
               k_k, k_a, r_k, gn_w, gn_b, ctx_out):
    Bsz, N, _ = h_lat.shape
    L = h_ctx.shape[1]
    p_lat = h_lat @ w_in
    p_ctx = h_ctx @ w_in

    def qkv_a(p, T):
        q, k, v = jnp.split(p[..., :A_COLS], A_SPLITS, axis=-1)
        q = rms_norm(q.reshape(Bsz, T, A_HEADS, HEAD_DIM), q_gain)
        k = rms_norm(k.reshape(Bsz, T, A_KV_HEADS, HEAD_DIM), k_gain)
        return q, k, v.reshape(Bsz, T, A_KV_HEADS, HEAD_DIM)

    q_l, k_l, v_l = qkv_a(p_lat, N)
    q_c, k_c, v_c = qkv_a(p_ctx, L)
    oa_lat, oa_ctx = global_gqa(apply_rope(q_l, cos, sin), apply_rope(k_l, cos, sin), v_l,
                                q_c, k_c, v_c, ctx_out)
    f_lat = rwkv7_features(p_lat[..., A_COLS:], mu, w0, w2, a0, a2, g2, k_k, k_a)
    f_ctx = rwkv7_features(p_ctx[..., A_COLS:], mu, w0, w2, a0, a2, g2, k_k, k_a)
    ob_lat, ob_ctx = rwkv7_bidir(f_lat, f_ctx, r_k, gn_w, gn_b, ctx_out)
    o_lat = jnp.concatenate([oa_lat, ob_lat], axis=-1) @ w_out
    o_ctx = jnp.concatenate([oa_ctx, ob_ctx], axis=-1) @ w_out if ctx_out else None
    return o_lat, o_ctx


def odd_mixer(h_lat, h_ctx, cos, sin, w_in, w_out, sink, ctx_out):
    Bsz, N, _ = h_lat.shape
    L = h_ctx.shape[1]

    def qkv_c(h, T):
        q, k, v = jnp.split(h @ w_in, ODD_SPLITS, axis=-1)
        return (q.reshape(Bsz, T, C_HEADS, HEAD_DIM), k.reshape(Bsz, T, C_KV_HEADS, HEAD_DIM),
                v.reshape(Bsz, T, C_KV_HEADS, HEAD_DIM))

    q_l, k_l, v_l = qkv_c(h_lat, N)
    q_c, k_c, v_c = qkv_c(h_ctx, L)
    o_lat, o_ctx = window_gqa(apply_rope(q_l, cos, sin), apply_rope(k_l, cos, sin), v_l,
                              q_c, k_c, v_c, sink, ctx_out)
    return o_lat @ w_out, (o_ctx @ w_out if ctx_out else None)


def setup_inputs(seed: int = 0) -> dict:
    key = jax.random.key(seed)
    ks = iter(jax.random.split(key, 32))
    nrm = lambda shape, s: s * jax.random.normal(next(ks), shape, jnp.float32)
    uni = lambda shape, lo, hi: jax.random.uniform(next(ks), shape, jnp.float32, lo, hi)
    n_even = (DEPTH + 1) // 2
    n_odd = DEPTH // 2
    D = D_MODEL
    return {
        "x": nrm((BATCH, SEQ, D), 1.0),
        "c": nrm((BATCH, D), 1.0),
        "ctx": nrm((BATCH, CTX_LEN, D), 1.0),
        "c_ctx": nrm((D,), 1.0),
        "w_mod": nrm((DEPTH, D, N_MOD * D), 0.5 * D ** -0.5),
        "b_mod": nrm((DEPTH, N_MOD * D), 0.02),
        "ffn_in": nrm((DEPTH, 2, D, 2 * D_FF), D ** -0.5),
        "ffn_out": nrm((DEPTH, 2, D_FF, D), D_FF ** -0.5),
        "even_w_in": nrm((n_even, D, EVEN_COLS), D ** -0.5),
        "even_w_out": nrm((n_even, A_WIDTH + B_WIDTH, D), (A_WIDTH + B_WIDTH) ** -0.5),
        "q_gain": 1.0 + nrm((n_even, HEAD_DIM), 0.05),
        "k_gain": 1.0 + nrm((n_even, HEAD_DIM), 0.05),
        "rwkv_mu": uni((n_even, 2, RWKV_COLS), 0.0, 0.5),
        "rwkv_w0": uni((n_even, 2, B_WIDTH), -4.0, 1.0),
        "rwkv_w2": nrm((n_even, 2, DECAY_LORA, B_WIDTH), 0.1),
        "rwkv_a0": nrm((n_even, 2, B_WIDTH), 0.5),
        "rwkv_a2": nrm((n_even, 2, ICLR_LORA, B_WIDTH), 0.1),
        "rwkv_g2": nrm((n_even, GATE_LORA, B_WIDTH), GATE_LORA ** -0.5),
        "rwkv_k_k": 0.85 + nrm((n_even, B_WIDTH), 0.05),
        "rwkv_k_a": 1.0 + nrm((n_even, B_WIDTH), 0.05),
        "rwkv_r_k": nrm((n_even, B_HEADS, HEAD_DIM), 0.1),
        "rwkv_gn_w": 1.0 + nrm((n_even, B_WIDTH), 0.05),
        "rwkv_gn_b": nrm((n_even, B_WIDTH), 0.02),
        "odd_w_in": nrm((n_odd, D, ODD_COLS), D ** -0.5),
        "odd_w_out": nrm((n_odd, C_WIDTH, D), C_WIDTH ** -0.5),
        "sink": nrm((n_odd, C_HEADS), 1.0),
        "final_gain": 1.0 + nrm((D,), 0.05),
    }


def reference(x, c, ctx, c_ctx, w_mod, b_mod, ffn_in, ffn_out, even_w_in, even_w_out, q_gain, k_gain,
              rwkv_mu, rwkv_w0, rwkv_w2, rwkv_a0, rwkv_a2, rwkv_g2, rwkv_k_k, rwkv_k_a, rwkv_r_k,
              rwkv_gn_w, rwkv_gn_b, odd_w_in, odd_w_out, sink, final_gain):
    Bsz, N, D = x.shape
    cos, sin = axial_rope_tables(N)
    silu_c = jax.nn.silu(c)
    silu_cc = jax.nn.silu(c_ctx)
    for l in range(DEPTH):
        ctx_out = l < DEPTH - 1
        m_lat = (silu_c @ w_mod[l] + b_mod[l]).reshape(Bsz, N_MOD, 1, D)
        m_ctx = (silu_cc @ w_mod[l] + b_mod[l]).reshape(N_MOD, D)
        x = x + 0.5 * m_lat[:, 2] * swiglu(modulate(rms_norm(x), m_lat[:, 0], m_lat[:, 1]), ffn_in[l, 0], ffn_out[l, 0])
        ctx = ctx + 0.5 * m_ctx[2] * swiglu(modulate(rms_norm(ctx), m_ctx[0], m_ctx[1]), ffn_in[l, 0], ffn_out[l, 0])
        h_lat = modulate(rms_norm(x), m_lat[:, 3], m_lat[:, 4])
        h_ctx = modulate(rms_norm(ctx), m_ctx[3], m_ctx[4])
        if l % 2 == 0:
            e = l // 2
            o_lat, o_ctx = even_mixer(h_lat, h_ctx, cos, sin, even_w_in[e], even_w_out[e], q_gain[e], k_gain[e],
                                      rwkv_mu[e], rwkv_w0[e], rwkv_w2[e], rwkv_a0[e], rwkv_a2[e], rwkv_g2[e],
                                      rwkv_k_k[e], rwkv_k_a[e], rwkv_r_k[e], rwkv_gn_w[e], rwkv_gn_b[e], ctx_out)
        else:
            o = l // 2
            o_lat, o_ctx = odd_mixer(h_lat, h_ctx, cos, sin, odd_w_in[o], odd_w_out[o], sink[o], ctx_out)
        x = x + m_lat[:, 5] * o_lat
        x = x + 0.5 * m_lat[:, 8] * swiglu(modulate(rms_norm(x), m_lat[:, 6], m_lat[:, 7]), ffn_in[l, 1], ffn_out[l, 1])
        if ctx_out:
            ctx = ctx + m_ctx[5] * o_ctx
            ctx = ctx + 0.5 * m_ctx[8] * swiglu(modulate(rms_norm(ctx), m_ctx[6], m_ctx[7]), ffn_in[l, 1], ffn_out[l, 1])
    return rms_norm(x, final_gain)
```

```python
import numpy as np
from contextlib import ExitStack
import concourse.bass as bass
import concourse.mybir as mybir
from concourse.bass_utils import run_bass_kernel_spmd

F32 = mybir.dt.float32
BF16 = mybir.dt.bfloat16
F32R = mybir.dt.float32r
AF = mybir.ActivationFunctionType
ALU = mybir.AluOpType
AX = mybir.AxisListType


class Tile:
    __slots__ = ("name", "lw", "rd", "dsem", "dcnt", "dlast", "dram")

    def __init__(self, name, dram=False):
        self.name = name
        self.lw = None
        self.rd = []
        self.dsem = None
        self.dcnt = 0
        self.dlast = None
        self.dram = dram


class V:
    __slots__ = ("ap", "t")

    def __init__(self, ap, t):
        self.ap = ap
        self.t = t

    def __getitem__(self, idx):
        return V(self.ap[idx], self.t)

    def re(self, pattern, **kw):
        return V(self.ap.rearrange(pattern, **kw), self.t)

    def bc(self, shape):
        return V(self.ap.broadcast_to(shape), self.t)

    def cast(self, dt):
        return V(self.ap.bitcast(dt), self.t)

    @property
    def shape(self):
        return self.ap.shape


class Ins:
    __slots__ = ("q", "fn", "deps", "dma", "sem", "val", "signal", "idx")

    def __init__(self, q, fn, dma=False):
        self.q = q
        self.fn = fn
        self.deps = []
        self.dma = dma
        self.sem = None
        self.val = 0
        self.signal = dma
        self.idx = 0


QUEUES = ("pe", "act", "dve", "pool", "sp")


class Scope:
    def __init__(self, P):
        self.P = P
        self.es = ExitStack()

    def __enter__(self):
        self.es.__enter__()
        self.tiles = []
        self.P.scope_stack.append(self.tiles)
        return self

    def sb(self, name, shape, dt=F32):
        self.P.ntile += 1
        name = "%s_%d" % (name, self.P.ntile)
        t = self.es.enter_context(self.P.nc.sbuf_tensor(name, list(shape), dt))
        return V(t[:], Tile(name))

    def __exit__(self, *a):
        P = self.P
        P.barrier()
        for t in self.tiles:
            if t.dsem is not None:
                P.live_sems.remove(t.dsem)
                P.free_sems.append(t.dsem)
                t.dsem = None
        P.scope_stack.pop()
        return self.es.__exit__(*a)


class Prog:
    def __init__(self, nc, es):
        self.nc = nc
        self.es = es
        self.q = {k: [] for k in QUEUES}
        self.esem = {}
        for k in ("pe", "act", "dve", "pool"):
            self.esem[k] = es.enter_context(nc.semaphore("sem_" + k))
        self.ntile = 0
        self.nsem = 4
        self.bar = {}
        self.free_sems = []
        self.live_sems = []
        self.scope_stack = []

    def sb(self, name, shape, dt=F32):
        t = self.es.enter_context(self.nc.sbuf_tensor(name, list(shape), dt))
        return V(t[:], Tile(name))

    def ps(self, name, shape, dt=F32):
        t = self.es.enter_context(self.nc.psum_tensor(name, list(shape), dt))
        return V(t[:], Tile(name))

    def dram(self, name, shape, dt=F32, kind="Internal"):
        t = self.nc.dram_tensor(name, list(shape), dt, kind=kind)
        return V(t.ap(), None)

    def sub(self, v, name):
        return V(v.ap, Tile(name))

    def _rec(self, q, fn, reads, writes, dma=False):
        ins = Ins(q, fn, dma)
        compute_inorder = (not dma) and q == "pe"
        deps = []
        reads = [v for v in reads if isinstance(v, V) and v.t is not None]
        writes = [v for v in writes if isinstance(v, V) and v.t is not None]
        for v in reads:
            t = v.t
            w = t.lw
            if w is not None:
                deps.append(w)
        for v in writes:
            t = v.t
            w = t.lw
            if w is not None and not (compute_inorder and not w.dma and w.q == q):
                deps.append(w)
            for r in t.rd:
                if not (compute_inorder and not r.dma and r.q == q):
                    deps.append(r)
        for v in reads:
            v.t.rd.append(ins)
        for v in writes:
            v.t.lw = ins
            v.t.rd = []
        if self.bar.get(q):
            deps.extend(self.bar[q])
            self.bar[q] = None
        for d in deps:
            if d is not ins:
                d.signal = True
        ins.deps = deps
        ins.idx = len(self.q[q])
        self.q[q].append(ins)
        return ins

    def op(self, q, fn, reads, writes):
        return self._rec(q, fn, reads, writes)

    def dma(self, out, in_, q="sp", **kw):
        st = in_.t if out.t is None else out.t
        if st.dsem is None:
            if self.free_sems:
                st.dsem = self.free_sems.pop()
            else:
                st.dsem = [self.es.enter_context(self.nc.semaphore("dsem%d" % self.nsem)), 0, None]
                self.nsem += 1
            self.live_sems.append(st.dsem)
            if self.scope_stack:
                self.scope_stack[-1].append(st)
        oap, iap = out.ap, in_.ap

        def fn(e):
            return e.dma_start(out=oap, in_=iap, **kw)
        ins = self._rec(q, fn, [in_], [out], dma=True)
        sem = st.dsem
        if sem[2] is not None:
            ins.deps.append(sem[2])
        sem[1] += 1
        sem[2] = ins
        ins.sem = sem[0]
        ins.val = 16 * sem[1]
        return ins

    def barrier(self):
        lst = []
        for k in ("pe", "act", "dve", "pool"):
            for ins in reversed(self.q[k]):
                if not ins.dma:
                    lst.append(ins)
                    break
        for sem in self.live_sems:
            if sem[2] is not None:
                lst.append(sem[2])
        for i in lst:
            i.signal = True
        for k in QUEUES:
            self.bar[k] = list(lst)

    def scope(self):
        return Scope(self)

    def mm(self, out, lhsT, rhs, start=True, stop=True, **kw):
        oa, la, ra = out.ap, lhsT.ap, rhs.ap
        return self.op("pe", lambda e: e.matmul(oa, la, ra, start=start, stop=stop, **kw), [lhsT, rhs], [out])

    def tr(self, out, in_, ident):
        oa, ia, da = out.ap, in_.ap, ident.ap
        return self.op("pe", lambda e: e.transpose(oa, ia, da), [in_, ident], [out])

    def act(self, out, in_, func, bias=None, scale=None, accum=None, q="act"):
        oa, ia = out.ap, in_.ap
        kw = {}
        rd = [in_]
        if bias is not None:
            kw["bias"] = bias.ap if isinstance(bias, V) else bias
            if isinstance(bias, V):
                rd.append(bias)
        if scale is not None:
            kw["scale"] = scale.ap if isinstance(scale, V) else scale
            if isinstance(scale, V):
                rd.append(scale)
        wr = [out]
        if accum is not None:
            kw["accum_out"] = accum.ap
            wr.append(accum)
        return self.op("act", lambda e: e.activation(oa, ia, func, **kw), rd, wr)

    def tt(self, out, in0, in1, op, q="dve"):
        oa, a, b = out.ap, in0.ap, in1.ap
        return self.op(q, lambda e: e.tensor_tensor(oa, a, b, op), [in0, in1], [out])

    def ts(self, out, in0, s1, op0, s2=None, op1=None, q="dve", accum=None):
        oa, a = out.ap, in0.ap
        rd = [in0]
        x1 = s1.ap if isinstance(s1, V) else s1
        x2 = s2.ap if isinstance(s2, V) else s2
        if isinstance(s1, V):
            rd.append(s1)
        if isinstance(s2, V):
            rd.append(s2)
        kw = {}
        wr = [out]
        if op1 is not None:
            kw["op1"] = op1
        if accum is not None:
            kw["accum_out"] = accum.ap
            wr.append(accum)
        return self.op(q, lambda e: e.tensor_scalar(oa, a, x1, x2, op0, **kw), rd, wr)

    def stt(self, out, in0, scalar, in1, op0, op1, q="dve"):
        oa, a, b = out.ap, in0.ap, in1.ap
        rd = [in0, in1]
        s = scalar.ap if isinstance(scalar, V) else scalar
        if isinstance(scalar, V):
            rd.append(scalar)
        return self.op(q, lambda e: e.scalar_tensor_tensor(oa, a, s, b, op0, op1), rd, [out])

    def copy(self, out, in_, q="dve"):
        oa, ia = out.ap, in_.ap
        if q == "act":
            return self.op(q, lambda e: e.copy(oa, ia), [in_], [out])
        return self.op(q, lambda e: e.tensor_copy(oa, ia), [in_], [out])

    def red(self, out, in_, op, axis=None, q="dve"):
        oa, ia = out.ap, in_.ap
        ax = AX.X if axis is None else axis
        return self.op(q, lambda e: e.tensor_reduce(oa, ia, ax, op), [in_], [out])

    def scan(self, out, d0, d1, init, op0, op1):
        oa, a, b = out.ap, d0.ap, d1.ap
        rd = [d0, d1]
        i0 = init.ap if isinstance(init, V) else init
        if isinstance(init, V):
            rd.append(init)
        return self.op("dve", lambda e: e.tensor_tensor_scan(oa, a, b, i0, op0, op1), rd, [out])

    def memset(self, out, val, q="dve"):
        oa = out.ap
        return self.op(q, lambda e: e.memset(oa, val), [], [out])

    def recip(self, out, in_):
        oa, ia = out.ap, in_.ap
        return self.op("dve", lambda e: e.reciprocal(oa, ia), [in_], [out])

    def emit(self, final_tiles):
        nc = self.nc
        for k in ("pe", "act", "dve", "pool"):
            c = 0
            for ins in self.q[k]:
                if ins.dma:
                    continue
                ins.sem = self.esem[k]
                if ins.signal:
                    c += 1
                    ins.val = c
                else:
                    ins.val = None
        engs = {"pe": "tensor", "act": "scalar", "dve": "vector", "pool": "gpsimd", "sp": "sync"}
        finals = [(i.sem, i.val) for i in final_tiles]
        with nc.Block() as block:
            for k in QUEUES:
                lst = self.q[k]
                if not lst and k != "sp":
                    continue

                def body(e, lst=lst, k=k):
                    waited = {}
                    for ins in lst:
                        need = {}
                        for d in ins.deps:
                            sid = id(d.sem)
                            if d.val is None:
                                raise RuntimeError("dep on non-signalling ins")
                            if waited.get(sid, 0) >= d.val:
                                continue
                            if sid not in need or need[sid][1] < d.val:
                                need[sid] = (d.sem, d.val)
                        for sid, (s, v) in need.items():
                            e.wait_ge(s, v)
                            waited[sid] = v
                        r = ins.fn(e)
                        if ins.dma:
                            r.then_inc(ins.sem, 16)
                        elif ins.signal:
                            r.then_inc(ins.sem, 1)
                    if k == "sp":
                        for s, v in finals:
                            e.wait_ge(s, v)
                getattr(block, engs[k])(body)


D = 1024
KC = 8
DFF = 2816
FC = 22
LCTX = 256
EPS = 1e-6


class G:
    pass


def dview(v, pattern, **kw):
    return V(v.ap.rearrange(pattern, **kw), None)


def setup_consts(g):
    P = g.P
    g.ident = P.sb("ident", [128, 128])
    P.memset(g.ident, 1.0, q="pool")
    ia = g.ident.ap
    P.op("pool", lambda e: e.affine_select(ia, ia, [[1, 128]], ALU.is_equal, 0.0, base=0, channel_multiplier=-1),
         [g.ident], [g.ident])
    g.onesb = P.sb("onesb", [128, 128], BF16)
    P.memset(g.onesb, 1.0)
    g.pb = [P.ps("pb%d" % i, [128, 512]) for i in range(8)]
    g.pbi = 0


def nbank(g):
    b = g.pb[g.pbi % 8]
    g.pbi += 1
    return b


def rows_to_cols(g, S, dst, src_rows, R, name):
    P = g.P
    st = S.sb(name, [R, 128])
    P.dma(st, src_rows)
    b = nbank(g)
    P.tr(b[:, 0:R], st, g.ident[0:R, 0:R])
    P.copy(dst, b[:, 0:R])


def emit_mod(g):
    P = g.P
    g.modT = P.sb("modT", [128, 4, 72, 2])
    g.modp1 = P.sb("modp1", [128, 4, 72, 2])
    g.modh = P.sb("modh", [128, 4, 72, 2])
    with P.scope() as S:
        crow = S.sb("crow", [2, D])
        P.dma(crow, g.d["cvec"])
        crs = S.sb("crs", [2, D])
        P.act(crs, crow, AF.Silu)
        csT = S.sb("csT", [128, KC, 2])
        b = nbank(g)
        for kc in range(KC):
            P.tr(b[:, 2 * kc:2 * kc + 2], crs[0:2, kc * 128:(kc + 1) * 128], g.ident[0:2, 0:2])
        P.copy(csT, b[:, 0:16].re("p (k n) -> p k n", n=2))
        bmT = S.sb("bmT", [128, 288])
        bm_rows = dview(g.d["b_mod"], "l (j p) -> (l j) p", p=128)
        for r in range(3):
            rows_to_cols(g, S, bmT[:, r * 96:(r + 1) * 96], bm_rows[r * 96:(r + 1) * 96, :], 96, "bmst%d" % r)
        slabs = [S.sb("wmslab%d" % i, [128, KC, 512]) for i in range(3)]
        mrow = S.sb("mrow", [2, 9 * D])
        n = 0
        for l in range(g.depth):
            wv = dview(g.d["w_mod"][l], "(kc p) n -> p kc n", p=128)
            for s_ in range(18):
                sl = slabs[n % 3]
                n += 1
                P.dma(sl, wv[:, :, s_ * 512:(s_ + 1) * 512])
                b = nbank(g)
                for kc in range(KC):
                    P.mm(b[0:2, :], csT[:, kc, :], sl[:, kc, :], start=(kc == 0), stop=(kc == KC - 1))
                P.copy(mrow[:, s_ * 512:(s_ + 1) * 512], b[0:2, :], q=("act" if s_ % 2 else "dve"))
            b = nbank(g)
            for jj in range(72):
                P.tr(b[:, 2 * jj:2 * jj + 2], mrow[0:2, jj * 128:(jj + 1) * 128], g.ident[0:2, 0:2])
            P.tt(g.modT[:, l, :, :], b[:, 0:144].re("p (j n) -> p j n", n=2),
                 bmT[:, l * 72:(l + 1) * 72].re("p (j o) -> p j o", o=1).bc([128, 72, 2]), ALU.add)
        dd = g.depth
        P.ts(g.modp1[:, 0:dd], g.modT[:, 0:dd], 1.0, ALU.add)
        P.ts(g.modh[:, 0:dd], g.modT[:, 0:dd], 0.5, ALU.mult)


def mcol(arr, l, i, c, n):
    return arr[:, l, i * 8 + c, n:n + 1]


def emit_in_transpose(g):
    P = g.P
    xTv = dview(g.d["xT"], "c p t -> p c t")
    with P.scope() as S:
        xin = [S.sb("xin%d" % i, [128, 4, D]) for i in range(2)]
        xtl = [S.sb("xtl%d" % i, [128, KC, 512]) for i in range(2)]
        groups = [("ctx", 0, 2, 0)] + [("x", t * 512, 4, LCTX + t * 512) for t in range(g.nlat // 512)]
        for gi, (nm, r0, nb, t0) in enumerate(groups):
            xi = xin[gi % 2]
            xt = xtl[gi % 2]
            src = dview(g.d[nm][r0:r0 + nb * 128, :], "(b p) f -> p b f", p=128)
            P.dma(xi[:, 0:nb, :], src)
            for c in range(KC):
                b = nbank(g)
                for bl in range(nb):
                    P.tr(b[:, bl * 128:(bl + 1) * 128], xi[:, bl, c * 128:(c + 1) * 128], g.ident)
                P.copy(xt[:, c, 0:nb * 128], b[:, 0:nb * 128], q=("act" if c % 2 else "dve"))
                g.bg.tick(2)
            P.dma(xTv[:, :, t0:t0 + nb * 128], xt[:, :, 0:nb * 128])


def norm_stats(g, sq, rstd, xt, ncols):
    P = g.P
    b = nbank(g)
    for c in range(KC):
        s_ = sq[c % 2]
        P.act(s_[:, 0:ncols], xt[:, c, 0:ncols], AF.Square)
        P.mm(b[:, 0:ncols], g.onesb, s_[:, 0:ncols], start=(c == 0), stop=(c == KC - 1))
    P.act(rstd[:, 0:ncols], b[:, 0:ncols], AF.Sqrt, scale=1.0 / D, bias=g.epsc[:, 0:1])
    P.recip(rstd[:, 0:ncols], rstd[:, 0:ncols])


def norm_apply(g, tmp, rstd, xt, hT, ncols, l, i_shift, n):
    P = g.P
    for c in range(KC):
        t_ = tmp[c % 2]
        P.stt(t_[:, 0:ncols], xt[:, c, 0:ncols], mcol(g.modp1, l, i_shift + 1, c, n), rstd[:, 0:ncols], ALU.mult, ALU.mult)
        P.act(hT[:, c, 0:ncols], t_[:, 0:ncols], AF.Identity, bias=mcol(g.modT, l, i_shift, c, n))


def emit_norm_mod(g, S, bufs, xt, hT, ncols, l, i_shift, n):
    sq, rstd, tmp = bufs
    norm_stats(g, sq, rstd, xt, ncols)
    norm_apply(g, tmp, rstd, xt, hT, ncols, l, i_shift, n)


class BgPrep:
    def __init__(self, g):
        self.g = g
        P = g.P
        self.st = [P.sb("bg_st%d" % i, [128, 2048]) for i in range(2)]
        self.ob = [P.sb("bg_ob%d" % i, [128, 2048], BF16) for i in range(2)]
        self.jobs = []
        for l in range(g.depth):
            for which in range(2):
                k = l * 2 + which
                for jp in range(FC // 2):
                    for kh in range(2):
                        self.jobs.append((k, "in", l, which, jp, kh))
                for fp in range(FC // 2):
                    self.jobs.append((k, "out", l, which, fp, 0))
        self.pos = 0
        self.pending = None
        self.calls = 0
        self.limit_k = 0

    def _emit_store(self):
        if self.pending is not None:
            dst, src = self.pending
            self.g.P.dma(dst, src)
            self.pending = None

    def step(self):
        if self.pos >= len(self.jobs):
            self._emit_store()
            return False
        g = self.g
        P = g.P
        k, kind, l, which, a, kh = self.jobs[self.pos]
        st = self.st[self.pos % 2]
        ob = self.ob[self.pos % 2]
        self.pos += 1
        if kind == "in":
            win = dview(g.d["ffn_in"][l, which], "(kc p) n -> p kc n", p=128)
            sv = st.re("p (k n) -> p k n", n=512)
            P.dma(sv[:, :, 0:256], win[:, kh * 4:kh * 4 + 4, a * 256:(a + 1) * 256])
            P.dma(sv[:, :, 256:512], win[:, kh * 4:kh * 4 + 4, DFF + a * 256:DFF + (a + 1) * 256])
            dst = V(g.d["w_in_b"].ap[k, a, :, kh * 2048:(kh + 1) * 2048], None)
        else:
            wout = dview(g.d["ffn_out"][l, which], "(fc p) n -> p fc n", p=128)
            P.dma(st.re("p (f n) -> p f n", n=1024), wout[:, a * 2:a * 2 + 2, :])
            dst = V(g.d["w_out_b"].ap[k, :, a * 2048:(a + 1) * 2048], None)
        self._emit_store()
        P.copy(ob, st, q="pool")
        self.pending = (dst, ob)
        return True

    def tick(self, every):
        self.calls += 1
        if self.calls % every == 0 and self.pos < len(self.jobs) and self.jobs[self.pos][0] <= self.limit_k:
            self.step()

    def finish(self, k):
        n = 0
        while self.pos < len(self.jobs) and self.jobs[self.pos][0] <= k:
            self.step()
            n += 1
        if self.pending is not None:
            self._emit_store()
            n += 1
        if n:
            self.g.P.barrier()


def emit_ffn(g, l, which):
    P = g.P
    i0 = 0 if which == 0 else 6
    k = l * 2 + which
    xTd = g.d["xT"]
    g.bg.finish(k)
    tiles = []
    if not (l == g.depth - 1 and which == 1):
        tiles.append((0, LCTX, 1))
    NT = 1024
    for t in range(g.nlat // NT):
        tiles.append((LCTX + t * NT, NT, 0))
    with P.scope() as S:
        xt = S.sb("f_xt", [128, KC, 512])
        hT = S.sb("f_hT", [128, KC, NT], BF16)
        actT = S.sb("f_actT", [128, FC, NT], BF16)
        sq = [S.sb("f_sq%d" % i, [128, 512], BF16) for i in range(2)]
        rstd = S.sb("f_rstd", [128, 512])
        tmp = [S.sb("f_tmp%d" % i, [128, 512]) for i in range(2)]
        wb = [S.sb("f_wb%d" % i, [128, KC, 512], BF16) for i in range(3)]
        wo = S.sb("f_wo", [128, FC, D], BF16)
        sg = [S.sb("f_sg%d" % i, [128, 512], BF16) for i in range(2)]
        xc = [S.sb("f_xc%d" % i, [128, NT]) for i in range(2)]
        for q4 in range(2):
            f0, f1 = q4 * 11, (q4 + 1) * 11
            P.dma(wo[:, f0:f1, :], V(g.d["w_out_b"].ap[k, :, f0 * D:f1 * D].rearrange("p (f n) -> p f n", n=D), None))
        rstdh = [rstd, S.sb("f_rstd2", [128, 512])]
        nw = 0
        nx = 0
        nsg = 0

        def load_x(tile, h):
            t0, nt, n = tile
            hw = min(512, nt)
            P.dma(xt[:, :, 0:hw], dview(xTd[:, :, t0 + h * hw:t0 + (h + 1) * hw], "c p t -> p c t"))

        def a1(tile, h):
            load_x(tile, h)
            norm_stats(g, sq, rstdh[h], xt, min(512, tile[1]))

        def a2(tile, h):
            t0, nt, n = tile
            hw = min(512, nt)
            load_x(tile, h)
            norm_apply(g, tmp, rstdh[h], xt, hT[:, :, h * hw:(h + 1) * hw], hw, l, i0, n)

        for h in range(tiles[0][1] // min(512, tiles[0][1])):
            a1(tiles[0], h)
        for h in range(tiles[0][1] // min(512, tiles[0][1])):
            a2(tiles[0], h)
        for ti, (t0, nt, n) in enumerate(tiles):
            hw = min(512, nt)
            nh = nt // hw
            nxt = tiles[ti + 1] if ti + 1 < len(tiles) else None
            nnh = (nxt[1] // min(512, nxt[1])) if nxt else 0
            for jp in range(FC // 2):
                w_ = wb[nw % 3]
                nw += 1
                P.dma(w_, V(g.d["w_in_b"].ap[k, jp].rearrange("p (k n) -> p k n", n=512), None))
                if nxt is not None and jp == 6:
                    a1(nxt, 0)
                if nxt is not None and jp == 8 and nnh > 1:
                    a1(nxt, 1)
                for jj in range(2):
                    j = jp * 2 + jj
                    for h in range(nh):
                        bg_ = nbank(g)
                        bu = nbank(g)
                        for kc in range(KC):
                            P.mm(bg_[:, 0:hw], w_[:, kc, jj * 128:(jj + 1) * 128], hT[:, kc, h * hw:(h + 1) * hw],
                                 start=(kc == 0), stop=(kc == KC - 1))
                        for kc in range(KC):
                            P.mm(bu[:, 0:hw], w_[:, kc, 256 + jj * 128:256 + (jj + 1) * 128], hT[:, kc, h * hw:(h + 1) * hw],
                                 start=(kc == 0), stop=(kc == KC - 1))
                        s_ = sg[nsg % 2]
                        nsg += 1
                        P.act(s_[:, 0:hw], bg_[:, 0:hw], AF.Silu)
                        P.tt(actT[:, j, h * hw:(h + 1) * hw], s_[:, 0:hw], bu[:, 0:hw], ALU.mult)
            for h in range(nnh):
                a2(nxt, h)
            for c in range(KC):
                x_ = xc[nx % 2]
                nx += 1
                P.dma(x_[:, 0:nt], V(xTd.ap[c, :, t0:t0 + nt], None))
                for h in range(nh):
                    b = nbank(g)
                    for f in range(FC):
                        P.mm(b[:, 0:hw], wo[:, f, c * 128:(c + 1) * 128], actT[:, f, h * hw:(h + 1) * hw],
                             start=(f == 0), stop=(f == FC - 1))
                    P.stt(x_[:, h * hw:(h + 1) * hw], b[:, 0:hw], mcol(g.modh, l, i0 + 2, c, n), x_[:, h * hw:(h + 1) * hw],
                          ALU.mult, ALU.add)
                P.dma(V(xTd.ap[c, :, t0:t0 + nt], None), x_[:, 0:nt])


def emit_final(g):
    P = g.P
    xTd = g.d["xT"]
    finals = []
    with P.scope() as S:
        fg = S.sb("fgT", [128, KC])
        rows_to_cols(g, S, fg, dview(g.d["final_gain"], "(c p) -> c p", p=128), KC, "fgst")
        xt = [S.sb("o_xt%d" % i, [128, KC, 512]) for i in range(2)]
        sq = [S.sb("o_sq%d" % i, [128, 512], BF16) for i in range(2)]
        rstd = S.sb("o_rstd", [128, 512])
        yt = S.sb("o_yt", [128, KC, 512])
        ot = [S.sb("o_ot%d" % i, [128, 4, D]) for i in range(2)]
        for t in range(g.nlat // 512):
            t0 = LCTX + t * 512
            x_ = xt[t % 2]
            P.dma(x_, dview(xTd[:, :, t0:t0 + 512], "c p t -> p c t"))
            b = nbank(g)
            for c in range(KC):
                s_ = sq[c % 2]
                P.act(s_, x_[:, c, :], AF.Square)
                P.mm(b, g.onesb, s_, start=(c == 0), stop=(c == KC - 1))
            P.act(rstd, b, AF.Sqrt, scale=1.0 / D, bias=g.epsc[:, 0:1])
            P.recip(rstd, rstd)
            for c in range(KC):
                P.stt(yt[:, c, :], x_[:, c, :], fg[:, c:c + 1], rstd, ALU.mult, ALU.mult)
            o_ = ot[t % 2]
            for bl in range(4):
                for c0 in range(0, KC, 4):
                    b = nbank(g)
                    for c in range(c0, c0 + 4):
                        P.tr(b[:, (c - c0) * 128:(c - c0 + 1) * 128], yt[:, c, bl * 128:(bl + 1) * 128], g.ident)
                    P.copy(o_[:, bl, c0 * 128:(c0 + 4) * 128], b, q=("act" if (c0 // 4) % 2 else "dve"))
            finals.append(P.dma(dview(g.d["out"][t * 512:(t + 1) * 512, :], "(b p) f -> p b f", p=128), o_))
    return finals


HD = 64
EPI_OFF = [3, 5]


def rot(g, key, lo, hi):
    c = g.rotc.get(key, 0)
    g.rotc[key] = c + 1
    return g.pb[lo + c % (hi - lo)]


def load_cast(g, S, st_pool, cnt, dst, src, cols, q="pool"):
    P = g.P
    st = st_pool[cnt[0] % len(st_pool)]
    cnt[0] += 1
    P.dma(st[:, :, 0:cols], src)
    P.copy(dst, st[:, :, 0:cols], q=q)
    return st


def emit_attn_proj(g, l):
    P = g.P
    even = (l % 2 == 0)
    e = l // 2
    xTd = g.d["xT"]
    if even:
        nqc, nkv = 4, 2
        wname, wsname = "even_w_in", "even_qk_sw"
        qcols, kcols, vcol0 = 512, 128, 640
    else:
        nqc, nkv = 8, 4
        wname, wsname = "odd_w_in", "odd_qk_sw"
        qcols, kcols, vcol0 = 1024, 256, 1280
    vcols = kcols
    win = dview(g.d[wname][e], "(kc p) n -> p kc n", p=128)
    wsw = dview(g.d[wsname][e], "(kc p) n -> p kc n", p=128)
    with P.scope() as S:
        wq = S.sb("wq", [128, KC, qcols], BF16)
        wqs = S.sb("wqs", [128, KC, qcols], BF16)
        wk2 = S.sb("wk2", [128, KC, nkv, 128], BF16)
        wks2 = S.sb("wks2", [128, KC, nkv, 128], BF16)
        wv = S.sb("wv", [128, KC, vcols], BF16)
        stp = [S.sb("pst%d" % i, [128, KC, 512]) for i in range(2)]
        cnt = [0]
        for s_ in range(qcols // 512):
            load_cast(g, S, stp, cnt, wq[:, :, s_ * 512:(s_ + 1) * 512], win[:, :, s_ * 512:(s_ + 1) * 512], 512)
            load_cast(g, S, stp, cnt, wqs[:, :, s_ * 512:(s_ + 1) * 512], wsw[:, :, s_ * 512:(s_ + 1) * 512], 512)
        for (dst2, srcw) in ((wk2, win), (wks2, wsw)):
            st = stp[cnt[0] % 2]
            cnt[0] += 1
            P.dma(st[:, :, 0:kcols], srcw[:, :, qcols:qcols + kcols])
            sv = st[:, :, 0:kcols].re("p k (g d) -> p k g d", d=64)
            P.copy(dst2[:, :, :, 0:64], sv, q="pool")
            P.copy(dst2[:, :, :, 64:128], sv, q="pool")
        load_cast(g, S, stp, cnt, wv, win[:, :, vcol0:vcol0 + vcols], vcols)
        if even:
            qg = S.sb("qg", [128, 4])
            for j, nm in enumerate(["q_gain", "q_gain_sw", "k_gain", "k_gain_sw"]):
                for hh in range(2):
                    P.dma(qg[hh * 64:(hh + 1) * 64, j:j + 1], dview(g.d[nm][e], "(d o) -> d o", o=1))
            blk = S.sb("blk", [128, 128], BF16)
            P.memset(blk, 0.0)
            P.memset(blk[0:64, 0:64], 1.0)
            P.memset(blk[64:128, 64:128], 1.0)
            sqb = [S.sb("sqb%d" % i, [128, 512], BF16) for i in range(2)]
            rs = [S.sb("rs%d" % i, [128, 512]) for i in range(2)]
        xt = S.sb("p_xt", [128, KC, 512])
        hT = S.sb("p_hT", [128, KC, 512], BF16)
        sq = [S.sb("p_sq%d" % i, [128, 512], BF16) for i in range(2)]
        rstd = S.sb("p_rstd", [128, 512])
        tmp = [S.sb("p_tmp%d" % i, [128, 512]) for i in range(2)]
        cs = S.sb("p_cos", [128, 512])
        sn = S.sb("p_sin", [128, 512])
        t1 = [S.sb("p_t1%d" % i, [128, 512]) for i in range(2)]
        t2 = [S.sb("p_t2%d" % i, [128, 512]) for i in range(2)]
        qTt = S.sb("p_qTt", [128, nqc, 512], BF16)
        kTt = S.sb("p_kTt", [128, nkv, 512], BF16)
        vt = S.sb("p_vt", [128, 4, nkv, 192], BF16)
        P.memset(vt, 1.0)
        if even:
            g.wrw = S.sb("wrw", [128, KC, 1920], BF16)
            wrw_d = dview(g.d["even_w_in"][e], "(kc p) n -> p kc n", p=128)
            for s_ in range(4):
                c0 = 768 + s_ * 512
                cw = min(512, 2688 - c0)
                load_cast(g, S, stp, cnt, g.wrw[:, :, s_ * 512:s_ * 512 + cw], wrw_d[:, :, c0:c0 + cw], cw)
            g.rwfo = [S.sb("rwfo%d" % i, [128, 512]) for i in range(2)]
        tiles = [(0, LCTX, 1, None)] + [(LCTX + t * 512, 512, 0, t * 512) for t in range(g.nlat // 512)]
        nr = 0
        for (t0, nt, n, a0) in tiles:
            P.dma(xt[:, :, 0:nt], dview(xTd[:, :, t0:t0 + nt], "c p t -> p c t"))
            emit_norm_mod(g, S, (sq, rstd, tmp), xt, hT, nt, l, 3, n)
            lat = a0 is not None
            if lat:
                P.dma(cs, g.d["cosT"][:, a0:a0 + 512])
                P.dma(sn, g.d["sinT"][:, a0:a0 + 512])
            jobs = [(wq[:, :, c * 128:(c + 1) * 128], wqs[:, :, c * 128:(c + 1) * 128], qTt[:, c, 0:nt], 0) for c in range(nqc)]
            jobs += [(wk2[:, :, c, :], wks2[:, :, c, :], kTt[:, c, 0:nt], 2) for c in range(nkv)]
            for (wa, wb_, dst, gi) in jobs:
                ba = rot(g, "pj", 0, 8)
                for kc in range(KC):
                    P.mm(ba[:, 0:nt], wa[:, kc, :], hT[:, kc, 0:nt], start=(kc == 0), stop=(kc == KC - 1))
                if lat:
                    bb = rot(g, "pj", 0, 8)
                    for kc in range(KC):
                        P.mm(bb[:, 0:nt], wb_[:, kc, :], hT[:, kc, 0:nt], start=(kc == 0), stop=(kc == KC - 1))
                if even:
                    s_ = sqb[nr % 2]
                    r_ = rs[nr % 2]
                    P.act(s_[:, 0:nt], ba[:, 0:nt], AF.Square)
                    bn = rot(g, "pj", 0, 8)
                    P.mm(bn[:, 0:nt], blk, s_[:, 0:nt])
                    P.act(r_[:, 0:nt], bn[:, 0:nt], AF.Sqrt, scale=1.0 / HD, bias=g.epsc[:, 0:1])
                    P.recip(r_[:, 0:nt], r_[:, 0:nt])
                a_ = t1[nr % 2]
                b_ = t2[nr % 2]
                nr += 1
                if even:
                    if lat:
                        P.stt(a_[:, 0:nt], ba[:, 0:nt], qg[:, gi:gi + 1], r_[:, 0:nt], ALU.mult, ALU.mult)
                        P.stt(b_[:, 0:nt], bb[:, 0:nt], qg[:, gi + 1:gi + 2], r_[:, 0:nt], ALU.mult, ALU.mult)
                        P.tt(a_[:, 0:nt], a_[:, 0:nt], cs[:, 0:nt], ALU.mult, q="pool")
                        P.tt(b_[:, 0:nt], b_[:, 0:nt], sn[:, 0:nt], ALU.mult, q="pool")
                        P.tt(dst, a_[:, 0:nt], b_[:, 0:nt], ALU.add, q="pool")
                    else:
                        P.stt(dst, ba[:, 0:nt], qg[:, gi:gi + 1], r_[:, 0:nt], ALU.mult, ALU.mult)
                else:
                    if lat:
                        P.tt(a_[:, 0:nt], ba[:, 0:nt], cs[:, 0:nt], ALU.mult)
                        P.tt(b_[:, 0:nt], bb[:, 0:nt], sn[:, 0:nt], ALU.mult)
                        P.tt(dst, a_[:, 0:nt], b_[:, 0:nt], ALU.add, q="pool")
                    else:
                        P.copy(dst, ba[:, 0:nt], q="act")
            for tb in range(nt // 128):
                bv = rot(g, "pj", 0, 8)
                for kc in range(KC):
                    P.mm(bv[:, 0:vcols], hT[:, kc, tb * 128:(tb + 1) * 128], wv[:, kc, :], start=(kc == 0), stop=(kc == KC - 1))
                P.copy(vt[:, tb, :, 64:128], bv[:, 0:vcols].re("p (g d) -> p g d", d=64), q="act")
            P.dma(dview(g.d["qT"][0:nqc, :, t0:t0 + nt], "c p t -> p c t"), qTt[:, :, 0:nt])
            P.dma(dview(g.d["kT2"][0:nkv, :, t0:t0 + nt], "c p t -> p c t"), kTt[:, :, 0:nt])
            P.dma(dview(g.d["Vd"][t0 // 128:(t0 + nt) // 128, :, 0:nkv * 192], "b p f -> p b f"),
                  vt[:, 0:nt // 128].re("p b g f -> p b (g f)"))
            if even:
                emit_rwkv_proj(g, S, l, hT, t0, nt)


def emit_rwkv_proj(g, S, l, hT, t0, nt):
    P = g.P
    for c in range(15):
        b = rot(g, "pj", 0, 8)
        for kc in range(KC):
            P.mm(b[:, 0:nt], g.wrw[:, kc, c * 128:(c + 1) * 128], hT[:, kc, 0:nt], start=(kc == 0), stop=(kc == KC - 1))
        fo = g.rwfo[c % 2]
        P.copy(fo[:, 0:nt], b[:, 0:nt], q=("act" if c % 2 else "dve"))
        P.dma(V(g.d["fT"].ap[c, :, t0:t0 + nt], None), fo[:, 0:nt])


def emit_attn(g, l):
    P = g.P
    even = (l % 2 == 0)
    o = l // 2
    ctx_out = l < g.depth - 1
    if even:
        nqc, nkv = 4, 2
    else:
        nqc, nkv = 8, 4
    NB = g.ntok // 128
    with P.scope() as S:
        kT2 = S.sb("a_kT2", [128, nkv, g.ntok], BF16)
        Va = S.sb("a_V", [128, NB, nkv, 192], BF16)
        for c in range(nkv):
            P.dma(kT2[:, c, :], V(g.d["kT2"].ap[c], None))
        P.dma(Va.re("p b g f -> p b (g f)"), dview(g.d["Vd"][:, :, 0:nkv * 192], "b p f -> p b f"))
        swapM = S.sb("a_swap", [128, 128])
        P.copy(swapM[:, 0:64], g.ident[:, 64:128])
        P.copy(swapM[:, 64:128], g.ident[:, 0:64])
        if not even:
            masks = S.sb("a_masks", [128, 6, 512], BF16)
            P.memset(masks, 1.0, q="pool")
            for r in range(-1, 5):
                ma = masks[:, r + 1, :].ap
                P.op("pool", lambda e, ma=ma, r=r: e.affine_select(ma, ma, [[1, 512]], ALU.is_ge, 0.0,
                                                                   base=-r * 128 + 128, channel_multiplier=-1), [masks], [masks])
                P.op("pool", lambda e, ma=ma, r=r: e.affine_select(ma, ma, [[-1, 512]], ALU.is_ge, 0.0,
                                                                   base=r * 128 + 128, channel_multiplier=1), [masks], [masks])
            eS = S.sb("a_sk", [128, 16])
            P.dma(eS, V(g.d["sink"].ap[o].rearrange("(o h) -> o h", o=1).broadcast_to([128, 16]), None))
            P.act(eS, eS, AF.Exp)
            padi = S.sb("a_padi", [128, 128], mybir.dt.int32)
            pia = padi.ap
            P.op("pool", lambda e: e.iota(pia, [[1, 128]], base=1, channel_multiplier=0), [], [padi])
            padcnt = S.sb("a_padcnt", [128, 128])
            P.copy(padcnt, padi)
        qz = [[S.sb("a_qz%d%d" % (hh, i), [128, nqc, 512], BF16) for i in range(2)] for hh in range(2)]
        for hh in range(2):
            for i in range(2):
                P.memset(qz[hh][i][(1 - hh) * 64:(2 - hh) * 64], 0.0)
        oT = [S.sb("a_oT%d" % i, [128, nqc, 512], BF16) for i in range(2)]
        pT = [S.sb("a_pT%d" % i, [128, 512], BF16) for i in range(8)]
        den = [S.sb("a_den%d" % i, [128, 512]) for i in range(3)]
        rshb = [S.sb("a_rsh%d" % i, [128, 512]) for i in range(3)]
        for i in range(3):
            P.memset(den[i], 1.0)
        tiles = []
        if ctx_out:
            tiles.append((0, LCTX, None))
        tiles += [(LCTX + t * 512, 512, t) for t in range(g.nlat // 512)]
        npT = 0
        nep = 0
        def load_q(ti_):
            t0_, nt_, _ = tiles[ti_]
            for hh_ in range(2):
                P.dma(qz[hh_][ti_ % 2][hh_ * 64:(hh_ + 1) * 64, :, 0:nt_],
                      dview(g.d["qT"][0:nqc, hh_ * 64:(hh_ + 1) * 64, t0_:t0_ + nt_], "c p t -> p c t"))
        load_q(0)
        deferred = []
        for ti, (t0, nt, tq) in enumerate(tiles):
            o_ = oT[ti % 2]
            qq = [qz[0][ti % 2], qz[1][ti % 2]]
            chunks = [(0, 0, nt, None), (1, 0, nt, None)]
            if tq is not None:
                if even:
                    chunks += [(2 + kb, 0, nt, None) for kb in range(g.nlat // 128)]
                else:
                    for r in range(-1, 5):
                        kb = tq * 4 + r
                        if 1 <= kb < g.nlat // 128:
                            chunks.append((2 + kb, max(0, r * 128 - 128), min(512, r * 128 + 256), r + 1))
            items = [(hc, ci, hh) for hc in range(nqc) for ci in range(len(chunks)) for hh in range(2)]
            LA = 4
            pbuf = {}

            def stage_a(idx):
                nonlocal npT
                hc, ci, hh = items[idx]
                kb, c0, c1, mi = chunks[ci]
                gk = hc // 2
                bS = rot(g, "at", 4, 7)
                P.mm(bS[:, c0:c1], kT2[:, gk, kb * 128:(kb + 1) * 128], qq[hh][:, hc, c0:c1])
                p_ = pT[npT % 8]
                npT += 1
                P.act(p_[:, c0:c1], bS[:, c0:c1], AF.Exp, scale=0.125)
                if mi is not None:
                    P.tt(p_[:, c0:c1], p_[:, c0:c1], masks[:, mi, c0:c1], ALU.mult)
                pbuf[idx] = p_

            def stage_b(idx):
                nonlocal nep
                hc, ci, hh = items[idx]
                kb, c0, c1, mi = chunks[ci]
                gk = hc // 2
                bo = g.pb[(hc % 2) * 2 + hh]
                p_ = pbuf.pop(idx)
                vsl = Va[:, kb, gk, 64:192] if hh == 0 else Va[:, kb, gk, 0:128]
                P.mm(bo[:, c0:c1], vsl, p_[:, c0:c1], start=(ci == 0), stop=(ci == len(chunks) - 1))
                if ci == len(chunks) - 1:
                    sr, orow = (1 - hh) * 64, hh * 64
                    h = 2 * hc + hh
                    d_ = den[nep % 3]
                    r_ = rshb[nep % 3]
                    nep += 1
                    if even:
                        P.recip(d_[sr:sr + 64, 0:nt], bo[sr:sr + 64, 0:nt])
                    else:
                        P.ts(d_[sr:sr + 64, 0:nt], bo[sr:sr + 64, 0:nt], eS[sr:sr + 64, h:h + 1], ALU.add)
                        if tq == g.nlat // 512 - 1:
                            P.tt(d_[sr:sr + 64, 384:512], d_[sr:sr + 64, 384:512], padcnt[sr:sr + 64, :], ALU.add)
                        P.recip(d_[sr:sr + 64, 0:nt], d_[sr:sr + 64, 0:nt])
                    st_ = {}

                    def e2a(d_=d_, st_=st_, nt=nt):
                        st_["b"] = g.pb[7]
                        P.mm(st_["b"][:, 0:nt], swapM, d_[:, 0:nt])

                    def e2b(r_=r_, st_=st_, orow=orow, nt=nt):
                        P.copy(r_[orow:orow + 64, 0:nt], st_["b"][orow:orow + 64, 0:nt], q="act")

                    def e3(r_=r_, bo=bo, o_=o_, orow=orow, hc=hc, nt=nt):
                        P.tt(o_[orow:orow + 64, hc, 0:nt], bo[orow:orow + 64, 0:nt], r_[orow:orow + 64, 0:nt], ALU.mult)
                    off = EPI_OFF[hh] if len(chunks) >= 6 else 0
                    deferred.append([off, e2a])
                    deferred.append([off + 2, e2b])
                    deferred.append([off + 6, e3])

            def run_deferred(flush=False):
                keep = []
                for it in deferred:
                    it[0] -= 1
                    if it[0] <= 0 or flush:
                        it[1]()
                    else:
                        keep.append(it)
                deferred[:] = keep

            for idx in range(len(items) + LA):
                g.bg.tick(8)
                if idx < len(items):
                    stage_a(idx)
                if idx >= LA:
                    stage_b(idx - LA)
                run_deferred()
                if idx == len(items) // 2 and ti + 1 < len(tiles):
                    load_q(ti + 1)
            while deferred:
                run_deferred(flush=True)
            P.dma(dview(g.d["oT"][0:nqc, :, t0:t0 + nt], "c p t -> p c t"), o_[:, :, 0:nt])


GN_EPS = 64e-5
CW = 128


def emit_rwkv(g, l):
    P = g.P
    e = l // 2
    NT = g.ntok
    NCH = NT // CW
    RW = g.rwdt
    d = g.d
    with P.scope() as S0:
        gC = S0.sb("rw_gC", [128, 2, 4, NCH])
        rwkv_features(g, l, S0, gC)
        with P.scope() as S:
            rwkv_scan(g, l, S, gC)
        with P.scope() as S:
            rwkv_readout(g, l, S)


def rwkv_features(g, l, S0, gC):
    P = g.P
    e = l // 2
    NT = g.ntok
    RW = g.rwdt
    d = g.d
    with P.scope() as S:
        NR = 30 + 8 + 8 + 4 + 4 + 4
        prow = S.sb("rw_prow", [NR, 128])
        P.dma(prow[0:30, :], dview(d["rwkv_mu"][e], "d (c p) -> (d c) p", p=128))
        P.dma(prow[30:38, :], dview(d["rwkv_w0"][e], "d (c p) -> (d c) p", p=128))
        P.dma(prow[38:46, :], dview(d["rwkv_a0"][e], "d (c p) -> (d c) p", p=128))
        P.dma(prow[46:50, :], dview(d["rwkv_k_k"][e], "(c p) -> c p", p=128))
        P.dma(prow[50:54, :], dview(d["rwkv_k_a"][e], "(c p) -> c p", p=128))
        P.dma(prow[54:58, :], V(d["rwkv_r_k"].ap[e].rearrange("h dk -> (h dk)").rearrange("(c p) -> c p", p=128), None))
        prm = S.sb("rw_prm", [128, NR])
        b = rot(g, "pj", 0, 8)
        P.tr(b[:, 0:NR], prow, g.ident[0:NR, 0:NR])
        P.copy(prm, b[:, 0:NR])
        mu0 = lambda c: prm[:, c:c + 1]
        mu1 = lambda c: prm[:, 15 + c:16 + c]
        w0c = lambda dd, j: prm[:, 30 + dd * 4 + j:31 + dd * 4 + j]
        a0c = lambda dd, j: prm[:, 38 + dd * 4 + j:39 + dd * 4 + j]
        kkc = lambda j: prm[:, 46 + j:47 + j]
        kac = lambda j: prm[:, 50 + j:51 + j]
        rkc = lambda j: prm[:, 54 + j:55 + j]
        mc = S.sb("rw_mc", [128, 15])
        P.tt(mc, prm[:, 0:15], prm[:, 15:30], ALU.add)
        P.ts(mc, mc, -1.0, ALU.mult, 1.0, ALU.add)
        omka = S.sb("rw_omka", [128, 4])
        P.ts(omka, prm[:, 50:54], -1.0, ALU.mult, 1.0, ALU.add)
        w2T = S.sb("rw_w2T", [128, 512])
        a2T = S.sb("rw_a2T", [128, 512])
        g2s = S.sb("rw_g2s", [128, 512])
        P.dma(w2T, dview(d["rwkv_w2"][e], "d r c -> (d r) c"))
        P.dma(a2T, dview(d["rwkv_a2"][e], "d r c -> (d r) c"))
        P.dma(g2s, d["rwkv_g2"][e])
        blkf = S.sb("rw_blkf", [128, 128])
        P.memset(blkf, 0.0)
        P.memset(blkf[0:64, 0:64], 1.0)
        P.memset(blkf[64:128, 64:128], 1.0)
        rst = S.sb("rw_rst", [128, 512])
        P.memset(rst, 1.0)
        P.memset(rst.re("p (c i) -> p c i", i=CW)[:, :, 0:1], 0.0)
        zer = S.sb("rw_zer", [128, 512])
        P.memset(zer, 0.0)
        fb = S.sb("rw_fb", [128, 15, 514])
        fs = S.sb("rw_fs", [128, 15, 512])
        nT = [0]
        tpool = [S.sb("rw_t%d" % i, [128, 512]) for i in range(14)]

        def tmp():
            t = tpool[nT[0] % len(tpool)]
            nT[0] += 1
            return t
        obuf = [S.sb("rw_ob%d" % i, [128, 512], RW) for i in range(4)]
        nO = [0]

        def ob():
            t = obuf[nO[0] % len(obuf)]
            nO[0] += 1
            return t
        g.rw_vt = S.sb("rw_vt", [128, 4, 512], RW)
        g.rw_kt = [S.sb("rw_kt%d" % i, [128, 4, 512], RW) for i in range(2)]
        g.rw_bt = [S.sb("rw_bt%d" % i, [128, 4, 512], RW) for i in range(2)]
        twl = S.sb("rw_twl", [128, 512])
        sgl = S.sb("rw_sgl", [128, 512])
        kk = S.sb("rw_kk", [128, 512])
        gob = [S.sb("rw_go%d" % i, [128, 512], BF16) for i in range(2)]
        identR = g.ident
        if RW != F32:
            identR = S.sb("rw_identb", [128, 128], RW)
            P.copy(identR, g.ident)
        tiles = [(0, LCTX, 0, LCTX)] + [(LCTX + t * 512, 512, LCTX, NT) for t in range(g.nlat // 512)]
        for (t0, nt, s0, s1) in tiles:
            nch = nt // CW
            ch0 = t0 // CW
            lo = 1 if t0 == s0 else 0
            hi = 1 if t0 + nt == s1 else 0
            if lo:
                P.memset(fb[:, :, 0:1], 0.0)
            if hi:
                P.memset(fb[:, :, nt + 1:nt + 2], 0.0)
            P.dma(fb[:, :, lo:nt + 2 - hi], dview(d["fT"][:, :, t0 - 1 + lo:t0 + nt + 1 - hi], "c p t -> p c t"))
            for c in range(15):
                t_ = tmp()
                P.ts(t_[:, 0:nt], fb[:, c, 1:nt + 1], mc[:, c:c + 1], ALU.mult)
                P.stt(t_[:, 0:nt], fb[:, c, 0:nt], mu0(c), t_[:, 0:nt], ALU.mult, ALU.add)
                P.stt(fs[:, c, 0:nt], fb[:, c, 2:nt + 2], mu1(c), t_[:, 0:nt], ALU.mult, ALU.add)
            R_ = lambda j: fs[:, j, 0:nt]
            K_ = lambda j: fs[:, 4 + j, 0:nt]
            V_ = lambda j: fs[:, 8 + j, 0:nt]
            P.act(twl[:, 0:nt], fs[:, 12, 0:nt], AF.Tanh)
            P.act(sgl[:, 0:nt], fs[:, 14, 0:nt], AF.Sigmoid)
            for j in range(4):
                kkr = tmp()
                P.ts(kkr[:, 0:nt], K_(j), kkc(j), ALU.mult)
                sq = tmp()
                P.act(sq[:, 0:nt], kkr[:, 0:nt], AF.Square)
                b = rot(g, "pj", 0, 8)
                P.mm(b[:, 0:nt], blkf, sq[:, 0:nt])
                rn = tmp()
                P.act(rn[:, 0:nt], b[:, 0:nt], AF.Sqrt, bias=g.epsc[:, 0:1])
                P.recip(rn[:, 0:nt], rn[:, 0:nt])
                P.tt(kk[:, 0:nt], kkr[:, 0:nt], rn[:, 0:nt], ALU.mult)
                b = rot(g, "pj", 0, 8)
                P.mm(b[:, 0:nt], g2s[:, j * 128:(j + 1) * 128], sgl[:, 0:nt])
                go = gob[j % 2]
                P.copy(go[:, 0:nt], b[:, 0:nt], q="act")
                P.dma(V(d["gT"].ap[j, :, t0:t0 + nt], None), go[:, 0:nt])
                rk = tmp()
                P.stt(rk[:, 0:nt], R_(j), rkc(j), K_(j), ALU.mult, ALU.mult)
                b = rot(g, "pj", 0, 8)
                P.mm(b[:, 0:nt], blkf, rk[:, 0:nt])
                bon = tmp()
                P.tt(bon[:, 0:nt], b[:, 0:nt], V_(j), ALU.mult)
                P.dma(V(d["bonT"].ap[j, :, t0:t0 + nt], None), bon[:, 0:nt])
                if RW != F32:
                    vr = ob()
                    P.copy(vr[:, 0:nt], V_(j), q="pool")
                    vsrc = vr
                else:
                    vsrc = fs[:, 8 + j, :]
                b = rot(g, "pj", 0, 8)
                bR = b.cast(RW) if RW != F32 else b
                for cb in range(nch):
                    P.tr(bR[:, cb * 128:(cb + 1) * 128], vsrc[:, cb * 128:(cb + 1) * 128], identR)
                vtile = g.rw_vt
                P.copy(vtile[:, 0:nch, j * 128:(j + 1) * 128], bR[:, 0:nt].re("p (c f) -> p c f", f=128), q="act")
                for dd in range(2):
                    p0 = dd * 64
                    b = rot(g, "pj", 0, 8)
                    P.mm(b[:, 0:nt], w2T[p0:p0 + 64, j * 128:(j + 1) * 128], twl[p0:p0 + 64, 0:nt])
                    lw = tmp()
                    P.act(lw[:, 0:nt], b[:, 0:nt], AF.Sigmoid, bias=w0c(dd, j))
                    P.ts(lw[:, 0:nt], lw[:, 0:nt], -0.6065306597126334, ALU.mult)
                    b = rot(g, "pj", 0, 8)
                    P.mm(b[:, 0:nt], a2T[p0:p0 + 64, j * 128:(j + 1) * 128], fs[p0:p0 + 64, 13, 0:nt])
                    a_ = tmp()
                    P.act(a_[:, 0:nt], b[:, 0:nt], AF.Sigmoid, bias=a0c(dd, j))
                    pf = tmp()
                    P.scan(pf[:, 0:nt], rst[:, 0:nt], lw[:, 0:nt], 0.0, ALU.mult, ALU.add)
                    if dd == 1:
                        pb = tmp()
                        P.tt(pb[:, 0:nt], lw[:, 0:nt], pf[:, 0:nt], ALU.subtract)
                        tot = pf[:, 0:nt].re("p (c i) -> p c i", i=CW)[:, :, CW - 1:CW].bc([128, nch, CW])
                        P.tt(pb[:, 0:nt].re("p (c i) -> p c i", i=CW), pb[:, 0:nt].re("p (c i) -> p c i", i=CW), tot, ALU.add)
                        pp = pb
                    else:
                        pp = pf
                    ep = tmp()
                    P.act(ep[:, 0:nt], pp[:, 0:nt], AF.Exp)
                    en = tmp()
                    P.act(en[:, 0:nt], pp[:, 0:nt], AF.Exp, scale=-1.0)
                    pm = tmp()
                    P.tt(pm[:, 0:nt], pp[:, 0:nt], lw[:, 0:nt], ALU.subtract)
                    P.act(pm[:, 0:nt], pm[:, 0:nt], AF.Exp)
                    epv = ep[:, 0:nt].re("p (c i) -> p c i", i=CW)
                    idx = CW - 1 if dd == 0 else 0
                    P.copy(gC[:, dd, j, ch0:ch0 + nch], epv[:, :, idx], q="pool")
                    kkt = ob()
                    P.tt(kkt[:, 0:nt], kk[:, 0:nt], pm[:, 0:nt], ALU.mult)
                    rt = ob()
                    P.tt(rt[:, 0:nt], R_(j), ep[:, 0:nt], ALU.mult)
                    qd = V(d["QR"].ap[dd, j, :, ch0 * 2 * CW:(ch0 + nch) * 2 * CW].rearrange("p (c w i) -> p c w i", w=2, i=CW), None)
                    P.dma(qd[:, :, 0, :], kkt[:, 0:nt].re("p (c i) -> p c i", i=CW))
                    P.dma(qd[:, :, 1, :], rt[:, 0:nt].re("p (c i) -> p c i", i=CW))
                    kd = tmp()
                    P.ts(kd[:, 0:nt], a_[:, 0:nt], kac(j), ALU.mult, omka[:, j:j + 1], ALU.add)
                    P.tt(kd[:, 0:nt], kd[:, 0:nt], K_(j), ALU.mult)
                    kh = ob()
                    P.tt(kh[:, 0:nt], kd[:, 0:nt], en[:, 0:nt], ALU.mult)
                    bd = tmp()
                    P.tt(bd[:, 0:nt], kk[:, 0:nt], a_[:, 0:nt], ALU.mult)
                    bh = ob()
                    P.tt(bh[:, 0:nt], bd[:, 0:nt], en[:, 0:nt], ALU.mult)
                    P.dma(V(d["KH"].ap[dd, j, :, t0:t0 + nt], None), kh[:, 0:nt])
                    P.dma(V(d["BH"].ap[dd, j, :, t0:t0 + nt], None), bh[:, 0:nt])
                    for (src, dst) in ((kh, g.rw_kt[dd]), (bh, g.rw_bt[dd])):
                        b = rot(g, "pj", 0, 8)
                        bR = b.cast(RW) if RW != F32 else b
                        for cb in range(nch):
                            P.tr(bR[:, cb * 128:(cb + 1) * 128], src[:, cb * 128:(cb + 1) * 128], identR)
                        P.copy(dst[:, 0:nch, j * 128:(j + 1) * 128], bR[:, 0:nt].re("p (c f) -> p c f", f=128), q="act")
            P.dma(dview(d["Vt"][ch0:ch0 + nch], "c p f -> p c f"), g.rw_vt[:, 0:nch, :])
            for dd in range(2):
                P.dma(dview(d["KHt"][dd, ch0:ch0 + nch], "c p f -> p c f"), g.rw_kt[dd][:, 0:nch, :])
                P.dma(dview(d["BHt"][dd, ch0:ch0 + nch], "c p f -> p c f"), g.rw_bt[dd][:, 0:nch, :])


def rwkv_scan(g, l, S, gC):
    P = g.P
    NT = g.ntok
    NCH = NT // CW
    RW = g.rwdt
    d = g.d
    mskA = [S.sb("rw_mA%d" % i, [128, 256]) for i in range(2)]
    mskN = [S.sb("rw_mN%d" % i, [128, 128]) for i in range(2)]

    def tri(v, step, cm, op):
        a = v.ap
        P.memset(v, 1.0, q="pool")
        P.op("pool", lambda e_: e_.affine_select(a, a, [[step, 128]], op, 0.0, base=0, channel_multiplier=cm), [v], [v])
    tri(mskA[0][:, 0:128], 1, -1, ALU.is_gt)
    tri(mskA[0][:, 128:256], 1, -1, ALU.is_ge)
    tri(mskN[0], -1, 1, ALU.is_gt)
    tri(mskA[1][:, 0:128], -1, 1, ALU.is_gt)
    tri(mskA[1][:, 128:256], -1, 1, ALU.is_ge)
    tri(mskN[1], 1, -1, ALU.is_gt)
    for dd in range(2):
        P.ts(mskN[dd], mskN[dd], -1.0, ALU.mult)
    Hs = [[S.sb("rw_H%d%d" % (dd, i), [128, 4, 64], RW) for i in range(2)] for dd in range(2)]
    for dd in range(2):
        P.memset(Hs[dd][0], 0.0)
    QRb = [S.sb("rw_QRb%d" % dd, [128, 4, 256], RW) for dd in range(2)]
    KHb = [S.sb("rw_KHb%d" % dd, [128, 4, 128], RW) for dd in range(2)]
    BHb = [S.sb("rw_BHb%d" % dd, [128, 4, 128], RW) for dd in range(2)]
    KHtb = [S.sb("rw_KHtb%d" % dd, [128, 512], RW) for dd in range(2)]
    BHtb = [S.sb("rw_BHtb%d" % dd, [128, 512], RW) for dd in range(2)]
    Vtb = [S.sb("rw_Vtb%d" % dd, [128, 512], RW) for dd in range(2)]
    A1s = [S.sb("rw_A1s%d" % dd, [128, 8, 256], RW) for dd in range(2)]
    A2s = [S.sb("rw_A2s%d" % dd, [128, 8, 256], RW) for dd in range(2)]
    ZR = [[S.sb("rw_ZR%d%d" % (dd, i), [128, 8, 256], RW) for i in range(2)] for dd in range(2)]
    Ys = [[S.sb("rw_Y%d%d" % (dd, i), [128, 8, 128], RW) for i in range(2)] for dd in range(2)]
    Rfin = [None, None]
    Wsb = [S.sb("rw_W%d" % dd, [128, 8, 64], RW) for dd in range(2)]
    Un = [S.sb("rw_Un%d" % dd, [128, 8, 64], RW) for dd in range(2)]
    yo = [S.sb("rw_yo%d" % dd, [128, 4, 128]) for dd in range(2)]
    identb = g.ident.re("p (o f) -> p o f", o=1).bc([128, 4, 128])
    nctx = LCTX // CW
    order = [list(range(NCH)), list(range(nctx - 1, -1, -1)) + list(range(NCH - 1, nctx - 1, -1))]
    ev = [0]

    fr = False

    def rr(v):
        return v.cast(F32R) if fr else v

    def evac(dst, src):
        P.copy(dst, src, q=("act" if ev[0] % 2 else "dve"))
        ev[0] += 1

    def prod8(dd, lhs, rhs, evac_fn):
        for half in range(2):
            b = rot(g, "rw", 0, 8)
            bv = b.re("p (h f) -> p h f", f=128)
            for hh in range(4):
                h = half * 4 + hh
                P.mm(bv[:, hh, :], rr(lhs[:, h, :]), rr(rhs[:, h, :]))
            evac_fn(half, bv)

    for step in range(NCH):
        cur = step % 2
        nxt = 1 - cur
        cs = [order[0][step], order[1][step]]
        for dd in range(2):
            c = cs[dd]
            P.dma(QRb[dd], dview(d["QR"][dd, :, :, c * 256:(c + 1) * 256], "j p f -> p j f"))
            P.dma(KHb[dd], dview(d["KH"][dd, :, :, c * CW:(c + 1) * CW], "j p f -> p j f"))
            P.dma(BHb[dd], dview(d["BH"][dd, :, :, c * CW:(c + 1) * CW], "j p f -> p j f"))
            P.dma(KHtb[dd], V(d["KHt"].ap[dd, c], None))
            P.dma(BHtb[dd], V(d["BHt"].ap[dd, c], None))
            P.dma(Vtb[dd], V(d["Vt"].ap[c], None))
        for dd in range(2):
            for j in range(4):
                b1 = rot(g, "rw", 0, 8)
                b2 = rot(g, "rw", 0, 8)
                b3 = rot(g, "rw", 0, 8)
                b1v = b1.re("p (h f) -> p h f", f=256)
                b2v = b2.re("p (h f) -> p h f", f=256)
                b3v = b3[:, 0:256].re("p (h f) -> p h f", f=128)
                for hp in range(2):
                    p0 = hp * 64
                    P.mm(b1v[:, hp, :], KHb[dd][p0:p0 + 64, j, :], QRb[dd][p0:p0 + 64, j, :])
                    P.mm(b2v[:, hp, :], BHb[dd][p0:p0 + 64, j, :], QRb[dd][p0:p0 + 64, j, :])
                    P.mm(b3v[:, hp, :], QRb[dd][p0:p0 + 64, j, 0:128], BHb[dd][p0:p0 + 64, j, :])
                mA = mskA[dd].re("p (o f) -> p o f", o=1).bc([128, 2, 256])
                mN = mskN[dd].re("p (o f) -> p o f", o=1).bc([128, 2, 128])
                P.tt(A1s[dd][:, 2 * j:2 * j + 2, :], b1v, mA, ALU.mult)
                P.tt(A2s[dd][:, 2 * j:2 * j + 2, :], b2v, mA, ALU.mult)
                P.tt(rr(Ys[dd][0][:, 2 * j:2 * j + 2, :]), b3v, mN, ALU.mult)
            P.ts(ZR[dd][0][:, :, 0:128], A2s[dd][:, :, 0:128], -1.0, ALU.mult)
            for half in range(2):
                P.copy(ZR[dd][0][:, half * 4:half * 4 + 4, 128:256], identb, q="pool")
        nlev = 7
        for lev in range(nlev):
            a, bn = lev % 2, (lev + 1) % 2
            last = (lev == nlev - 1)
            for dd in range(2):
                zr, zn, yz = ZR[dd][a], ZR[dd][bn], Ys[dd][a]
                for hp2 in range(4):
                    b = rot(g, "rw", 0, 8)
                    bv = b.re("p (h f) -> p h f", f=256)
                    for hh in range(2):
                        h = hp2 * 2 + hh
                        if last:
                            P.mm(bv[:, hh, 128:256], yz[:, h, :], zr[:, h, 128:256])
                        else:
                            P.mm(bv[:, hh, :], yz[:, h, :], zr[:, h, :])
                    h0 = hp2 * 2
                    if not last:
                        evac(zn[:, h0:h0 + 2, 0:128], bv[:, :, 0:128])
                    P.tt(zn[:, h0:h0 + 2, 128:256], bv[:, :, 128:256], zr[:, h0:h0 + 2, 128:256], ALU.add)
            if not last:
                for dd in range(2):
                    zn, yn = ZR[dd][bn], Ys[dd][bn]
                    for half in range(2):
                        b = rot(g, "rw", 0, 8)
                        bv = b.re("p (h f) -> p h f", f=128)
                        for hh in range(4):
                            h = half * 4 + hh
                            P.tr(bv[:, hh, :], zn[:, h, 0:128], g.ident)
                        evac(yn[:, half * 4:half * 4 + 4, :], bv)
        for dd in range(2):
            Rfin[dd] = ZR[dd][nlev % 2]
        for dd in range(2):
            H0 = Hs[dd][cur]
            bW = rot(g, "rw", 0, 8)
            bWv = bW.re("p (h f) -> p h f", f=64)
            for h in range(8):
                j, p0 = h // 2, (h % 2) * 64
                P.mm(bWv[:, h, :], A1s[dd][:, h, 0:128], Vtb[dd][:, h * 64:(h + 1) * 64], start=True, stop=False)
                P.mm(bWv[:, h, :], QRb[dd][p0:p0 + 64, j, 0:128], H0[p0:p0 + 64, j, :], start=False, stop=True)
            evac(rr(Wsb[dd]), bWv)
        for dd in range(2):
            bU = rot(g, "rw", 0, 8)
            bUv = bU.re("p (h f) -> p h f", f=64)
            for h in range(8):
                P.mm(bUv[:, h, :], Rfin[dd][:, h, 128:256], Wsb[dd][:, h, :])
            P.act(Un[dd], bUv, AF.Copy, scale=-1.0)
        for dd in range(2):
            c = cs[dd]
            H0 = Hs[dd][cur]
            H1 = Hs[dd][nxt]
            bY = rot(g, "rw", 0, 8)
            bYv = bY.re("p (j f) -> p j f", f=128)
            for h in range(8):
                j, p0 = h // 2, (h % 2) * 64
                P.mm(bYv[p0:p0 + 64, j, :], H0[p0:p0 + 64, j, :], QRb[dd][p0:p0 + 64, j, 128:256], start=True, stop=False)
                P.mm(bYv[p0:p0 + 64, j, :], Vtb[dd][:, h * 64:(h + 1) * 64], A1s[dd][:, h, 128:256], start=False, stop=False)
                P.mm(bYv[p0:p0 + 64, j, :], Un[dd][:, h, :], A2s[dd][:, h, 128:256], start=False, stop=True)
            evac(yo[dd], bYv)
            P.dma(dview(d["yT"][dd, :, :, c * CW:(c + 1) * CW], "j p f -> p j f"), yo[dd])
            bH = rot(g, "rw", 0, 8)
            bHv = bH[:, 0:256].re("p (j f) -> p j f", f=64)
            for h in range(8):
                j, p0 = h // 2, (h % 2) * 64
                P.mm(bHv[p0:p0 + 64, j, :], KHtb[dd][:, h * 64:(h + 1) * 64], Vtb[dd][:, h * 64:(h + 1) * 64], start=True, stop=False)
                P.mm(bHv[p0:p0 + 64, j, :], BHtb[dd][:, h * 64:(h + 1) * 64], Un[dd][:, h, :], start=False, stop=True)
            P.tt(H1, bHv, H0, ALU.add)
            P.tt(H1, H1, gC[:, dd, :, c:c + 1].bc([128, 4, 64]), ALU.mult)


def rwkv_readout(g, l, S):
    P = g.P
    e = l // 2
    d = g.d
    gn = S.sb("rw_gn", [128, 8])
    rows_to_cols(g, S, gn[:, 0:4], dview(d["rwkv_gn_w"][e], "(c p) -> c p", p=128), 4, "rw_gnws")
    rows_to_cols(g, S, gn[:, 4:8], dview(d["rwkv_gn_b"][e], "(c p) -> c p", p=128), 4, "rw_gnbs")
    blkf = S.sb("rw_blkf2", [128, 128])
    P.memset(blkf, 0.0)
    P.memset(blkf[0:64, 0:64], 1.0)
    P.memset(blkf[64:128, 64:128], 1.0)
    gne = S.sb("rw_gne", [128, 1])
    P.memset(gne, GN_EPS)
    nb = 2
    yf = [S.sb("ro_yf%d" % i, [128, 512]) for i in range(nb)]
    yb = [S.sb("ro_yb%d" % i, [128, 512]) for i in range(nb)]
    bo_ = [S.sb("ro_bon%d" % i, [128, 512]) for i in range(nb)]
    gt = [S.sb("ro_g%d" % i, [128, 512], BF16) for i in range(nb)]
    sq = [S.sb("ro_sq%d" % i, [128, 512]) for i in range(nb)]
    mean = [S.sb("ro_mean%d" % i, [128, 512]) for i in range(nb)]
    var = [S.sb("ro_var%d" % i, [128, 512]) for i in range(nb)]
    oo = [S.sb("ro_o%d" % i, [128, 512], BF16) for i in range(nb)]
    ctx_out = l < g.depth - 1
    tiles = ([(0, LCTX)] if ctx_out else []) + [(LCTX + t * 512, 512) for t in range(g.nlat // 512)]
    n = 0
    for (t0, nt) in tiles:
        for j in range(4):
            i = n % nb
            n += 1
            y_, y2, b_, g_, s_, m_, v_, o_ = yf[i], yb[i], bo_[i], gt[i], sq[i], mean[i], var[i], oo[i]
            P.dma(y_[:, 0:nt], V(d["yT"].ap[0, j, :, t0:t0 + nt], None))
            P.dma(y2[:, 0:nt], V(d["yT"].ap[1, j, :, t0:t0 + nt], None))
            P.dma(b_[:, 0:nt], V(d["bonT"].ap[j, :, t0:t0 + nt], None))
            P.dma(g_[:, 0:nt], V(d["gT"].ap[j, :, t0:t0 + nt], None))
            P.tt(y_[:, 0:nt], y_[:, 0:nt], y2[:, 0:nt], ALU.add, q="pool")
            P.act(s_[:, 0:nt], y_[:, 0:nt], AF.Square)
            b1 = rot(g, "pj", 0, 8)
            b2 = rot(g, "pj", 0, 8)
            P.mm(b1[:, 0:nt], blkf, y_[:, 0:nt])
            P.mm(b2[:, 0:nt], blkf, s_[:, 0:nt])
            P.act(m_[:, 0:nt], b1[:, 0:nt], AF.Copy, scale=1.0 / HD)
            P.tt(s_[:, 0:nt], m_[:, 0:nt], m_[:, 0:nt], ALU.mult, q="pool")
            P.stt(v_[:, 0:nt], b2[:, 0:nt], 1.0 / HD, s_[:, 0:nt], ALU.mult, ALU.subtract)
            P.act(v_[:, 0:nt], v_[:, 0:nt], AF.Sqrt, bias=gne[:, 0:1])
            P.recip(v_[:, 0:nt], v_[:, 0:nt])
            P.tt(y_[:, 0:nt], y_[:, 0:nt], m_[:, 0:nt], ALU.subtract, q="pool")
            P.tt(y_[:, 0:nt], y_[:, 0:nt], v_[:, 0:nt], ALU.mult)
            P.ts(y_[:, 0:nt], y_[:, 0:nt], gn[:, j:j + 1], ALU.mult, gn[:, 4 + j:5 + j], ALU.add)
            P.tt(y_[:, 0:nt], y_[:, 0:nt], b_[:, 0:nt], ALU.add, q="pool")
            P.tt(o_[:, 0:nt], y_[:, 0:nt], g_[:, 0:nt], ALU.mult)
            P.dma(V(d["oT"].ap[4 + j, :, t0:t0 + nt], None), o_[:, 0:nt])


def emit_outproj(g, l):
    P = g.P
    even = (l % 2 == 0)
    ctx_out = l < g.depth - 1
    wo_d = dview(g.d["even_w_out" if even else "odd_w_out"][l // 2], "(kc p) n -> p kc n", p=128)
    xTd = g.d["xT"]
    with P.scope() as S:
        wo = S.sb("wo", [128, KC, D], BF16)
        stp = [S.sb("ost%d" % i, [128, KC, 512]) for i in range(2)]
        cnt = [0]
        for s_ in range(2):
            load_cast(g, S, stp, cnt, wo[:, :, s_ * 512:(s_ + 1) * 512], wo_d[:, :, s_ * 512:(s_ + 1) * 512], 512)
        oTt = [S.sb("op_oT%d" % i, [128, KC, 512], BF16) for i in range(2)]
        xt = [S.sb("op_xt%d" % i, [128, KC, 512]) for i in range(2)]
        tiles = []
        if ctx_out:
            tiles.append((0, LCTX, 1))
        tiles += [(LCTX + t * 512, 512, 0) for t in range(g.nlat // 512)]
        for ti, (t0, nt, n) in enumerate(tiles):
            o_ = oTt[ti % 2]
            x_ = xt[ti % 2]
            P.dma(o_[:, :, 0:nt], dview(g.d["oT"][:, :, t0:t0 + nt], "c p t -> p c t"))
            P.dma(x_[:, :, 0:nt], dview(xTd[:, :, t0:t0 + nt], "c p t -> p c t"))
            for c in range(KC):
                b = rot(g, "op", 0, 8)
                for fc in range(KC):
                    P.mm(b[:, 0:nt], wo[:, fc, c * 128:(c + 1) * 128], o_[:, fc, 0:nt], start=(fc == 0), stop=(fc == KC - 1))
                P.stt(x_[:, c, 0:nt], b[:, 0:nt], mcol(g.modT, l, 5, c, n), x_[:, c, 0:nt], ALU.mult, ALU.add)
            P.dma(dview(xTd[:, :, t0:t0 + nt], "c p t -> p c t"), x_[:, :, 0:nt])


WEIGHT_SPECS = [
    ("w_mod", [4, D, 9 * D]), ("b_mod", [4, 9 * D]), ("ffn_in", [4, 2, D, 2 * DFF]), ("ffn_out", [4, 2, DFF, D]),
    ("final_gain", [D]),
    ("odd_w_in", [2, D, 1536]), ("odd_qk_sw", [2, D, 1280]), ("odd_w_out", [2, D, D]), ("sink", [2, 16]),
    ("even_w_in", [2, D, 2688]), ("even_qk_sw", [2, D, 640]), ("even_w_out", [2, D, D]),
    ("q_gain", [2, 64]), ("q_gain_sw", [2, 64]), ("k_gain", [2, 64]), ("k_gain_sw", [2, 64]),
    ("rwkv_mu", [2, 2, 1920]), ("rwkv_w0", [2, 2, 512]), ("rwkv_w2", [2, 2, 64, 512]), ("rwkv_a0", [2, 2, 512]),
    ("rwkv_a2", [2, 2, 64, 512]), ("rwkv_g2", [2, 128, 512]), ("rwkv_k_k", [2, 512]), ("rwkv_k_a", [2, 512]),
    ("rwkv_r_k", [2, 8, 64]), ("rwkv_gn_w", [2, 512]), ("rwkv_gn_b", [2, 512]),
]


def default_plan(depth):
    plan = []
    for l in range(depth):
        plan += [("ffn1", l), ("mix", l), ("ffn2", l)]
    return plan


def build(nlat=4096, depth=4, plan=None, debug=False, rwdt=F32):
    nc = bass.Bass("TRN2", target_bir_lowering=False)
    es = ExitStack()
    with es:
        g = G()
        g.P = P = Prog(nc, es)
        g.nlat = nlat
        g.depth = depth
        g.ntok = LCTX + nlat
        g.rotc = {}
        g.d = {}
        g.d["x"] = P.dram("x", [nlat, D], kind="ExternalInput")
        g.d["ctx"] = P.dram("ctx", [LCTX, D], kind="ExternalInput")
        g.d["cvec"] = P.dram("cvec", [2, D], kind="ExternalInput")
        g.d["cosT"] = P.dram("cosT", [128, nlat], kind="ExternalInput")
        g.d["sinT"] = P.dram("sinT", [128, nlat], kind="ExternalInput")
        for nm, shp in WEIGHT_SPECS:
            g.d[nm] = P.dram(nm, shp, kind="ExternalInput")
        g.d["out"] = P.dram("out", [nlat, D], kind="ExternalOutput")
        g.d["xT"] = P.dram("xT", [KC, 128, g.ntok], kind="Internal")
        sk = "ExternalOutput" if debug else "Internal"
        g.d["qT"] = P.dram("qT", [KC, 128, g.ntok], BF16, kind=sk)
        g.d["kT2"] = P.dram("kT2", [4, 128, g.ntok], BF16, kind=sk)
        g.d["Vd"] = P.dram("Vd", [g.ntok // 128, 128, 4 * 192], BF16, kind=sk)
        g.d["oT"] = P.dram("oT", [KC, 128, g.ntok], BF16, kind=sk)
        g.rwdt = rwdt
        nch = g.ntok // CW
        g.d["fT"] = P.dram("fT", [15, 128, g.ntok], kind=sk)
        g.d["QR"] = P.dram("QR", [2, 4, 128, nch * 2 * CW], rwdt, kind=sk)
        g.d["KH"] = P.dram("KH", [2, 4, 128, g.ntok], rwdt, kind=sk)
        g.d["BH"] = P.dram("BH", [2, 4, 128, g.ntok], rwdt, kind=sk)
        g.d["KHt"] = P.dram("KHt", [2, nch, 128, 512], rwdt, kind=sk)
        g.d["BHt"] = P.dram("BHt", [2, nch, 128, 512], rwdt, kind=sk)
        g.d["Vt"] = P.dram("Vt", [nch, 128, 512], rwdt, kind=sk)
        g.d["gT"] = P.dram("gT", [4, 128, g.ntok], BF16, kind=sk)
        g.d["bonT"] = P.dram("bonT", [4, 128, g.ntok], kind=sk)
        g.d["yT"] = P.dram("yT", [2, 4, 128, g.ntok], kind=sk)
        g.d["w_in_b"] = P.dram("w_in_b", [2 * depth, FC // 2, 128, KC * 512], BF16, kind="Internal")
        g.d["w_out_b"] = P.dram("w_out_b", [2 * depth, 128, FC * D], BF16, kind="Internal")
        setup_consts(g)
        g.bg = BgPrep(g)
        g.epsc = P.sb("epsc", [128, 1])
        P.memset(g.epsc, EPS)
        emit_mod(g)
        emit_in_transpose(g)
        for (st, l) in (plan if plan is not None else default_plan(depth)):
            if st == "ffn1":
                emit_ffn(g, l, 0)
            elif st == "ffn2":
                emit_ffn(g, l, 1)
            elif st == "mix":
                g.bg.limit_k = 2 * l + 2
                emit_attn_proj(g, l)
                emit_attn(g, l)
                if l % 2 == 0:
                    emit_rwkv(g, l)
                emit_outproj(g, l)
        finals = emit_final(g)
        P.emit(finals)
    return nc


def rope_tables(nlat):
    n = np.arange(nlat)
    row = (n // 64).astype(np.float32)
    col = (n % 64).astype(np.float32)
    nf = 16
    inv = (np.float32(10000.0) ** (-np.arange(nf, dtype=np.float32) / np.float32(nf))).astype(np.float32)
    ang = np.concatenate([row[:, None] * inv, col[:, None] * inv], axis=-1).astype(np.float32)
    cos, sin = np.cos(ang).astype(np.float32), np.sin(ang).astype(np.float32)
    d = np.arange(64)
    cosT = cos[:, d // 2].T
    sgn = np.where(d % 2 == 0, -1.0, 1.0).astype(np.float32)
    sinT = (sin[:, d // 2] * sgn[None, :]).T
    return (np.ascontiguousarray(np.concatenate([cosT, cosT], 0)), np.ascontiguousarray(np.concatenate([sinT, sinT], 0)))


def host_layout(inputs, b, nlat=4096):
    f = lambda a: np.ascontiguousarray(np.asarray(a, dtype=np.float32))
    sw = lambda w, n: f(w[..., (np.arange(n) ^ 1)])
    cosT, sinT = rope_tables(nlat)
    m = {
        "x": f(inputs["x"][b, :nlat]), "ctx": f(inputs["ctx"][b]),
        "cvec": f(np.stack([np.asarray(inputs["c"][b]), np.asarray(inputs["c_ctx"])])),
        "cosT": cosT, "sinT": sinT,
        "odd_qk_sw": sw(np.asarray(inputs["odd_w_in"])[:, :, :1280], 1280),
        "even_qk_sw": sw(np.asarray(inputs["even_w_in"])[:, :, :640], 640),
        "q_gain_sw": sw(np.asarray(inputs["q_gain"]), 64), "k_gain_sw": sw(np.asarray(inputs["k_gain"]), 64),
    }
    for nm, _ in WEIGHT_SPECS:
        if nm not in m:
            m[nm] = f(inputs[nm])
    return m


_NC_CACHE = {}


def kernel(**inputs):
    nlat = int(np.asarray(inputs["x"]).shape[1])
    nb = int(np.asarray(inputs["x"]).shape[0])
    if "nc" not in _NC_CACHE:
        _NC_CACHE["nc"] = build(nlat=nlat, depth=4)
    nc = _NC_CACHE["nc"]
    in_maps = [host_layout(inputs, b, nlat) for b in range(nb)]
    res = run_bass_kernel_spmd(nc, in_maps, core_ids=list(range(nb)))
    return np.stack([np.asarray(r["out"], dtype=np.float32) for r in res.results], axis=0)
```

```python
import numpy as np
from contextlib import ExitStack
import concourse.bass as bass
import concourse.mybir as mybir
from concourse.bass_utils import run_bass_kernel_spmd

F32 = mybir.dt.float32
BF16 = mybir.dt.bfloat16
F32R = mybir.dt.float32r
AF = mybir.ActivationFunctionType
ALU = mybir.AluOpType
AX = mybir.AxisListType


class Tile:
    __slots__ = ("name", "lw", "rd", "dsem", "dcnt", "dlast", "dram")

    def __init__(self, name, dram=False):
        self.name = name
        self.lw = None
        self.rd = []
        self.dsem = None
        self.dcnt = 0
        self.dlast = None
        self.dram = dram


class V:
    __slots__ = ("ap", "t")

    def __init__(self, ap, t):
        self.ap = ap
        self.t = t

    def __getitem__(self, idx):
        return V(self.ap[idx], self.t)

    def re(self, pattern, **kw):
        return V(self.ap.rearrange(pattern, **kw), self.t)

    def bc(self, shape):
        return V(self.ap.broadcast_to(shape), self.t)

    def cast(self, dt):
        return V(self.ap.bitcast(dt), self.t)

    @property
    def shape(self):
        return self.ap.shape


class Ins:
    __slots__ = ("q", "fn", "deps", "dma", "sem", "val", "signal", "idx")

    def __init__(self, q, fn, dma=False):
        self.q = q
        self.fn = fn
        self.deps = []
        self.dma = dma
        self.sem = None
        self.val = 0
        self.signal = dma
        self.idx = 0


QUEUES = ("pe", "act", "dve", "pool", "sp")


class Scope:
    def __init__(self, P):
        self.P = P
        self.es = ExitStack()

    def __enter__(self):
        self.es.__enter__()
        self.tiles = []
        self.P.scope_stack.append(self.tiles)
        return self

    def sb(self, name, shape, dt=F32):
        self.P.ntile += 1
        name = "%s_%d" % (name, self.P.ntile)
        t = self.es.enter_context(self.P.nc.sbuf_tensor(name, list(shape), dt))
        return V(t[:], Tile(name))

    def __exit__(self, *a):
        P = self.P
        P.barrier()
        for t in self.tiles:
            if t.dsem is not None:
                P.live_sems.remove(t.dsem)
                P.free_sems.append(t.dsem)
                t.dsem = None
        P.scope_stack.pop()
        return self.es.__exit__(*a)


class Prog:
    def __init__(self, nc, es):
        self.nc = nc
        self.es = es
        self.q = {k: [] for k in QUEUES}
        self.esem = {}
        for k in ("pe", "act", "dve", "pool"):
            self.esem[k] = es.enter_context(nc.semaphore("sem_" + k))
        self.ntile = 0
        self.nsem = 4
        self.bar = {}
        self.free_sems = []
        self.live_sems = []
        self.scope_stack = []

    def sb(self, name, shape, dt=F32):
        t = self.es.enter_context(self.nc.sbuf_tensor(name, list(shape), dt))
        return V(t[:], Tile(name))

    def ps(self, name, shape, dt=F32):
        t = self.es.enter_context(self.nc.psum_tensor(name, list(shape), dt))
        return V(t[:], Tile(name))

    def dram(self, name, shape, dt=F32, kind="Internal"):
        t = self.nc.dram_tensor(name, list(shape), dt, kind=kind)
        return V(t.ap(), None)

    def sub(self, v, name):
        return V(v.ap, Tile(name))

    def _rec(self, q, fn, reads, writes, dma=False):
        ins = Ins(q, fn, dma)
        compute_inorder = (not dma) and q == "pe"
        deps = []
        reads = [v for v in reads if isinstance(v, V) and v.t is not None]
        writes = [v for v in writes if isinstance(v, V) and v.t is not None]
        for v in reads:
            t = v.t
            w = t.lw
            if w is not None:
                deps.append(w)
        for v in writes:
            t = v.t
            w = t.lw
            if w is not None and not (compute_inorder and not w.dma and w.q == q):
                deps.append(w)
            for r in t.rd:
                if not (compute_inorder and not r.dma and r.q == q):
                    deps.append(r)
        for v in reads:
            v.t.rd.append(ins)
        for v in writes:
            v.t.lw = ins
            v.t.rd = []
        if self.bar.get(q):
            deps.extend(self.bar[q])
            self.bar[q] = None
        for d in deps:
            if d is not ins:
                d.signal = True
        ins.deps = deps
        ins.idx = len(self.q[q])
        self.q[q].append(ins)
        return ins

    def op(self, q, fn, reads, writes):
        return self._rec(q, fn, reads, writes)

    def dma(self, out, in_, q="sp", **kw):
        st = in_.t if out.t is None else out.t
        if st.dsem is None:
            if self.free_sems:
                st.dsem = self.free_sems.pop()
            else:
                st.dsem = [self.es.enter_context(self.nc.semaphore("dsem%d" % self.nsem)), 0, None]
                self.nsem += 1
            self.live_sems.append(st.dsem)
            if self.scope_stack:
                self.scope_stack[-1].append(st)
        oap, iap = out.ap, in_.ap

        def fn(e):
            return e.dma_start(out=oap, in_=iap, **kw)
        ins = self._rec(q, fn, [in_], [out], dma=True)
        sem = st.dsem
        if sem[2] is not None:
            ins.deps.append(sem[2])
        sem[1] += 1
        sem[2] = ins
        ins.sem = sem[0]
        ins.val = 16 * sem[1]
        return ins

    def barrier(self):
        lst = []
        for k in ("pe", "act", "dve", "pool"):
            for ins in reversed(self.q[k]):
                if not ins.dma:
                    lst.append(ins)
                    break
        for sem in self.live_sems:
            if sem[2] is not None:
                lst.append(sem[2])
        for i in lst:
            i.signal = True
        for k in QUEUES:
            self.bar[k] = list(lst)

    def scope(self):
        return Scope(self)

    def mm(self, out, lhsT, rhs, start=True, stop=True, **kw):
        oa, la, ra = out.ap, lhsT.ap, rhs.ap
        return self.op("pe", lambda e: e.matmul(oa, la, ra, start=start, stop=stop, **kw), [lhsT, rhs], [out])

    def tr(self, out, in_, ident):
        oa, ia, da = out.ap, in_.ap, ident.ap
        return self.op("pe", lambda e: e.transpose(oa, ia, da), [in_, ident], [out])

    def act(self, out, in_, func, bias=None, scale=None, accum=None, q="act"):
        oa, ia = out.ap, in_.ap
        kw = {}
        rd = [in_]
        if bias is not None:
            kw["bias"] = bias.ap if isinstance(bias, V) else bias
            if isinstance(bias, V):
                rd.append(bias)
        if scale is not None:
            kw["scale"] = scale.ap if isinstance(scale, V) else scale
            if isinstance(scale, V):
                rd.append(scale)
        wr = [out]
        if accum is not None:
            kw["accum_out"] = accum.ap
            wr.append(accum)
        return self.op("act", lambda e: e.activation(oa, ia, func, **kw), rd, wr)

    def tt(self, out, in0, in1, op, q="dve"):
        oa, a, b = out.ap, in0.ap, in1.ap
        return self.op(q, lambda e: e.tensor_tensor(oa, a, b, op), [in0, in1], [out])

    def ts(self, out, in0, s1, op0, s2=None, op1=None, q="dve", accum=None):
        oa, a = out.ap, in0.ap
        rd = [in0]
        x1 = s1.ap if isinstance(s1, V) else s1
        x2 = s2.ap if isinstance(s2, V) else s2
        if isinstance(s1, V):
            rd.append(s1)
        if isinstance(s2, V):
            rd.append(s2)
        kw = {}
        wr = [out]
        if op1 is not None:
            kw["op1"] = op1
        if accum is not None:
            kw["accum_out"] = accum.ap
            wr.append(accum)
        return self.op(q, lambda e: e.tensor_scalar(oa, a, x1, x2, op0, **kw), rd, wr)

    def stt(self, out, in0, scalar, in1, op0, op1, q="dve"):
        oa, a, b = out.ap, in0.ap, in1.ap
        rd = [in0, in1]
        s = scalar.ap if isinstance(scalar, V) else scalar
        if isinstance(scalar, V):
            rd.append(scalar)
        return self.op(q, lambda e: e.scalar_tensor_tensor(oa, a, s, b, op0, op1), rd, [out])

    def copy(self, out, in_, q="dve"):
        oa, ia = out.ap, in_.ap
        if q == "act":
            return self.op(q, lambda e: e.copy(oa, ia), [in_], [out])
        return self.op(q, lambda e: e.tensor_copy(oa, ia), [in_], [out])

    def red(self, out, in_, op, axis=None, q="dve"):
        oa, ia = out.ap, in_.ap
        ax = AX.X if axis is None else axis
        return self.op(q, lambda e: e.tensor_reduce(oa, ia, ax, op), [in_], [out])

    def scan(self, out, d0, d1, init, op0, op1):
        oa, a, b = out.ap, d0.ap, d1.ap
        rd = [d0, d1]
        i0 = init.ap if isinstance(init, V) else init
        if isinstance(init, V):
            rd.append(init)
        return self.op("dve", lambda e: e.tensor_tensor_scan(oa, a, b, i0, op0, op1), rd, [out])

    def memset(self, out, val, q="dve"):
        oa = out.ap
        return self.op(q, lambda e: e.memset(oa, val), [], [out])

    def recip(self, out, in_):
        oa, ia = out.ap, in_.ap
        return self.op("dve", lambda e: e.reciprocal(oa, ia), [in_], [out])

    def emit(self, final_tiles):
        nc = self.nc
        for k in ("pe", "act", "dve", "pool"):
            c = 0
            for ins in self.q[k]:
                if ins.dma:
                    continue
                ins.sem = self.esem[k]
                if ins.signal:
                    c += 1
                    ins.val = c
                else:
                    ins.val = None
        engs = {"pe": "tensor", "act": "scalar", "dve": "vector", "pool": "gpsimd", "sp": "sync"}
        finals = [(i.sem, i.val) for i in final_tiles]
        with nc.Block() as block:
            for k in QUEUES:
                lst = self.q[k]
                if not lst and k != "sp":
                    continue

                def body(e, lst=lst, k=k):
                    waited = {}
                    for ins in lst:
                        need = {}
                        for d in ins.deps:
                            sid = id(d.sem)
                            if d.val is None:
                                raise RuntimeError("dep on non-signalling ins")
                            if waited.get(sid, 0) >= d.val:
                                continue
                            if sid not in need or need[sid][1] < d.val:
                                need[sid] = (d.sem, d.val)
                        for sid, (s, v) in need.items():
                            e.wait_ge(s, v)
                            waited[sid] = v
                        r = ins.fn(e)
                        if ins.dma:
                            r.then_inc(ins.sem, 16)
                        elif ins.signal:
                            r.then_inc(ins.sem, 1)
                    if k == "sp":
                        for s, v in finals:
                            e.wait_ge(s, v)
                getattr(block, engs[k])(body)


D = 1024
KC = 8
DFF = 2816
FC = 22
LCTX = 256
EPS = 1e-6


class G:
    pass


def dview(v, pattern, **kw):
    return V(v.ap.rearrange(pattern, **kw), None)


def setup_consts(g):
    P = g.P
    g.ident = P.sb("ident", [128, 128])
    P.memset(g.ident, 1.0, q="pool")
    ia = g.ident.ap
    P.op("pool", lambda e: e.affine_select(ia, ia, [[1, 128]], ALU.is_equal, 0.0, base=0, channel_multiplier=-1),
         [g.ident], [g.ident])
    g.onesb = P.sb("onesb", [128, 128], BF16)
    P.memset(g.onesb, 1.0)
    g.pb = [P.ps("pb%d" % i, [128, 512]) for i in range(8)]
    g.pbi = 0


def nbank(g):
    b = g.pb[g.pbi % 8]
    g.pbi += 1
    return b


def rows_to_cols(g, S, dst, src_rows, R, name):
    P = g.P
    st = S.sb(name, [R, 128])
    P.dma(st, src_rows)
    b = nbank(g)
    P.tr(b[:, 0:R], st, g.ident[0:R, 0:R])
    P.copy(dst, b[:, 0:R])


def emit_mod(g):
    P = g.P
    g.modT = P.sb("modT", [128, 4, 72, 2])
    g.modp1 = P.sb("modp1", [128, 4, 72, 2])
    g.modh = P.sb("modh", [128, 4, 72, 2])
    with P.scope() as S:
        crow = S.sb("crow", [2, D])
        P.dma(crow, g.d["cvec"])
        crs = S.sb("crs", [2, D])
        P.act(crs, crow, AF.Silu)
        csT = S.sb("csT", [128, KC, 2])
        b = nbank(g)
        for kc in range(KC):
            P.tr(b[:, 2 * kc:2 * kc + 2], crs[0:2, kc * 128:(kc + 1) * 128], g.ident[0:2, 0:2])
        P.copy(csT, b[:, 0:16].re("p (k n) -> p k n", n=2))
        bmT = S.sb("bmT", [128, 288])
        bm_rows = dview(g.d["b_mod"], "l (j p) -> (l j) p", p=128)
        for r in range(3):
            rows_to_cols(g, S, bmT[:, r * 96:(r + 1) * 96], bm_rows[r * 96:(r + 1) * 96, :], 96, "bmst%d" % r)
        slabs = [S.sb("wmslab%d" % i, [128, KC, 512]) for i in range(3)]
        mrow = S.sb("mrow", [2, 9 * D])
        n = 0
        for l in range(g.depth):
            wv = dview(g.d["w_mod"][l], "(kc p) n -> p kc n", p=128)
            for s_ in range(18):
                sl = slabs[n % 3]
                n += 1
                P.dma(sl, wv[:, :, s_ * 512:(s_ + 1) * 512])
                b = nbank(g)
                for kc in range(KC):
                    P.mm(b[0:2, :], csT[:, kc, :], sl[:, kc, :], start=(kc == 0), stop=(kc == KC - 1))
                P.copy(mrow[:, s_ * 512:(s_ + 1) * 512], b[0:2, :], q=("act" if s_ % 2 else "dve"))
            b = nbank(g)
            for jj in range(72):
                P.tr(b[:, 2 * jj:2 * jj + 2], mrow[0:2, jj * 128:(jj + 1) * 128], g.ident[0:2, 0:2])
            P.tt(g.modT[:, l, :, :], b[:, 0:144].re("p (j n) -> p j n", n=2),
                 bmT[:, l * 72:(l + 1) * 72].re("p (j o) -> p j o", o=1).bc([128, 72, 2]), ALU.add)
        dd = g.depth
        P.ts(g.modp1[:, 0:dd], g.modT[:, 0:dd], 1.0, ALU.add)
        P.ts(g.modh[:, 0:dd], g.modT[:, 0:dd], 0.5, ALU.mult)


def mcol(arr, l, i, c, n):
    return arr[:, l, i * 8 + c, n:n + 1]


def emit_in_transpose(g):
    P = g.P
    xTv = dview(g.d["xT"], "c p t -> p c t")
    with P.scope() as S:
        xin = [S.sb("xin%d" % i, [128, 4, D]) for i in range(2)]
        xtl = [S.sb("xtl%d" % i, [128, KC, 512]) for i in range(2)]
        groups = [("ctx", 0, 2, 0)] + [("x", t * 512, 4, LCTX + t * 512) for t in range(g.nlat // 512)]
        for gi, (nm, r0, nb, t0) in enumerate(groups):
            xi = xin[gi % 2]
            xt = xtl[gi % 2]
            src = dview(g.d[nm][r0:r0 + nb * 128, :], "(b p) f -> p b f", p=128)
            P.dma(xi[:, 0:nb, :], src)
            for c in range(KC):
                b = nbank(g)
                for bl in range(nb):
                    P.tr(b[:, bl * 128:(bl + 1) * 128], xi[:, bl, c * 128:(c + 1) * 128], g.ident)
                P.copy(xt[:, c, 0:nb * 128], b[:, 0:nb * 128], q=("act" if c % 2 else "dve"))
                g.bg.tick(2)
            P.dma(xTv[:, :, t0:t0 + nb * 128], xt[:, :, 0:nb * 128])


def norm_stats(g, sq, rstd, xt, ncols):
    P = g.P
    b = nbank(g)
    for c in range(KC):
        s_ = sq[c % 2]
        P.act(s_[:, 0:ncols], xt[:, c, 0:ncols], AF.Square)
        P.mm(b[:, 0:ncols], g.onesb, s_[:, 0:ncols], start=(c == 0), stop=(c == KC - 1))
    P.act(rstd[:, 0:ncols], b[:, 0:ncols], AF.Sqrt, scale=1.0 / D, bias=g.epsc[:, 0:1])
    P.recip(rstd[:, 0:ncols], rstd[:, 0:ncols])


def norm_apply(g, tmp, rstd, xt, hT, ncols, l, i_shift, n):
    P = g.P
    for c in range(KC):
        t_ = tmp[c % 2]
        P.stt(t_[:, 0:ncols], xt[:, c, 0:ncols], mcol(g.modp1, l, i_shift + 1, c, n), rstd[:, 0:ncols], ALU.mult, ALU.mult)
        P.act(hT[:, c, 0:ncols], t_[:, 0:ncols], AF.Identity, bias=mcol(g.modT, l, i_shift, c, n))


def emit_norm_mod(g, S, bufs, xt, hT, ncols, l, i_shift, n):
    sq, rstd, tmp = bufs
    norm_stats(g, sq, rstd, xt, ncols)
    norm_apply(g, tmp, rstd, xt, hT, ncols, l, i_shift, n)


class BgPrep:
    def __init__(self, g):
        self.g = g
        P = g.P
        self.st = [P.sb("bg_st%d" % i, [128, 2048]) for i in range(2)]
        self.ob = [P.sb("bg_ob%d" % i, [128, 2048], BF16) for i in range(2)]
        self.jobs = []
        for l in range(g.depth):
            for which in range(2):
                k = l * 2 + which
                for jp in range(FC // 2):
                    for kh in range(2):
                        self.jobs.append((k, "in", l, which, jp, kh))
                for fp in range(FC // 2):
                    self.jobs.append((k, "out", l, which, fp, 0))
        self.pos = 0
        self.pending = None
        self.calls = 0
        self.limit_k = 0

    def _emit_store(self):
        if self.pending is not None:
            dst, src = self.pending
            self.g.P.dma(dst, src)
            self.pending = None

    def step(self):
        if self.pos >= len(self.jobs):
            self._emit_store()
            return False
        g = self.g
        P = g.P
        k, kind, l, which, a, kh = self.jobs[self.pos]
        st = self.st[self.pos % 2]
        ob = self.ob[self.pos % 2]
        self.pos += 1
        if kind == "in":
            win = dview(g.d["ffn_in"][l, which], "(kc p) n -> p kc n", p=128)
            sv = st.re("p (k n) -> p k n", n=512)
            P.dma(sv[:, :, 0:256], win[:, kh * 4:kh * 4 + 4, a * 256:(a + 1) * 256])
            P.dma(sv[:, :, 256:512], win[:, kh * 4:kh * 4 + 4, DFF + a * 256:DFF + (a + 1) * 256])
            dst = V(g.d["w_in_b"].ap[k, a, :, kh * 2048:(kh + 1) * 2048], None)
        else:
            wout = dview(g.d["ffn_out"][l, which], "(fc p) n -> p fc n", p=128)
            P.dma(st.re("p (f n) -> p f n", n=1024), wout[:, a * 2:a * 2 + 2, :])
            dst = V(g.d["w_out_b"].ap[k, :, a * 2048:(a + 1) * 2048], None)
        self._emit_store()
        P.copy(ob, st, q="pool")
        self.pending = (dst, ob)
        return True

    def tick(self, every):
        self.calls += 1
        if self.calls % every == 0 and self.pos < len(self.jobs) and self.jobs[self.pos][0] <= self.limit_k:
            self.step()

    def finish(self, k):
        n = 0
        while self.pos < len(self.jobs) and self.jobs[self.pos][0] <= k:
            self.step()
            n += 1
        if self.pending is not None:
            self._emit_store()
            n += 1
        if n:
            self.g.P.barrier()


def emit_ffn(g, l, which):
    P = g.P
    i0 = 0 if which == 0 else 6
    k = l * 2 + which
    xTd = g.d["xT"]
    g.bg.finish(k)
    tiles = []
    if not (l == g.depth - 1 and which == 1):
        tiles.append((0, LCTX, 1))
    NT = 1024
    for t in range(g.nlat // NT):
        tiles.append((LCTX + t * NT, NT, 0))
    with P.scope() as S:
        xt = S.sb("f_xt", [128, KC, 512])
        hT = S.sb("f_hT", [128, KC, NT], BF16)
        actT = S.sb("f_actT", [128, FC, NT], BF16)
        sq = [S.sb("f_sq%d" % i, [128, 512], BF16) for i in range(2)]
        rstd = S.sb("f_rstd", [128, 512])
        tmp = [S.sb("f_tmp%d" % i, [128, 512]) for i in range(2)]
        wb = [S.sb("f_wb%d" % i, [128, KC, 512], BF16) for i in range(3)]
        wo = S.sb("f_wo", [128, FC, D], BF16)
        sg = [S.sb("f_sg%d" % i, [128, 512], BF16) for i in range(2)]
        xc = [S.sb("f_xc%d" % i, [128, NT]) for i in range(2)]
        for q4 in range(2):
            f0, f1 = q4 * 11, (q4 + 1) * 11
            P.dma(wo[:, f0:f1, :], V(g.d["w_out_b"].ap[k, :, f0 * D:f1 * D].rearrange("p (f n) -> p f n", n=D), None))
        rstdh = [rstd, S.sb("f_rstd2", [128, 512])]
        nw = 0
        nx = 0
        nsg = 0

        def load_x(tile, h):
            t0, nt, n = tile
            hw = min(512, nt)
            P.dma(xt[:, :, 0:hw], dview(xTd[:, :, t0 + h * hw:t0 + (h + 1) * hw], "c p t -> p c t"))

        def a1(tile, h):
            load_x(tile, h)
            norm_stats(g, sq, rstdh[h], xt, min(512, tile[1]))

        def a2(tile, h):
            t0, nt, n = tile
            hw = min(512, nt)
            load_x(tile, h)
            norm_apply(g, tmp, rstdh[h], xt, hT[:, :, h * hw:(h + 1) * hw], hw, l, i0, n)

        for h in range(tiles[0][1] // min(512, tiles[0][1])):
            a1(tiles[0], h)
        for h in range(tiles[0][1] // min(512, tiles[0][1])):
            a2(tiles[0], h)
        for ti, (t0, nt, n) in enumerate(tiles):
            hw = min(512, nt)
            nh = nt // hw
            nxt = tiles[ti + 1] if ti + 1 < len(tiles) else None
            nnh = (nxt[1] // min(512, nxt[1])) if nxt else 0
            for jp in range(FC // 2):
                w_ = wb[nw % 3]
                nw += 1
                P.dma(w_, V(g.d["w_in_b"].ap[k, jp].rearrange("p (k n) -> p k n", n=512), None))
                if nxt is not None and jp == 6:
                    a1(nxt, 0)
                if nxt is not None and jp == 8 and nnh > 1:
                    a1(nxt, 1)
                for jj in range(2):
                    j = jp * 2 + jj
                    for h in range(nh):
                        bg_ = nbank(g)
                        bu = nbank(g)
                        for kc in range(KC):
                            P.mm(bg_[:, 0:hw], w_[:, kc, jj * 128:(jj + 1) * 128], hT[:, kc, h * hw:(h + 1) * hw],
                                 start=(kc == 0), stop=(kc == KC - 1))
                        for kc in range(KC):
                            P.mm(bu[:, 0:hw], w_[:, kc, 256 + jj * 128:256 + (jj + 1) * 128], hT[:, kc, h * hw:(h + 1) * hw],
                                 start=(kc == 0), stop=(kc == KC - 1))
                        s_ = sg[nsg % 2]
                        nsg += 1
                        P.act(s_[:, 0:hw], bg_[:, 0:hw], AF.Silu)
                        P.tt(actT[:, j, h * hw:(h + 1) * hw], s_[:, 0:hw], bu[:, 0:hw], ALU.mult)
            for h in range(nnh):
                a2(nxt, h)
            for c in range(KC):
                x_ = xc[nx % 2]
                nx += 1
                P.dma(x_[:, 0:nt], V(xTd.ap[c, :, t0:t0 + nt], None))
                for h in range(nh):
                    b = nbank(g)
                    for f in range(FC):
                        P.mm(b[:, 0:hw], wo[:, f, c * 128:(c + 1) * 128], actT[:, f, h * hw:(h + 1) * hw],
                             start=(f == 0), stop=(f == FC - 1))
                    P.stt(x_[:, h * hw:(h + 1) * hw], b[:, 0:hw], mcol(g.modh, l, i0 + 2, c, n), x_[:, h * hw:(h + 1) * hw],
                          ALU.mult, ALU.add)
                P.dma(V(xTd.ap[c, :, t0:t0 + nt], None), x_[:, 0:nt])


def emit_final(g):
    P = g.P
    xTd = g.d["xT"]
    finals = []
    with P.scope() as S:
        fg = S.sb("fgT", [128, KC])
        rows_to_cols(g, S, fg, dview(g.d["final_gain"], "(c p) -> c p", p=128), KC, "fgst")
        xt = [S.sb("o_xt%d" % i, [128, KC, 512]) for i in range(2)]
        sq = [S.sb("o_sq%d" % i, [128, 512], BF16) for i in range(2)]
        rstd = S.sb("o_rstd", [128, 512])
        yt = S.sb("o_yt", [128, KC, 512])
        ot = [S.sb("o_ot%d" % i, [128, 4, D]) for i in range(2)]
        for t in range(g.nlat // 512):
            t0 = LCTX + t * 512
            x_ = xt[t % 2]
            P.dma(x_, dview(xTd[:, :, t0:t0 + 512], "c p t -> p c t"))
            b = nbank(g)
            for c in range(KC):
                s_ = sq[c % 2]
                P.act(s_, x_[:, c, :], AF.Square)
                P.mm(b, g.onesb, s_, start=(c == 0), stop=(c == KC - 1))
            P.act(rstd, b, AF.Sqrt, scale=1.0 / D, bias=g.epsc[:, 0:1])
            P.recip(rstd, rstd)
            for c in range(KC):
                P.stt(yt[:, c, :], x_[:, c, :], fg[:, c:c + 1], rstd, ALU.mult, ALU.mult)
            o_ = ot[t % 2]
            for bl in range(4):
                for c0 in range(0, KC, 4):
                    b = nbank(g)
                    for c in range(c0, c0 + 4):
                        P.tr(b[:, (c - c0) * 128:(c - c0 + 1) * 128], yt[:, c, bl * 128:(bl + 1) * 128], g.ident)
                    P.copy(o_[:, bl, c0 * 128:(c0 + 4) * 128], b, q=("act" if (c0 // 4) % 2 else "dve"))
            finals.append(P.dma(dview(g.d["out"][t * 512:(t + 1) * 512, :], "(b p) f -> p b f", p=128), o_))
    return finals


HD = 64
EPI_OFF = [3, 5]


def rot(g, key, lo, hi):
    c = g.rotc.get(key, 0)
    g.rotc[key] = c + 1
    return g.pb[lo + c % (hi - lo)]


def load_cast(g, S, st_pool, cnt, dst, src, cols, q="pool"):
    P = g.P
    st = st_pool[cnt[0] % len(st_pool)]
    cnt[0] += 1
    P.dma(st[:, :, 0:cols], src)
    P.copy(dst, st[:, :, 0:cols], q=q)
    return st


def emit_attn_proj(g, l):
    P = g.P
    even = (l % 2 == 0)
    e = l // 2
    xTd = g.d["xT"]
    if even:
        nqc, nkv = 4, 2
        wname, wsname = "even_w_in", "even_qk_sw"
        qcols, kcols, vcol0 = 512, 128, 640
    else:
        nqc, nkv = 8, 4
        wname, wsname = "odd_w_in", "odd_qk_sw"
        qcols, kcols, vcol0 = 1024, 256, 1280
    vcols = kcols
    win = dview(g.d[wname][e], "(kc p) n -> p kc n", p=128)
    wsw = dview(g.d[wsname][e], "(kc p) n -> p kc n", p=128)
    with P.scope() as S:
        wq = S.sb("wq", [128, KC, qcols], BF16)
        wqs = S.sb("wqs", [128, KC, qcols], BF16)
        wk2 = S.sb("wk2", [128, KC, nkv, 128], BF16)
        wks2 = S.sb("wks2", [128, KC, nkv, 128], BF16)
        wv = S.sb("wv", [128, KC, vcols], BF16)
        stp = [S.sb("pst%d" % i, [128, KC, 512]) for i in range(2)]
        cnt = [0]
        for s_ in range(qcols // 512):
            load_cast(g, S, stp, cnt, wq[:, :, s_ * 512:(s_ + 1) * 512], win[:, :, s_ * 512:(s_ + 1) * 512], 512)
            load_cast(g, S, stp, cnt, wqs[:, :, s_ * 512:(s_ + 1) * 512], wsw[:, :, s_ * 512:(s_ + 1) * 512], 512)
        for (dst2, srcw) in ((wk2, win), (wks2, wsw)):
            st = stp[cnt[0] % 2]
            cnt[0] += 1
            P.dma(st[:, :, 0:kcols], srcw[:, :, qcols:qcols + kcols])
            sv = st[:, :, 0:kcols].re("p k (g d) -> p k g d", d=64)
            P.copy(dst2[:, :, :, 0:64], sv, q="pool")
            P.copy(dst2[:, :, :, 64:128], sv, q="pool")
        load_cast(g, S, stp, cnt, wv, win[:, :, vcol0:vcol0 + vcols], vcols)
        if even:
            qg = S.sb("qg", [128, 4])
            for j, nm in enumerate(["q_gain", "q_gain_sw", "k_gain", "k_gain_sw"]):
                for hh in range(2):
                    P.dma(qg[hh * 64:(hh + 1) * 64, j:j + 1], dview(g.d[nm][e], "(d o) -> d o", o=1))
            blk = S.sb("blk", [128, 128], BF16)
            P.memset(blk, 0.0)
            P.memset(blk[0:64, 0:64], 1.0)
            P.memset(blk[64:128, 64:128], 1.0)
            sqb = [S.sb("sqb%d" % i, [128, 512], BF16) for i in range(2)]
            rs = [S.sb("rs%d" % i, [128, 512]) for i in range(2)]
        xt = S.sb("p_xt", [128, KC, 512])
        hT = S.sb("p_hT", [128, KC, 512], BF16)
        sq = [S.sb("p_sq%d" % i, [128, 512], BF16) for i in range(2)]
        rstd = S.sb("p_rstd", [128, 512])
        tmp = [S.sb("p_tmp%d" % i, [128, 512]) for i in range(2)]
        cs = S.sb("p_cos", [128, 512])
        sn = S.sb("p_sin", [128, 512])
        t1 = [S.sb("p_t1%d" % i, [128, 512]) for i in range(2)]
        t2 = [S.sb("p_t2%d" % i, [128, 512]) for i in range(2)]
        qTt = S.sb("p_qTt", [128, nqc, 512], BF16)
        kTt = S.sb("p_kTt", [128, nkv, 512], BF16)
        vt = S.sb("p_vt", [128, 4, nkv, 192], BF16)
        P.memset(vt, 1.0)
        if even:
            g.wrw = S.sb("wrw", [128, KC, 1920], BF16)
            wrw_d = dview(g.d["even_w_in"][e], "(kc p) n -> p kc n", p=128)
            for s_ in range(4):
                c0 = 768 + s_ * 512
                cw = min(512, 2688 - c0)
                load_cast(g, S, stp, cnt, g.wrw[:, :, s_ * 512:s_ * 512 + cw], wrw_d[:, :, c0:c0 + cw], cw)
            g.rwfo = [S.sb("rwfo%d" % i, [128, 512]) for i in range(2)]
        tiles = [(0, LCTX, 1, None)] + [(LCTX + t * 512, 512, 0, t * 512) for t in range(g.nlat // 512)]
        nr = 0
        for (t0, nt, n, a0) in tiles:
            P.dma(xt[:, :, 0:nt], dview(xTd[:, :, t0:t0 + nt], "c p t -> p c t"))
            emit_norm_mod(g, S, (sq, rstd, tmp), xt, hT, nt, l, 3, n)
            lat = a0 is not None
            if lat:
                P.dma(cs, g.d["cosT"][:, a0:a0 + 512])
                P.dma(sn, g.d["sinT"][:, a0:a0 + 512])
            jobs = [(wq[:, :, c * 128:(c + 1) * 128], wqs[:, :, c * 128:(c + 1) * 128], qTt[:, c, 0:nt], 0) for c in range(nqc)]
            jobs += [(wk2[:, :, c, :], wks2[:, :, c, :], kTt[:, c, 0:nt], 2) for c in range(nkv)]
            for (wa, wb_, dst, gi) in jobs:
                ba = rot(g, "pj", 0, 8)
                for kc in range(KC):
                    P.mm(ba[:, 0:nt], wa[:, kc, :], hT[:, kc, 0:nt], start=(kc == 0), stop=(kc == KC - 1))
                if lat:
                    bb = rot(g, "pj", 0, 8)
                    for kc in range(KC):
                        P.mm(bb[:, 0:nt], wb_[:, kc, :], hT[:, kc, 0:nt], start=(kc == 0), stop=(kc == KC - 1))
                if even:
                    s_ = sqb[nr % 2]
                    r_ = rs[nr % 2]
                    P.act(s_[:, 0:nt], ba[:, 0:nt], AF.Square)
                    bn = rot(g, "pj", 0, 8)
                    P.mm(bn[:, 0:nt], blk, s_[:, 0:nt])
                    P.act(r_[:, 0:nt], bn[:, 0:nt], AF.Sqrt, scale=1.0 / HD, bias=g.epsc[:, 0:1])
                    P.recip(r_[:, 0:nt], r_[:, 0:nt])
                a_ = t1[nr % 2]
                b_ = t2[nr % 2]
                nr += 1
                if even:
                    if lat:
                        P.stt(a_[:, 0:nt], ba[:, 0:nt], qg[:, gi:gi + 1], r_[:, 0:nt], ALU.mult, ALU.mult)
                        P.stt(b_[:, 0:nt], bb[:, 0:nt], qg[:, gi + 1:gi + 2], r_[:, 0:nt], ALU.mult, ALU.mult)
                        P.tt(a_[:, 0:nt], a_[:, 0:nt], cs[:, 0:nt], ALU.mult, q="pool")
                        P.tt(b_[:, 0:nt], b_[:, 0:nt], sn[:, 0:nt], ALU.mult, q="pool")
                        P.tt(dst, a_[:, 0:nt], b_[:, 0:nt], ALU.add, q="pool")
                    else:
                        P.stt(dst, ba[:, 0:nt], qg[:, gi:gi + 1], r_[:, 0:nt], ALU.mult, ALU.mult)
                else:
                    if lat:
                        P.tt(a_[:, 0:nt], ba[:, 0:nt], cs[:, 0:nt], ALU.mult)
                        P.tt(b_[:, 0:nt], bb[:, 0:nt], sn[:, 0:nt], ALU.mult)
                        P.tt(dst, a_[:, 0:nt], b_[:, 0:nt], ALU.add, q="pool")
                    else:
                        P.copy(dst, ba[:, 0:nt], q="act")
            for tb in range(nt // 128):
                bv = rot(g, "pj", 0, 8)
                for kc in range(KC):
                    P.mm(bv[:, 0:vcols], hT[:, kc, tb * 128:(tb + 1) * 128], wv[:, kc, :], start=(kc == 0), stop=(kc == KC - 1))
                P.copy(vt[:, tb, :, 64:128], bv[:, 0:vcols].re("p (g d) -> p g d", d=64), q="act")
            P.dma(dview(g.d["qT"][0:nqc, :, t0:t0 + nt], "c p t -> p c t"), qTt[:, :, 0:nt])
            P.dma(dview(g.d["kT2"][0:nkv, :, t0:t0 + nt], "c p t -> p c t"), kTt[:, :, 0:nt])
            P.dma(dview(g.d["Vd"][t0 // 128:(t0 + nt) // 128, :, 0:nkv * 192], "b p f -> p b f"),
                  vt[:, 0:nt // 128].re("p b g f -> p b (g f)"))
            if even:
                emit_rwkv_proj(g, S, l, hT, t0, nt)


def emit_rwkv_proj(g, S, l, hT, t0, nt):
    P = g.P
    for c in range(15):
        b = rot(g, "pj", 0, 8)
        for kc in range(KC):
            P.mm(b[:, 0:nt], g.wrw[:, kc, c * 128:(c + 1) * 128], hT[:, kc, 0:nt], start=(kc == 0), stop=(kc == KC - 1))
        fo = g.rwfo[c % 2]
        P.copy(fo[:, 0:nt], b[:, 0:nt], q=("act" if c % 2 else "dve"))
        P.dma(V(g.d["fT"].ap[c, :, t0:t0 + nt], None), fo[:, 0:nt])


def emit_attn(g, l):
    P = g.P
    even = (l % 2 == 0)
    o = l // 2
    ctx_out = l < g.depth - 1
    if even:
        nqc, nkv = 4, 2
    else:
        nqc, nkv = 8, 4
    NB = g.ntok // 128
    with P.scope() as S:
        kT2 = S.sb("a_kT2", [128, nkv, g.ntok], BF16)
        Va = S.sb("a_V", [128, NB, nkv, 192], BF16)
        for c in range(nkv):
            P.dma(kT2[:, c, :], V(g.d["kT2"].ap[c], None))
        P.dma(Va.re("p b g f -> p b (g f)"), dview(g.d["Vd"][:, :, 0:nkv * 192], "b p f -> p b f"))
        swapM = S.sb("a_swap", [128, 128])
        P.copy(swapM[:, 0:64], g.ident[:, 64:128])
        P.copy(swapM[:, 64:128], g.ident[:, 0:64])
        if not even:
            masks = S.sb("a_masks", [128, 6, 512], BF16)
            P.memset(masks, 1.0, q="pool")
            for r in range(-1, 5):
                ma = masks[:, r + 1, :].ap
                P.op("pool", lambda e, ma=ma, r=r: e.affine_select(ma, ma, [[1, 512]], ALU.is_ge, 0.0,
                                                                   base=-r * 128 + 128, channel_multiplier=-1), [masks], [masks])
                P.op("pool", lambda e, ma=ma, r=r: e.affine_select(ma, ma, [[-1, 512]], ALU.is_ge, 0.0,
                                                                   base=r * 128 + 128, channel_multiplier=1), [masks], [masks])
            eS = S.sb("a_sk", [128, 16])
            P.dma(eS, V(g.d["sink"].ap[o].rearrange("(o h) -> o h", o=1).broadcast_to([128, 16]), None))
            P.act(eS, eS, AF.Exp)
            padi = S.sb("a_padi", [128, 128], mybir.dt.int32)
            pia = padi.ap
            P.op("pool", lambda e: e.iota(pia, [[1, 128]], base=1, channel_multiplier=0), [], [padi])
            padcnt = S.sb("a_padcnt", [128, 128])
            P.copy(padcnt, padi)
        qz = [[S.sb("a_qz%d%d" % (hh, i), [128, nqc, 512], BF16) for i in range(2)] for hh in range(2)]
        for hh in range(2):
            for i in range(2):
                P.memset(qz[hh][i][(1 - hh) * 64:(2 - hh) * 64], 0.0)
        oT = [S.sb("a_oT%d" % i, [128, nqc, 512], BF16) for i in range(2)]
        pT = [S.sb("a_pT%d" % i, [128, 512], BF16) for i in range(8)]
        den = [S.sb("a_den%d" % i, [128, 512]) for i in range(3)]
        rshb = [S.sb("a_rsh%d" % i, [128, 512]) for i in range(3)]
        for i in range(3):
            P.memset(den[i], 1.0)
        tiles = []
        if ctx_out:
            tiles.append((0, LCTX, None))
        tiles += [(LCTX + t * 512, 512, t) for t in range(g.nlat // 512)]
        npT = 0
        nep = 0
        def load_q(ti_):
            t0_, nt_, _ = tiles[ti_]
            for hh_ in range(2):
                P.dma(qz[hh_][ti_ % 2][hh_ * 64:(hh_ + 1) * 64, :, 0:nt_],
                      dview(g.d["qT"][0:nqc, hh_ * 64:(hh_ + 1) * 64, t0_:t0_ + nt_], "c p t -> p c t"))
        load_q(0)
        deferred = []
        for ti, (t0, nt, tq) in enumerate(tiles):
            o_ = oT[ti % 2]
            qq = [qz[0][ti % 2], qz[1][ti % 2]]
            chunks = [(0, 0, nt, None), (1, 0, nt, None)]
            if tq is not None:
                if even:
                    chunks += [(2 + kb, 0, nt, None) for kb in range(g.nlat // 128)]
                else:
                    for r in range(-1, 5):
                        kb = tq * 4 + r
                        if 1 <= kb < g.nlat // 128:
                            chunks.append((2 + kb, max(0, r * 128 - 128), min(512, r * 128 + 256), r + 1))
            items = [(hc, ci, hh) for hc in range(nqc) for ci in range(len(chunks)) for hh in range(2)]
            LA = 4
            pbuf = {}

            def stage_a(idx):
                nonlocal npT
                hc, ci, hh = items[idx]
                kb, c0, c1, mi = chunks[ci]
                gk = hc // 2
                bS = rot(g, "at", 4, 7)
                P.mm(bS[:, c0:c1], kT2[:, gk, kb * 128:(kb + 1) * 128], qq[hh][:, hc, c0:c1])
                p_ = pT[npT % 8]
                npT += 1
                P.act(p_[:, c0:c1], bS[:, c0:c1], AF.Exp, scale=0.125)
                if mi is not None:
                    P.tt(p_[:, c0:c1], p_[:, c0:c1], masks[:, mi, c0:c1], ALU.mult)
                pbuf[idx] = p_

            def stage_b(idx):
                nonlocal nep
                hc, ci, hh = items[idx]
                kb, c0, c1, mi = chunks[ci]
                gk = hc // 2
                bo = g.pb[(hc % 2) * 2 + hh]
                p_ = pbuf.pop(idx)
                vsl = Va[:, kb, gk, 64:192] if hh == 0 else Va[:, kb, gk, 0:128]
                P.mm(bo[:, c0:c1], vsl, p_[:, c0:c1], start=(ci == 0), stop=(ci == len(chunks) - 1))
                if ci == len(chunks) - 1:
                    sr, orow = (1 - hh) * 64, hh * 64
                    h = 2 * hc + hh
                    d_ = den[nep % 3]
                    r_ = rshb[nep % 3]
                    nep += 1
                    if even:
                        P.recip(d_[sr:sr + 64, 0:nt], bo[sr:sr + 64, 0:nt])
                    else:
                        P.ts(d_[sr:sr + 64, 0:nt], bo[sr:sr + 64, 0:nt], eS[sr:sr + 64, h:h + 1], ALU.add)
                        if tq == g.nlat // 512 - 1:
                            P.tt(d_[sr:sr + 64, 384:512], d_[sr:sr + 64, 384:512], padcnt[sr:sr + 64, :], ALU.add)
                        P.recip(d_[sr:sr + 64, 0:nt], d_[sr:sr + 64, 0:nt])
                    st_ = {}

                    def e2a(d_=d_, st_=st_, nt=nt):
                        st_["b"] = g.pb[7]
                        P.mm(st_["b"][:, 0:nt], swapM, d_[:, 0:nt])

                    def e2b(r_=r_, st_=st_, orow=orow, nt=nt):
                        P.copy(r_[orow:orow + 64, 0:nt], st_["b"][orow:orow + 64, 0:nt], q="act")

                    def e3(r_=r_, bo=bo, o_=o_, orow=orow, hc=hc, nt=nt):
                        P.tt(o_[orow:orow + 64, hc, 0:nt], bo[orow:orow + 64, 0:nt], r_[orow:orow + 64, 0:nt], ALU.mult)
                    off = EPI_OFF[hh] if len(chunks) >= 6 else 0
                    deferred.append([off, e2a])
                    deferred.append([off + 2, e2b])
                    deferred.append([off + 6, e3])

            def run_deferred(flush=False):
                keep = []
                for it in deferred:
                    it[0] -= 1
                    if it[0] <= 0 or flush:
                        it[1]()
                    else:
                        keep.append(it)
                deferred[:] = keep

            for idx in range(len(items) + LA):
                g.bg.tick(8)
                if idx < len(items):
                    stage_a(idx)
                if idx >= LA:
                    stage_b(idx - LA)
                run_deferred()
                if idx == len(items) // 2 and ti + 1 < len(tiles):
                    load_q(ti + 1)
            while deferred:
                run_deferred(flush=True)
            P.dma(dview(g.d["oT"][0:nqc, :, t0:t0 + nt], "c p t -> p c t"), o_[:, :, 0:nt])


GN_EPS = 64e-5
CW = 128


def emit_rwkv(g, l):
    P = g.P
    e = l // 2
    NT = g.ntok
    NCH = NT // CW
    RW = g.rwdt
    d = g.d
    with P.scope() as S0:
        gC = S0.sb("rw_gC", [128, 2, 4, NCH])
        rwkv_features(g, l, S0, gC)
        with P.scope() as S:
            rwkv_scan(g, l, S, gC)
        with P.scope() as S:
            rwkv_readout(g, l, S)


def rwkv_features(g, l, S0, gC):
    P = g.P
    e = l // 2
    NT = g.ntok
    RW = g.rwdt
    d = g.d
    with P.scope() as S:
        NR = 30 + 8 + 8 + 4 + 4 + 4
        prow = S.sb("rw_prow", [NR, 128])
        P.dma(prow[0:30, :], dview(d["rwkv_mu"][e], "d (c p) -> (d c) p", p=128))
        P.dma(prow[30:38, :], dview(d["rwkv_w0"][e], "d (c p) -> (d c) p", p=128))
        P.dma(prow[38:46, :], dview(d["rwkv_a0"][e], "d (c p) -> (d c) p", p=128))
        P.dma(prow[46:50, :], dview(d["rwkv_k_k"][e], "(c p) -> c p", p=128))
        P.dma(prow[50:54, :], dview(d["rwkv_k_a"][e], "(c p) -> c p", p=128))
        P.dma(prow[54:58, :], V(d["rwkv_r_k"].ap[e].rearrange("h dk -> (h dk)").rearrange("(c p) -> c p", p=128), None))
        prm = S.sb("rw_prm", [128, NR])
        b = rot(g, "pj", 0, 8)
        P.tr(b[:, 0:NR], prow, g.ident[0:NR, 0:NR])
        P.copy(prm, b[:, 0:NR])
        mu0 = lambda c: prm[:, c:c + 1]
        mu1 = lambda c: prm[:, 15 + c:16 + c]
        w0c = lambda dd, j: prm[:, 30 + dd * 4 + j:31 + dd * 4 + j]
        a0c = lambda dd, j: prm[:, 38 + dd * 4 + j:39 + dd * 4 + j]
        kkc = lambda j: prm[:, 46 + j:47 + j]
        kac = lambda j: prm[:, 50 + j:51 + j]
        rkc = lambda j: prm[:, 54 + j:55 + j]
        mc = S.sb("rw_mc", [128, 15])
        P.tt(mc, prm[:, 0:15], prm[:, 15:30], ALU.add)
        P.ts(mc, mc, -1.0, ALU.mult, 1.0, ALU.add)
        omka = S.sb("rw_omka", [128, 4])
        P.ts(omka, prm[:, 50:54], -1.0, ALU.mult, 1.0, ALU.add)
        w2T = S.sb("rw_w2T", [128, 512])
        a2T = S.sb("rw_a2T", [128, 512])
        g2s = S.sb("rw_g2s", [128, 512])
        P.dma(w2T, dview(d["rwkv_w2"][e], "d r c -> (d r) c"))
        P.dma(a2T, dview(d["rwkv_a2"][e], "d r c -> (d r) c"))
        P.dma(g2s, d["rwkv_g2"][e])
        blkf = S.sb("rw_blkf", [128, 128])
        P.memset(blkf, 0.0)
        P.memset(blkf[0:64, 0:64], 1.0)
        P.memset(blkf[64:128, 64:128], 1.0)
        rst = S.sb("rw_rst", [128, 512])
        P.memset(rst, 1.0)
        P.memset(rst.re("p (c i) -> p c i", i=CW)[:, :, 0:1], 0.0)
        zer = S.sb("rw_zer", [128, 512])
        P.memset(zer, 0.0)
        fb = S.sb("rw_fb", [128, 15, 514])
        fs = S.sb("rw_fs", [128, 15, 512])
        nT = [0]
        tpool = [S.sb("rw_t%d" % i, [128, 512]) for i in range(14)]

        def tmp():
            t = tpool[nT[0] % len(tpool)]
            nT[0] += 1
            return t
        obuf = [S.sb("rw_ob%d" % i, [128, 512], RW) for i in range(4)]
        nO = [0]

        def ob():
            t = obuf[nO[0] % len(obuf)]
            nO[0] += 1
            return t
        g.rw_vt = S.sb("rw_vt", [128, 4, 512], RW)
        g.rw_kt = [S.sb("rw_kt%d" % i, [128, 4, 512], RW) for i in range(2)]
        g.rw_bt = [S.sb("rw_bt%d" % i, [128, 4, 512], RW) for i in range(2)]
        twl = S.sb("rw_twl", [128, 512])
        sgl = S.sb("rw_sgl", [128, 512])
        kk = S.sb("rw_kk", [128, 512])
        gob = [S.sb("rw_go%d" % i, [128, 512], BF16) for i in range(2)]
        identR = g.ident
        if RW != F32:
            identR = S.sb("rw_identb", [128, 128], RW)
            P.copy(identR, g.ident)
        tiles = [(0, LCTX, 0, LCTX)] + [(LCTX + t * 512, 512, LCTX, NT) for t in range(g.nlat // 512)]
        for (t0, nt, s0, s1) in tiles:
            nch = nt // CW
            ch0 = t0 // CW
            lo = 1 if t0 == s0 else 0
            hi = 1 if t0 + nt == s1 else 0
            if lo:
                P.memset(fb[:, :, 0:1], 0.0)
            if hi:
                P.memset(fb[:, :, nt + 1:nt + 2], 0.0)
            P.dma(fb[:, :, lo:nt + 2 - hi], dview(d["fT"][:, :, t0 - 1 + lo:t0 + nt + 1 - hi], "c p t -> p c t"))
            for c in range(15):
                t_ = tmp()
                P.ts(t_[:, 0:nt], fb[:, c, 1:nt + 1], mc[:, c:c + 1], ALU.mult)
                P.stt(t_[:, 0:nt], fb[:, c, 0:nt], mu0(c), t_[:, 0:nt], ALU.mult, ALU.add)
                P.stt(fs[:, c, 0:nt], fb[:, c, 2:nt + 2], mu1(c), t_[:, 0:nt], ALU.mult, ALU.add)
            R_ = lambda j: fs[:, j, 0:nt]
            K_ = lambda j: fs[:, 4 + j, 0:nt]
            V_ = lambda j: fs[:, 8 + j, 0:nt]
            P.act(twl[:, 0:nt], fs[:, 12, 0:nt], AF.Tanh)
            P.act(sgl[:, 0:nt], fs[:, 14, 0:nt], AF.Sigmoid)
            for j in range(4):
                kkr = tmp()
                P.ts(kkr[:, 0:nt], K_(j), kkc(j), ALU.mult)
                sq = tmp()
                P.act(sq[:, 0:nt], kkr[:, 0:nt], AF.Square)
                b = rot(g, "pj", 0, 8)
                P.mm(b[:, 0:nt], blkf, sq[:, 0:nt])
                rn = tmp()
                P.act(rn[:, 0:nt], b[:, 0:nt], AF.Sqrt, bias=g.epsc[:, 0:1])
                P.recip(rn[:, 0:nt], rn[:, 0:nt])
                P.tt(kk[:, 0:nt], kkr[:, 0:nt], rn[:, 0:nt], ALU.mult)
                b = rot(g, "pj", 0, 8)
                P.mm(b[:, 0:nt], g2s[:, j * 128:(j + 1) * 128], sgl[:, 0:nt])
                go = gob[j % 2]
                P.copy(go[:, 0:nt], b[:, 0:nt], q="act")
                P.dma(V(d["gT"].ap[j, :, t0:t0 + nt], None), go[:, 0:nt])
                rk = tmp()
                P.stt(rk[:, 0:nt], R_(j), rkc(j), K_(j), ALU.mult, ALU.mult)
                b = rot(g, "pj", 0, 8)
                P.mm(b[:, 0:nt], blkf, rk[:, 0:nt])
                bon = tmp()
                P.tt(bon[:, 0:nt], b[:, 0:nt], V_(j), ALU.mult)
                P.dma(V(d["bonT"].ap[j, :, t0:t0 + nt], None), bon[:, 0:nt])
                if RW != F32:
                    vr = ob()
                    P.copy(vr[:, 0:nt], V_(j), q="pool")
                    vsrc = vr
                else:
                    vsrc = fs[:, 8 + j, :]
                b = rot(g, "pj", 0, 8)
                bR = b.cast(RW) if RW != F32 else b
                for cb in range(nch):
                    P.tr(bR[:, cb * 128:(cb + 1) * 128], vsrc[:, cb * 128:(cb + 1) * 128], identR)
                vtile = g.rw_vt
                P.copy(vtile[:, 0:nch, j * 128:(j + 1) * 128], bR[:, 0:nt].re("p (c f) -> p c f", f=128), q="act")
                for dd in range(2):
                    p0 = dd * 64
                    b = rot(g, "pj", 0, 8)
                    P.mm(b[:, 0:nt], w2T[p0:p0 + 64, j * 128:(j + 1) * 128], twl[p0:p0 + 64, 0:nt])
                    lw = tmp()
                    P.act(lw[:, 0:nt], b[:, 0:nt], AF.Sigmoid, bias=w0c(dd, j))
                    P.ts(lw[:, 0:nt], lw[:, 0:nt], -0.6065306597126334, ALU.mult)
                    b = rot(g, "pj", 0, 8)
                    P.mm(b[:, 0:nt], a2T[p0:p0 + 64, j * 128:(j + 1) * 128], fs[p0:p0 + 64, 13, 0:nt])
                    a_ = tmp()
                    P.act(a_[:, 0:nt], b[:, 0:nt], AF.Sigmoid, bias=a0c(dd, j))
                    pf = tmp()
                    P.scan(pf[:, 0:nt], rst[:, 0:nt], lw[:, 0:nt], 0.0, ALU.mult, ALU.add)
                    if dd == 1:
                        pb = tmp()
                        P.tt(pb[:, 0:nt], lw[:, 0:nt], pf[:, 0:nt], ALU.subtract)
                        tot = pf[:, 0:nt].re("p (c i) -> p c i", i=CW)[:, :, CW - 1:CW].bc([128, nch, CW])
                        P.tt(pb[:, 0:nt].re("p (c i) -> p c i", i=CW), pb[:, 0:nt].re("p (c i) -> p c i", i=CW), tot, ALU.add)
                        pp = pb
                    else:
                        pp = pf
                    ep = tmp()
                    P.act(ep[:, 0:nt], pp[:, 0:nt], AF.Exp)
                    en = tmp()
                    P.act(en[:, 0:nt], pp[:, 0:nt], AF.Exp, scale=-1.0)
                    pm = tmp()
                    P.tt(pm[:, 0:nt], pp[:, 0:nt], lw[:, 0:nt], ALU.subtract)
                    P.act(pm[:, 0:nt], pm[:, 0:nt], AF.Exp)
                    epv = ep[:, 0:nt].re("p (c i) -> p c i", i=CW)
                    idx = CW - 1 if dd == 0 else 0
                    P.copy(gC[:, dd, j, ch0:ch0 + nch], epv[:, :, idx], q="pool")
                    kkt = ob()
                    P.tt(kkt[:, 0:nt], kk[:, 0:nt], pm[:, 0:nt], ALU.mult)
                    rt = ob()
                    P.tt(rt[:, 0:nt], R_(j), ep[:, 0:nt], ALU.mult)
                    qd = V(d["QR"].ap[dd, j, :, ch0 * 2 * CW:(ch0 + nch) * 2 * CW].rearrange("p (c w i) -> p c w i", w=2, i=CW), None)
                    P.dma(qd[:, :, 0, :], kkt[:, 0:nt].re("p (c i) -> p c i", i=CW))
                    P.dma(qd[:, :, 1, :], rt[:, 0:nt].re("p (c i) -> p c i", i=CW))
                    kd = tmp()
                    P.ts(kd[:, 0:nt], a_[:, 0:nt], kac(j), ALU.mult, omka[:, j:j + 1], ALU.add)
                    P.tt(kd[:, 0:nt], kd[:, 0:nt], K_(j), ALU.mult)
                    kh = ob()
                    P.tt(kh[:, 0:nt], kd[:, 0:nt], en[:, 0:nt], ALU.mult)
                    bd = tmp()
                    P.tt(bd[:, 0:nt], kk[:, 0:nt], a_[:, 0:nt], ALU.mult)
                    bh = ob()
                    P.tt(bh[:, 0:nt], bd[:, 0:nt], en[:, 0:nt], ALU.mult)
                    P.dma(V(d["KH"].ap[dd, j, :, t0:t0 + nt], None), kh[:, 0:nt])
                    P.dma(V(d["BH"].ap[dd, j, :, t0:t0 + nt], None), bh[:, 0:nt])
                    for (src, dst) in ((kh, g.rw_kt[dd]), (bh, g.rw_bt[dd])):
                        b = rot(g, "pj", 0, 8)
                        bR = b.cast(RW) if RW != F32 else b
                        for cb in range(nch):
                            P.tr(bR[:, cb * 128:(cb + 1) * 128], src[:, cb * 128:(cb + 1) * 128], identR)
                        P.copy(dst[:, 0:nch, j * 128:(j + 1) * 128], bR[:, 0:nt].re("p (c f) -> p c f", f=128), q="act")
            P.dma(dview(d["Vt"][ch0:ch0 + nch], "c p f -> p c f"), g.rw_vt[:, 0:nch, :])
            for dd in range(2):
                P.dma(dview(d["KHt"][dd, ch0:ch0 + nch], "c p f -> p c f"), g.rw_kt[dd][:, 0:nch, :])
                P.dma(dview(d["BHt"][dd, ch0:ch0 + nch], "c p f -> p c f"), g.rw_bt[dd][:, 0:nch, :])


def rwkv_scan(g, l, S, gC):
    P = g.P
    NT = g.ntok
    NCH = NT // CW
    RW = g.rwdt
    d = g.d
    mskA = [S.sb("rw_mA%d" % i, [128, 256]) for i in range(2)]
    mskN = [S.sb("rw_mN%d" % i, [128, 128]) for i in range(2)]

    def tri(v, step, cm, op):
        a = v.ap
        P.memset(v, 1.0, q="pool")
        P.op("pool", lambda e_: e_.affine_select(a, a, [[step, 128]], op, 0.0, base=0, channel_multiplier=cm), [v], [v])
    tri(mskA[0][:, 0:128], 1, -1, ALU.is_gt)
    tri(mskA[0][:, 128:256], 1, -1, ALU.is_ge)
    tri(mskN[0], -1, 1, ALU.is_gt)
    tri(mskA[1][:, 0:128], -1, 1, ALU.is_gt)
    tri(mskA[1][:, 128:256], -1, 1, ALU.is_ge)
    tri(mskN[1], 1, -1, ALU.is_gt)
    for dd in range(2):
        P.ts(mskN[dd], mskN[dd], -1.0, ALU.mult)
    Hs = [[S.sb("rw_H%d%d" % (dd, i), [128, 4, 64], RW) for i in range(2)] for dd in range(2)]
    for dd in range(2):
        P.memset(Hs[dd][0], 0.0)
    QRb = [S.sb("rw_QRb%d" % dd, [128, 4, 256], RW) for dd in range(2)]
    KHb = [S.sb("rw_KHb%d" % dd, [128, 4, 128], RW) for dd in range(2)]
    BHb = [S.sb("rw_BHb%d" % dd, [128, 4, 128], RW) for dd in range(2)]
    KHtb = [S.sb("rw_KHtb%d" % dd, [128, 512], RW) for dd in range(2)]
    BHtb = [S.sb("rw_BHtb%d" % dd, [128, 512], RW) for dd in range(2)]
    Vtb = [S.sb("rw_Vtb%d" % dd, [128, 512], RW) for dd in range(2)]
    A1s = [S.sb("rw_A1s%d" % dd, [128, 8, 256], RW) for dd in range(2)]
    A2s = [S.sb("rw_A2s%d" % dd, [128, 8, 256], RW) for dd in range(2)]
    ZR = [[S.sb("rw_ZR%d%d" % (dd, i), [128, 8, 256], RW) for i in range(2)] for dd in range(2)]
    Ys = [[S.sb("rw_Y%d%d" % (dd, i), [128, 8, 128], RW) for i in range(2)] for dd in range(2)]
    Rfin = [None, None]
    Wsb = [S.sb("rw_W%d" % dd, [128, 8, 64], RW) for dd in range(2)]
    Un = [S.sb("rw_Un%d" % dd, [128, 8, 64], RW) for dd in range(2)]
    yo = [S.sb("rw_yo%d" % dd, [128, 4, 128]) for dd in range(2)]
    identb = g.ident.re("p (o f) -> p o f", o=1).bc([128, 4, 128])
    nctx = LCTX // CW
    order = [list(range(NCH)), list(range(nctx - 1, -1, -1)) + list(range(NCH - 1, nctx - 1, -1))]
    ev = [0]

    fr = False

    def rr(v):
        return v.cast(F32R) if fr else v

    def evac(dst, src):
        P.copy(dst, src, q=("act" if ev[0] % 2 else "dve"))
        ev[0] += 1

    def prod8(dd, lhs, rhs, evac_fn):
        for half in range(2):
            b = rot(g, "rw", 0, 8)
            bv = b.re("p (h f) -> p h f", f=128)
            for hh in range(4):
                h = half * 4 + hh
                P.mm(bv[:, hh, :], rr(lhs[:, h, :]), rr(rhs[:, h, :]))
            evac_fn(half, bv)

    for step in range(NCH):
        cur = step % 2
        nxt = 1 - cur
        cs = [order[0][step], order[1][step]]
        for dd in range(2):
            c = cs[dd]
            P.dma(QRb[dd], dview(d["QR"][dd, :, :, c * 256:(c + 1) * 256], "j p f -> p j f"))
            P.dma(KHb[dd], dview(d["KH"][dd, :, :, c * CW:(c + 1) * CW], "j p f -> p j f"))
            P.dma(BHb[dd], dview(d["BH"][dd, :, :, c * CW:(c + 1) * CW], "j p f -> p j f"))
            P.dma(KHtb[dd], V(d["KHt"].ap[dd, c], None))
            P.dma(BHtb[dd], V(d["BHt"].ap[dd, c], None))
            P.dma(Vtb[dd], V(d["Vt"].ap[c], None))
        for dd in range(2):
            for j in range(4):
                b1 = rot(g, "rw", 0, 8)
                b2 = rot(g, "rw", 0, 8)
                b3 = rot(g, "rw", 0, 8)
                b1v = b1.re("p (h f) -> p h f", f=256)
                b2v = b2.re("p (h f) -> p h f", f=256)
                b3v = b3[:, 0:256].re("p (h f) -> p h f", f=128)
                for hp in range(2):
                    p0 = hp * 64
                    P.mm(b1v[:, hp, :], KHb[dd][p0:p0 + 64, j, :], QRb[dd][p0:p0 + 64, j, :])
                    P.mm(b2v[:, hp, :], BHb[dd][p0:p0 + 64, j, :], QRb[dd][p0:p0 + 64, j, :])
                    P.mm(b3v[:, hp, :], QRb[dd][p0:p0 + 64, j, 0:128], BHb[dd][p0:p0 + 64, j, :])
                mA = mskA[dd].re("p (o f) -> p o f", o=1).bc([128, 2, 256])
                mN = mskN[dd].re("p (o f) -> p o f", o=1).bc([128, 2, 128])
                P.tt(A1s[dd][:, 2 * j:2 * j + 2, :], b1v, mA, ALU.mult)
                P.tt(A2s[dd][:, 2 * j:2 * j + 2, :], b2v, mA, ALU.mult)
                P.tt(rr(Ys[dd][0][:, 2 * j:2 * j + 2, :]), b3v, mN, ALU.mult)
            P.ts(ZR[dd][0][:, :, 0:128], A2s[dd][:, :, 0:128], -1.0, ALU.mult)
            for half in range(2):
                P.copy(ZR[dd][0][:, half * 4:half * 4 + 4, 128:256], identb, q="pool")
        nlev = 7
        for lev in range(nlev):
            a, bn = lev % 2, (lev + 1) % 2
            last = (lev == nlev - 1)
            for dd in range(2):
                zr, zn, yz = ZR[dd][a], ZR[dd][bn], Ys[dd][a]
                for hp2 in range(4):
                    b = rot(g, "rw", 0, 8)
                    bv = b.re("p (h f) -> p h f", f=256)
                    for hh in range(2):
                        h = hp2 * 2 + hh
                        if last:
                            P.mm(bv[:, hh, 128:256], yz[:, h, :], zr[:, h, 128:256])
                        else:
                            P.mm(bv[:, hh, :], yz[:, h, :], zr[:, h, :])
                    h0 = hp2 * 2
                    if not last:
                        P.copy(zn[:, h0:h0 + 2, 0:128], bv[:, :, 0:128], q="act")
                    P.tt(zn[:, h0:h0 + 2, 128:256], bv[:, :, 128:256], zr[:, h0:h0 + 2, 128:256], ALU.add)
            if not last:
                for dd in range(2):
                    zn, yn = ZR[dd][bn], Ys[dd][bn]
                    for half in range(2):
                        b = rot(g, "rw", 0, 8)
                        bv = b.re("p (h f) -> p h f", f=128)
                        for hh in range(4):
                            h = half * 4 + hh
                            P.tr(bv[:, hh, :], zn[:, h, 0:128], g.ident)
                        P.copy(yn[:, half * 4:half * 4 + 4, :], bv, q="act")
        for dd in range(2):
            Rfin[dd] = ZR[dd][nlev % 2]
        for dd in range(2):
            H0 = Hs[dd][cur]
            bW = rot(g, "rw", 0, 8)
            bWv = bW.re("p (h f) -> p h f", f=64)
            for h in range(8):
                j, p0 = h // 2, (h % 2) * 64
                P.mm(bWv[:, h, :], A1s[dd][:, h, 0:128], Vtb[dd][:, h * 64:(h + 1) * 64], start=True, stop=False)
                P.mm(bWv[:, h, :], QRb[dd][p0:p0 + 64, j, 0:128], H0[p0:p0 + 64, j, :], start=False, stop=True)
            evac(rr(Wsb[dd]), bWv)
        for dd in range(2):
            bU = rot(g, "rw", 0, 8)
            bUv = bU.re("p (h f) -> p h f", f=64)
            for h in range(8):
                P.mm(bUv[:, h, :], Rfin[dd][:, h, 128:256], Wsb[dd][:, h, :])
            P.act(Un[dd], bUv, AF.Copy, scale=-1.0)
        for dd in range(2):
            c = cs[dd]
            H0 = Hs[dd][cur]
            H1 = Hs[dd][nxt]
            bY = rot(g, "rw", 0, 8)
            bYv = bY.re("p (j f) -> p j f", f=128)
            for h in range(8):
                j, p0 = h // 2, (h % 2) * 64
                P.mm(bYv[p0:p0 + 64, j, :], H0[p0:p0 + 64, j, :], QRb[dd][p0:p0 + 64, j, 128:256], start=True, stop=False)
                P.mm(bYv[p0:p0 + 64, j, :], Vtb[dd][:, h * 64:(h + 1) * 64], A1s[dd][:, h, 128:256], start=False, stop=False)
                P.mm(bYv[p0:p0 + 64, j, :], Un[dd][:, h, :], A2s[dd][:, h, 128:256], start=False, stop=True)
            evac(yo[dd], bYv)
            P.dma(dview(d["yT"][dd, :, :, c * CW:(c + 1) * CW], "j p f -> p j f"), yo[dd])
            bH = rot(g, "rw", 0, 8)
            bHv = bH[:, 0:256].re("p (j f) -> p j f", f=64)
            for h in range(8):
                j, p0 = h // 2, (h % 2) * 64
                P.mm(bHv[p0:p0 + 64, j, :], KHtb[dd][:, h * 64:(h + 1) * 64], Vtb[dd][:, h * 64:(h + 1) * 64], start=True, stop=False)
                P.mm(bHv[p0:p0 + 64, j, :], BHtb[dd][:, h * 64:(h + 1) * 64], Un[dd][:, h, :], start=False, stop=True)
            P.tt(H1, bHv, H0, ALU.add)
            P.tt(H1, H1, gC[:, dd, :, c:c + 1].bc([128, 4, 64]), ALU.mult)


def rwkv_readout(g, l, S):
    P = g.P
    e = l // 2
    d = g.d
    gn = S.sb("rw_gn", [128, 8])
    rows_to_cols(g, S, gn[:, 0:4], dview(d["rwkv_gn_w"][e], "(c p) -> c p", p=128), 4, "rw_gnws")
    rows_to_cols(g, S, gn[:, 4:8], dview(d["rwkv_gn_b"][e], "(c p) -> c p", p=128), 4, "rw_gnbs")
    blkf = S.sb("rw_blkf2", [128, 128])
    P.memset(blkf, 0.0)
    P.memset(blkf[0:64, 0:64], 1.0)
    P.memset(blkf[64:128, 64:128], 1.0)
    gne = S.sb("rw_gne", [128, 1])
    P.memset(gne, GN_EPS)
    nb = 2
    yf = [S.sb("ro_yf%d" % i, [128, 512]) for i in range(nb)]
    yb = [S.sb("ro_yb%d" % i, [128, 512]) for i in range(nb)]
    bo_ = [S.sb("ro_bon%d" % i, [128, 512]) for i in range(nb)]
    gt = [S.sb("ro_g%d" % i, [128, 512], BF16) for i in range(nb)]
    sq = [S.sb("ro_sq%d" % i, [128, 512]) for i in range(nb)]
    mean = [S.sb("ro_mean%d" % i, [128, 512]) for i in range(nb)]
    var = [S.sb("ro_var%d" % i, [128, 512]) for i in range(nb)]
    oo = [S.sb("ro_o%d" % i, [128, 512], BF16) for i in range(nb)]
    ctx_out = l < g.depth - 1
    tiles = ([(0, LCTX)] if ctx_out else []) + [(LCTX + t * 512, 512) for t in range(g.nlat // 512)]
    n = 0
    for (t0, nt) in tiles:
        for j in range(4):
            i = n % nb
            n += 1
            y_, y2, b_, g_, s_, m_, v_, o_ = yf[i], yb[i], bo_[i], gt[i], sq[i], mean[i], var[i], oo[i]
            P.dma(y_[:, 0:nt], V(d["yT"].ap[0, j, :, t0:t0 + nt], None))
            P.dma(y2[:, 0:nt], V(d["yT"].ap[1, j, :, t0:t0 + nt], None))
            P.dma(b_[:, 0:nt], V(d["bonT"].ap[j, :, t0:t0 + nt], None))
            P.dma(g_[:, 0:nt], V(d["gT"].ap[j, :, t0:t0 + nt], None))
            P.tt(y_[:, 0:nt], y_[:, 0:nt], y2[:, 0:nt], ALU.add, q="pool")
            P.act(s_[:, 0:nt], y_[:, 0:nt], AF.Square)
            b1 = rot(g, "pj", 0, 8)
            b2 = rot(g, "pj", 0, 8)
            P.mm(b1[:, 0:nt], blkf, y_[:, 0:nt])
            P.mm(b2[:, 0:nt], blkf, s_[:, 0:nt])
            P.act(m_[:, 0:nt], b1[:, 0:nt], AF.Copy, scale=1.0 / HD)
            P.tt(s_[:, 0:nt], m_[:, 0:nt], m_[:, 0:nt], ALU.mult, q="pool")
            P.stt(v_[:, 0:nt], b2[:, 0:nt], 1.0 / HD, s_[:, 0:nt], ALU.mult, ALU.subtract)
            P.act(v_[:, 0:nt], v_[:, 0:nt], AF.Sqrt, bias=gne[:, 0:1])
            P.recip(v_[:, 0:nt], v_[:, 0:nt])
            P.tt(y_[:, 0:nt], y_[:, 0:nt], m_[:, 0:nt], ALU.subtract, q="pool")
            P.tt(y_[:, 0:nt], y_[:, 0:nt], v_[:, 0:nt], ALU.mult)
            P.ts(y_[:, 0:nt], y_[:, 0:nt], gn[:, j:j + 1], ALU.mult, gn[:, 4 + j:5 + j], ALU.add)
            P.tt(y_[:, 0:nt], y_[:, 0:nt], b_[:, 0:nt], ALU.add, q="pool")
            P.tt(o_[:, 0:nt], y_[:, 0:nt], g_[:, 0:nt], ALU.mult)
            P.dma(V(d["oT"].ap[4 + j, :, t0:t0 + nt], None), o_[:, 0:nt])


def emit_outproj(g, l):
    P = g.P
    even = (l % 2 == 0)
    ctx_out = l < g.depth - 1
    wo_d = dview(g.d["even_w_out" if even else "odd_w_out"][l // 2], "(kc p) n -> p kc n", p=128)
    xTd = g.d["xT"]
    with P.scope() as S:
        wo = S.sb("wo", [128, KC, D], BF16)
        stp = [S.sb("ost%d" % i, [128, KC, 512]) for i in range(2)]
        cnt = [0]
        for s_ in range(2):
            load_cast(g, S, stp, cnt, wo[:, :, s_ * 512:(s_ + 1) * 512], wo_d[:, :, s_ * 512:(s_ + 1) * 512], 512)
        oTt = [S.sb("op_oT%d" % i, [128, KC, 512], BF16) for i in range(2)]
        xt = [S.sb("op_xt%d" % i, [128, KC, 512]) for i in range(2)]
        tiles = []
        if ctx_out:
            tiles.append((0, LCTX, 1))
        tiles += [(LCTX + t * 512, 512, 0) for t in range(g.nlat // 512)]
        for ti, (t0, nt, n) in enumerate(tiles):
            o_ = oTt[ti % 2]
            x_ = xt[ti % 2]
            P.dma(o_[:, :, 0:nt], dview(g.d["oT"][:, :, t0:t0 + nt], "c p t -> p c t"))
            P.dma(x_[:, :, 0:nt], dview(xTd[:, :, t0:t0 + nt], "c p t -> p c t"))
            for c in range(KC):
                b = rot(g, "op", 0, 8)
                for fc in range(KC):
                    P.mm(b[:, 0:nt], wo[:, fc, c * 128:(c + 1) * 128], o_[:, fc, 0:nt], start=(fc == 0), stop=(fc == KC - 1))
                P.stt(x_[:, c, 0:nt], b[:, 0:nt], mcol(g.modT, l, 5, c, n), x_[:, c, 0:nt], ALU.mult, ALU.add)
            P.dma(dview(xTd[:, :, t0:t0 + nt], "c p t -> p c t"), x_[:, :, 0:nt])


WEIGHT_SPECS = [
    ("w_mod", [4, D, 9 * D]), ("b_mod", [4, 9 * D]), ("ffn_in", [4, 2, D, 2 * DFF]), ("ffn_out", [4, 2, DFF, D]),
    ("final_gain", [D]),
    ("odd_w_in", [2, D, 1536]), ("odd_qk_sw", [2, D, 1280]), ("odd_w_out", [2, D, D]), ("sink", [2, 16]),
    ("even_w_in", [2, D, 2688]), ("even_qk_sw", [2, D, 640]), ("even_w_out", [2, D, D]),
    ("q_gain", [2, 64]), ("q_gain_sw", [2, 64]), ("k_gain", [2, 64]), ("k_gain_sw", [2, 64]),
    ("rwkv_mu", [2, 2, 1920]), ("rwkv_w0", [2, 2, 512]), ("rwkv_w2", [2, 2, 64, 512]), ("rwkv_a0", [2, 2, 512]),
    ("rwkv_a2", [2, 2, 64, 512]), ("rwkv_g2", [2, 128, 512]), ("rwkv_k_k", [2, 512]), ("rwkv_k_a", [2, 512]),
    ("rwkv_r_k", [2, 8, 64]), ("rwkv_gn_w", [2, 512]), ("rwkv_gn_b", [2, 512]),
]


def default_plan(depth):
    plan = []
    for l in range(depth):
        plan += [("ffn1", l), ("mix", l), ("ffn2", l)]
    return plan


def build(nlat=4096, depth=4, plan=None, debug=False, rwdt=F32):
    nc = bass.Bass("TRN2", target_bir_lowering=False)
    es = ExitStack()
    with es:
        g = G()
        g.P = P = Prog(nc, es)
        g.nlat = nlat
        g.depth = depth
        g.ntok = LCTX + nlat
        g.rotc = {}
        g.d = {}
        g.d["x"] = P.dram("x", [nlat, D], kind="ExternalInput")
        g.d["ctx"] = P.dram("ctx", [LCTX, D], kind="ExternalInput")
        g.d["cvec"] = P.dram("cvec", [2, D], kind="ExternalInput")
        g.d["cosT"] = P.dram("cosT", [128, nlat], kind="ExternalInput")
        g.d["sinT"] = P.dram("sinT", [128, nlat], kind="ExternalInput")
        for nm, shp in WEIGHT_SPECS:
            g.d[nm] = P.dram(nm, shp, kind="ExternalInput")
        g.d["out"] = P.dram("out", [nlat, D], kind="ExternalOutput")
        g.d["xT"] = P.dram("xT", [KC, 128, g.ntok], kind="Internal")
        sk = "ExternalOutput" if debug else "Internal"
        g.d["qT"] = P.dram("qT", [KC, 128, g.ntok], BF16, kind=sk)
        g.d["kT2"] = P.dram("kT2", [4, 128, g.ntok], BF16, kind=sk)
        g.d["Vd"] = P.dram("Vd", [g.ntok // 128, 128, 4 * 192], BF16, kind=sk)
        g.d["oT"] = P.dram("oT", [KC, 128, g.ntok], BF16, kind=sk)
        g.rwdt = rwdt
        nch = g.ntok // CW
        g.d["fT"] = P.dram("fT", [15, 128, g.ntok], kind=sk)
        g.d["QR"] = P.dram("QR", [2, 4, 128, nch * 2 * CW], rwdt, kind=sk)
        g.d["KH"] = P.dram("KH", [2, 4, 128, g.ntok], rwdt, kind=sk)
        g.d["BH"] = P.dram("BH", [2, 4, 128, g.ntok], rwdt, kind=sk)
        g.d["KHt"] = P.dram("KHt", [2, nch, 128, 512], rwdt, kind=sk)
        g.d["BHt"] = P.dram("BHt", [2, nch, 128, 512], rwdt, kind=sk)
        g.d["Vt"] = P.dram("Vt", [nch, 128, 512], rwdt, kind=sk)
        g.d["gT"] = P.dram("gT", [4, 128, g.ntok], BF16, kind=sk)
        g.d["bonT"] = P.dram("bonT", [4, 128, g.ntok], kind=sk)
        g.d["yT"] = P.dram("yT", [2, 4, 128, g.ntok], kind=sk)
        g.d["w_in_b"] = P.dram("w_in_b", [2 * depth, FC // 2, 128, KC * 512], BF16, kind="Internal")
        g.d["w_out_b"] = P.dram("w_out_b", [2 * depth, 128, FC * D], BF16, kind="Internal")
        setup_consts(g)
        g.bg = BgPrep(g)
        g.epsc = P.sb("epsc", [128, 1])
        P.memset(g.epsc, EPS)
        emit_mod(g)
        emit_in_transpose(g)
        for (st, l) in (plan if plan is not None else default_plan(depth)):
            if st == "ffn1":
                emit_ffn(g, l, 0)
            elif st == "ffn2":
                emit_ffn(g, l, 1)
            elif st == "mix":
                g.bg.limit_k = 2 * l + 2
                emit_attn_proj(g, l)
                emit_attn(g, l)
                if l % 2 == 0:
                    emit_rwkv(g, l)
                emit_outproj(g, l)
        finals = emit_final(g)
        P.emit(finals)
    return nc


def rope_tables(nlat):
    n = np.arange(nlat)
    row = (n // 64).astype(np.float32)
    col = (n % 64).astype(np.float32)
    nf = 16
    inv = (np.float32(10000.0) ** (-np.arange(nf, dtype=np.float32) / np.float32(nf))).astype(np.float32)
    ang = np.concatenate([row[:, None] * inv, col[:, None] * inv], axis=-1).astype(np.float32)
    cos, sin = np.cos(ang).astype(np.float32), np.sin(ang).astype(np.float32)
    d = np.arange(64)
    cosT = cos[:, d // 2].T
    sgn = np.where(d % 2 == 0, -1.0, 1.0).astype(np.float32)
    sinT = (sin[:, d // 2] * sgn[None, :]).T
    return (np.ascontiguousarray(np.concatenate([cosT, cosT], 0)), np.ascontiguousarray(np.concatenate([sinT, sinT], 0)))


def host_layout(inputs, b, nlat=4096):
    f = lambda a: np.ascontiguousarray(np.asarray(a, dtype=np.float32))
    sw = lambda w, n: f(w[..., (np.arange(n) ^ 1)])
    cosT, sinT = rope_tables(nlat)
    m = {
        "x": f(inputs["x"][b, :nlat]), "ctx": f(inputs["ctx"][b]),
        "cvec": f(np.stack([np.asarray(inputs["c"][b]), np.asarray(inputs["c_ctx"])])),
        "cosT": cosT, "sinT": sinT,
        "odd_qk_sw": sw(np.asarray(inputs["odd_w_in"])[:, :, :1280], 1280),
        "even_qk_sw": sw(np.asarray(inputs["even_w_in"])[:, :, :640], 640),
        "q_gain_sw": sw(np.asarray(inputs["q_gain"]), 64), "k_gain_sw": sw(np.asarray(inputs["k_gain"]), 64),
    }
    for nm, _ in WEIGHT_SPECS:
        if nm not in m:
            m[nm] = f(inputs[nm])
    return m


_NC_CACHE = {}


def kernel(**inputs):
    nlat = int(np.asarray(inputs["x"]).shape[1])
    nb = int(np.asarray(inputs["x"]).shape[0])
    if "nc" not in _NC_CACHE:
        _NC_CACHE["nc"] = build(nlat=nlat, depth=4)
    nc = _NC_CACHE["nc"]
    in_maps = [host_layout(inputs, b, nlat) for b in range(nb)]
    res = run_bass_kernel_spmd(nc, in_maps, core_ids=list(range(nb)))
    return np.stack([np.asarray(r["out"], dtype=np.float32) for r in res.results], axis=0)
```

```python
import numpy as np
from contextlib import ExitStack
import concourse.bass as bass
import concourse.mybir as mybir
from concourse.bass_utils import run_bass_kernel_spmd

F32 = mybir.dt.float32
BF16 = mybir.dt.bfloat16
F32R = mybir.dt.float32r
AF = mybir.ActivationFunctionType
ALU = mybir.AluOpType
AX = mybir.AxisListType


class Tile:
    __slots__ = ("name", "lw", "rd", "dsem", "dcnt", "dlast", "dram")

    def __init__(self, name, dram=False):
        self.name = name
        self.lw = None
        self.rd = []
        self.dsem = None
        self.dcnt = 0
        self.dlast = None
        self.dram = dram


class V:
    __slots__ = ("ap", "t")

    def __init__(self, ap, t):
        self.ap = ap
        self.t = t

    def __getitem__(self, idx):
        return V(self.ap[idx], self.t)

    def re(self, pattern, **kw):
        return V(self.ap.rearrange(pattern, **kw), self.t)

    def bc(self, shape):
        return V(self.ap.broadcast_to(shape), self.t)

    def cast(self, dt):
        return V(self.ap.bitcast(dt), self.t)

    @property
    def shape(self):
        return self.ap.shape


class Ins:
    __slots__ = ("q", "fn", "deps", "dma", "sem", "val", "signal", "idx")

    def __init__(self, q, fn, dma=False):
        self.q = q
        self.fn = fn
        self.deps = []
        self.dma = dma
        self.sem = None
        self.val = 0
        self.signal = dma
        self.idx = 0


QUEUES = ("pe", "act", "dve", "pool", "sp")


class Scope:
    def __init__(self, P):
        self.P = P
        self.es = ExitStack()

    def __enter__(self):
        self.es.__enter__()
        self.tiles = []
        self.P.scope_stack.append(self.tiles)
        return self

    def sb(self, name, shape, dt=F32):
        self.P.ntile += 1
        name = "%s_%d" % (name, self.P.ntile)
        t = self.es.enter_context(self.P.nc.sbuf_tensor(name, list(shape), dt))
        return V(t[:], Tile(name))

    def __exit__(self, *a):
        P = self.P
        P.barrier()
        for t in self.tiles:
            if t.dsem is not None:
                P.live_sems.remove(t.dsem)
                P.free_sems.append(t.dsem)
                t.dsem = None
        P.scope_stack.pop()
        return self.es.__exit__(*a)


class Prog:
    def __init__(self, nc, es):
        self.nc = nc
        self.es = es
        self.q = {k: [] for k in QUEUES}
        self.esem = {}
        for k in ("pe", "act", "dve", "pool"):
            self.esem[k] = es.enter_context(nc.semaphore("sem_" + k))
        self.ntile = 0
        self.nsem = 4
        self.bar = {}
        self.free_sems = []
        self.live_sems = []
        self.scope_stack = []

    def sb(self, name, shape, dt=F32):
        t = self.es.enter_context(self.nc.sbuf_tensor(name, list(shape), dt))
        return V(t[:], Tile(name))

    def ps(self, name, shape, dt=F32):
        t = self.es.enter_context(self.nc.psum_tensor(name, list(shape), dt))
        return V(t[:], Tile(name))

    def dram(self, name, shape, dt=F32, kind="Internal"):
        t = self.nc.dram_tensor(name, list(shape), dt, kind=kind)
        return V(t.ap(), None)

    def sub(self, v, name):
        return V(v.ap, Tile(name))

    def _rec(self, q, fn, reads, writes, dma=False):
        ins = Ins(q, fn, dma)
        compute_inorder = (not dma) and q == "pe"
        deps = []
        reads = [v for v in reads if isinstance(v, V) and v.t is not None]
        writes = [v for v in writes if isinstance(v, V) and v.t is not None]
        for v in reads:
            t = v.t
            w = t.lw
            if w is not None:
                deps.append(w)
        for v in writes:
            t = v.t
            w = t.lw
            if w is not None and not (compute_inorder and not w.dma and w.q == q):
                deps.append(w)
            for r in t.rd:
                if not (compute_inorder and not r.dma and r.q == q):
                    deps.append(r)
        for v in reads:
            v.t.rd.append(ins)
        for v in writes:
            v.t.lw = ins
            v.t.rd = []
        if self.bar.get(q):
            deps.extend(self.bar[q])
            self.bar[q] = None
        for d in deps:
            if d is not ins:
                d.signal = True
        ins.deps = deps
        ins.idx = len(self.q[q])
        self.q[q].append(ins)
        return ins

    def op(self, q, fn, reads, writes):
        return self._rec(q, fn, reads, writes)

    def dma(self, out, in_, q="sp", **kw):
        st = in_.t if out.t is None else out.t
        if st.dsem is None:
            if self.free_sems:
                st.dsem = self.free_sems.pop()
            else:
                st.dsem = [self.es.enter_context(self.nc.semaphore("dsem%d" % self.nsem)), 0, None]
                self.nsem += 1
            self.live_sems.append(st.dsem)
            if self.scope_stack:
                self.scope_stack[-1].append(st)
        oap, iap = out.ap, in_.ap

        def fn(e):
            return e.dma_start(out=oap, in_=iap, **kw)
        ins = self._rec(q, fn, [in_], [out], dma=True)
        sem = st.dsem
        if sem[2] is not None:
            ins.deps.append(sem[2])
        sem[1] += 1
        sem[2] = ins
        ins.sem = sem[0]
        ins.val = 16 * sem[1]
        return ins

    def barrier(self):
        lst = []
        for k in ("pe", "act", "dve", "pool"):
            for ins in reversed(self.q[k]):
                if not ins.dma:
                    lst.append(ins)
                    break
        for sem in self.live_sems:
            if sem[2] is not None:
                lst.append(sem[2])
        for i in lst:
            i.signal = True
        for k in QUEUES:
            self.bar[k] = list(lst)

    def scope(self):
        return Scope(self)

    def mm(self, out, lhsT, rhs, start=True, stop=True, **kw):
        oa, la, ra = out.ap, lhsT.ap, rhs.ap
        return self.op("pe", lambda e: e.matmul(oa, la, ra, start=start, stop=stop, **kw), [lhsT, rhs], [out])

    def tr(self, out, in_, ident):
        oa, ia, da = out.ap, in_.ap, ident.ap
        return self.op("pe", lambda e: e.transpose(oa, ia, da), [in_, ident], [out])

    def act(self, out, in_, func, bias=None, scale=None, accum=None, q="act"):
        oa, ia = out.ap, in_.ap
        kw = {}
        rd = [in_]
        if bias is not None:
            kw["bias"] = bias.ap if isinstance(bias, V) else bias
            if isinstance(bias, V):
                rd.append(bias)
        if scale is not None:
            kw["scale"] = scale.ap if isinstance(scale, V) else scale
            if isinstance(scale, V):
                rd.append(scale)
        wr = [out]
        if accum is not None:
            kw["accum_out"] = accum.ap
            wr.append(accum)
        return self.op("act", lambda e: e.activation(oa, ia, func, **kw), rd, wr)

    def tt(self, out, in0, in1, op, q="dve"):
        oa, a, b = out.ap, in0.ap, in1.ap
        return self.op(q, lambda e: e.tensor_tensor(oa, a, b, op), [in0, in1], [out])

    def ts(self, out, in0, s1, op0, s2=None, op1=None, q="dve", accum=None):
        oa, a = out.ap, in0.ap
        rd = [in0]
        x1 = s1.ap if isinstance(s1, V) else s1
        x2 = s2.ap if isinstance(s2, V) else s2
        if isinstance(s1, V):
            rd.append(s1)
        if isinstance(s2, V):
            rd.append(s2)
        kw = {}
        wr = [out]
        if op1 is not None:
            kw["op1"] = op1
        if accum is not None:
            kw["accum_out"] = accum.ap
            wr.append(accum)
        return self.op(q, lambda e: e.tensor_scalar(oa, a, x1, x2, op0, **kw), rd, wr)

    def stt(self, out, in0, scalar, in1, op0, op1, q="dve"):
        oa, a, b = out.ap, in0.ap, in1.ap
        rd = [in0, in1]
        s = scalar.ap if isinstance(scalar, V) else scalar
        if isinstance(scalar, V):
            rd.append(scalar)
        return self.op(q, lambda e: e.scalar_tensor_tensor(oa, a, s, b, op0, op1), rd, [out])

    def copy(self, out, in_, q="dve"):
        oa, ia = out.ap, in_.ap
        if q == "act":
            return self.op(q, lambda e: e.copy(oa, ia), [in_], [out])
        return self.op(q, lambda e: e.tensor_copy(oa, ia), [in_], [out])

    def red(self, out, in_, op, axis=None, q="dve"):
        oa, ia = out.ap, in_.ap
        ax = AX.X if axis is None else axis
        return self.op(q, lambda e: e.tensor_reduce(oa, ia, ax, op), [in_], [out])

    def scan(self, out, d0, d1, init, op0, op1):
        oa, a, b = out.ap, d0.ap, d1.ap
        rd = [d0, d1]
        i0 = init.ap if isinstance(init, V) else init
        if isinstance(init, V):
            rd.append(init)
        return self.op("dve", lambda e: e.tensor_tensor_scan(oa, a, b, i0, op0, op1), rd, [out])

    def memset(self, out, val, q="dve"):
        oa = out.ap
        return self.op(q, lambda e: e.memset(oa, val), [], [out])

    def recip(self, out, in_):
        oa, ia = out.ap, in_.ap
        return self.op("dve", lambda e: e.reciprocal(oa, ia), [in_], [out])

    def emit(self, final_tiles):
        nc = self.nc
        for k in ("pe", "act", "dve", "pool"):
            c = 0
            for ins in self.q[k]:
                if ins.dma:
                    continue
                ins.sem = self.esem[k]
                if ins.signal:
                    c += 1
                    ins.val = c
                else:
                    ins.val = None
        engs = {"pe": "tensor", "act": "scalar", "dve": "vector", "pool": "gpsimd", "sp": "sync"}
        finals = [(i.sem, i.val) for i in final_tiles]
        with nc.Block() as block:
            for k in QUEUES:
                lst = self.q[k]
                if not lst and k != "sp":
                    continue

                def body(e, lst=lst, k=k):
                    waited = {}
                    for ins in lst:
                        need = {}
                        for d in ins.deps:
                            sid = id(d.sem)
                            if d.val is None:
                                raise RuntimeError("dep on non-signalling ins")
                            if waited.get(sid, 0) >= d.val:
                                continue
                            if sid not in need or need[sid][1] < d.val:
                                need[sid] = (d.sem, d.val)
                        for sid, (s, v) in need.items():
                            e.wait_ge(s, v)
                            waited[sid] = v
                        r = ins.fn(e)
                        if ins.dma:
                            r.then_inc(ins.sem, 16)
                        elif ins.signal:
                            r.then_inc(ins.sem, 1)
                    if k == "sp":
                        for s, v in finals:
                            e.wait_ge(s, v)
                getattr(block, engs[k])(body)


D = 1024
KC = 8
DFF = 2816
FC = 22
LCTX = 256
EPS = 1e-6


class G:
    pass


def dview(v, pattern, **kw):
    return V(v.ap.rearrange(pattern, **kw), None)


def setup_consts(g):
    P = g.P
    g.ident = P.sb("ident", [128, 128])
    P.memset(g.ident, 1.0, q="pool")
    ia = g.ident.ap
    P.op("pool", lambda e: e.affine_select(ia, ia, [[1, 128]], ALU.is_equal, 0.0, base=0, channel_multiplier=-1),
         [g.ident], [g.ident])
    g.onesb = P.sb("onesb", [128, 128], BF16)
    P.memset(g.onesb, 1.0)
    g.pb = [P.ps("pb%d" % i, [128, 512]) for i in range(8)]
    g.pbi = 0


def nbank(g):
    b = g.pb[g.pbi % 8]
    g.pbi += 1
    return b


def rows_to_cols(g, S, dst, src_rows, R, name):
    P = g.P
    st = S.sb(name, [R, 128])
    P.dma(st, src_rows)
    b = nbank(g)
    P.tr(b[:, 0:R], st, g.ident[0:R, 0:R])
    P.copy(dst, b[:, 0:R])


def emit_mod(g):
    P = g.P
    g.modT = P.sb("modT", [128, 4, 72, 2])
    g.modp1 = P.sb("modp1", [128, 4, 72, 2])
    g.modh = P.sb("modh", [128, 4, 72, 2])
    with P.scope() as S:
        crow = S.sb("crow", [2, D])
        P.dma(crow, g.d["cvec"])
        crs = S.sb("crs", [2, D])
        P.act(crs, crow, AF.Silu)
        csT = S.sb("csT", [128, KC, 2])
        b = nbank(g)
        for kc in range(KC):
            P.tr(b[:, 2 * kc:2 * kc + 2], crs[0:2, kc * 128:(kc + 1) * 128], g.ident[0:2, 0:2])
        P.copy(csT, b[:, 0:16].re("p (k n) -> p k n", n=2))
        bmT = S.sb("bmT", [128, 288])
        bm_rows = dview(g.d["b_mod"], "l (j p) -> (l j) p", p=128)
        for r in range(3):
            rows_to_cols(g, S, bmT[:, r * 96:(r + 1) * 96], bm_rows[r * 96:(r + 1) * 96, :], 96, "bmst%d" % r)
        slabs = [S.sb("wmslab%d" % i, [128, KC, 512]) for i in range(3)]
        mrow = S.sb("mrow", [2, 9 * D])
        n = 0
        for l in range(g.depth):
            wv = dview(g.d["w_mod"][l], "(kc p) n -> p kc n", p=128)
            for s_ in range(18):
                sl = slabs[n % 3]
                n += 1
                P.dma(sl, wv[:, :, s_ * 512:(s_ + 1) * 512])
                b = nbank(g)
                for kc in range(KC):
                    P.mm(b[0:2, :], csT[:, kc, :], sl[:, kc, :], start=(kc == 0), stop=(kc == KC - 1))
                P.copy(mrow[:, s_ * 512:(s_ + 1) * 512], b[0:2, :], q=("act" if s_ % 2 else "dve"))
            b = nbank(g)
            for jj in range(72):
                P.tr(b[:, 2 * jj:2 * jj + 2], mrow[0:2, jj * 128:(jj + 1) * 128], g.ident[0:2, 0:2])
            P.tt(g.modT[:, l, :, :], b[:, 0:144].re("p (j n) -> p j n", n=2),
                 bmT[:, l * 72:(l + 1) * 72].re("p (j o) -> p j o", o=1).bc([128, 72, 2]), ALU.add)
        dd = g.depth
        P.ts(g.modp1[:, 0:dd], g.modT[:, 0:dd], 1.0, ALU.add)
        P.ts(g.modh[:, 0:dd], g.modT[:, 0:dd], 0.5, ALU.mult)


def mcol(arr, l, i, c, n):
    return arr[:, l, i * 8 + c, n:n + 1]


def emit_in_transpose(g):
    P = g.P
    xTv = dview(g.d["xT"], "c p t -> p c t")
    with P.scope() as S:
        xin = [S.sb("xin%d" % i, [128, 4, D]) for i in range(2)]
        xtl = [S.sb("xtl%d" % i, [128, KC, 512]) for i in range(2)]
        groups = [("ctx", 0, 2, 0)] + [("x", t * 512, 4, LCTX + t * 512) for t in range(g.nlat // 512)]
        for gi, (nm, r0, nb, t0) in enumerate(groups):
            xi = xin[gi % 2]
            xt = xtl[gi % 2]
            src = dview(g.d[nm][r0:r0 + nb * 128, :], "(b p) f -> p b f", p=128)
            P.dma(xi[:, 0:nb, :], src)
            for c in range(KC):
                b = nbank(g)
                for bl in range(nb):
                    P.tr(b[:, bl * 128:(bl + 1) * 128], xi[:, bl, c * 128:(c + 1) * 128], g.ident)
                P.copy(xt[:, c, 0:nb * 128], b[:, 0:nb * 128], q=("act" if c % 2 else "dve"))
                g.bg.tick(2)
            P.dma(xTv[:, :, t0:t0 + nb * 128], xt[:, :, 0:nb * 128])


def norm_stats(g, sq, rstd, xt, ncols):
    P = g.P
    b = nbank(g)
    for c in range(KC):
        s_ = sq[c % 2]
        P.act(s_[:, 0:ncols], xt[:, c, 0:ncols], AF.Square)
        P.mm(b[:, 0:ncols], g.onesb, s_[:, 0:ncols], start=(c == 0), stop=(c == KC - 1))
    P.act(rstd[:, 0:ncols], b[:, 0:ncols], AF.Sqrt, scale=1.0 / D, bias=g.epsc[:, 0:1])
    P.recip(rstd[:, 0:ncols], rstd[:, 0:ncols])


def norm_apply(g, tmp, rstd, xt, hT, ncols, l, i_shift, n):
    P = g.P
    for c in range(KC):
        t_ = tmp[c % 2]
        P.stt(t_[:, 0:ncols], xt[:, c, 0:ncols], mcol(g.modp1, l, i_shift + 1, c, n), rstd[:, 0:ncols], ALU.mult, ALU.mult)
        P.act(hT[:, c, 0:ncols], t_[:, 0:ncols], AF.Identity, bias=mcol(g.modT, l, i_shift, c, n))


def emit_norm_mod(g, S, bufs, xt, hT, ncols, l, i_shift, n):
    sq, rstd, tmp = bufs
    norm_stats(g, sq, rstd, xt, ncols)
    norm_apply(g, tmp, rstd, xt, hT, ncols, l, i_shift, n)


class BgPrep:
    def __init__(self, g):
        self.g = g
        P = g.P
        self.st = [P.sb("bg_st%d" % i, [128, 2048]) for i in range(2)]
        self.ob = [P.sb("bg_ob%d" % i, [128, 2048], BF16) for i in range(2)]
        self.jobs = []
        for l in range(g.depth):
            for which in range(2):
                k = l * 2 + which
                for jp in range(FC // 2):
                    for kh in range(2):
                        self.jobs.append((k, "in", l, which, jp, kh))
                for fp in range(FC // 2):
                    self.jobs.append((k, "out", l, which, fp, 0))
        self.pos = 0
        self.pending = None
        self.calls = 0
        self.limit_k = 0

    def _emit_store(self):
        if self.pending is not None:
            dst, src = self.pending
            self.g.P.dma(dst, src)
            self.pending = None

    def step(self):
        if self.pos >= len(self.jobs):
            self._emit_store()
            return False
        g = self.g
        P = g.P
        k, kind, l, which, a, kh = self.jobs[self.pos]
        st = self.st[self.pos % 2]
        ob = self.ob[self.pos % 2]
        self.pos += 1
        if kind == "in":
            win = dview(g.d["ffn_in"][l, which], "(kc p) n -> p kc n", p=128)
            sv = st.re("p (k n) -> p k n", n=512)
            P.dma(sv[:, :, 0:256], win[:, kh * 4:kh * 4 + 4, a * 256:(a + 1) * 256])
            P.dma(sv[:, :, 256:512], win[:, kh * 4:kh * 4 + 4, DFF + a * 256:DFF + (a + 1) * 256])
            dst = V(g.d["w_in_b"].ap[k, a, :, kh * 2048:(kh + 1) * 2048], None)
        else:
            wout = dview(g.d["ffn_out"][l, which], "(fc p) n -> p fc n", p=128)
            P.dma(st.re("p (f n) -> p f n", n=1024), wout[:, a * 2:a * 2 + 2, :])
            dst = V(g.d["w_out_b"].ap[k, :, a * 2048:(a + 1) * 2048], None)
        self._emit_store()
        P.copy(ob, st, q="pool")
        self.pending = (dst, ob)
        return True

    def tick(self, every):
        self.calls += 1
        if self.calls % every == 0 and self.pos < len(self.jobs) and self.jobs[self.pos][0] <= self.limit_k:
            self.step()

    def finish(self, k):
        n = 0
        while self.pos < len(self.jobs) and self.jobs[self.pos][0] <= k:
            self.step()
            n += 1
        if self.pending is not None:
            self._emit_store()
            n += 1
        if n:
            self.g.P.barrier()


def emit_ffn(g, l, which):
    P = g.P
    i0 = 0 if which == 0 else 6
    k = l * 2 + which
    xTd = g.d["xT"]
    g.bg.finish(k)
    tiles = []
    if not (l == g.depth - 1 and which == 1):
        tiles.append((0, LCTX, 1))
    NT = 1024
    for t in range(g.nlat // NT):
        tiles.append((LCTX + t * NT, NT, 0))
    with P.scope() as S:
        xt = S.sb("f_xt", [128, KC, 512])
        hT = S.sb("f_hT", [128, KC, NT], BF16)
        actT = S.sb("f_actT", [128, FC, NT], BF16)
        sq = [S.sb("f_sq%d" % i, [128, 512], BF16) for i in range(2)]
        rstd = S.sb("f_rstd", [128, 512])
        tmp = [S.sb("f_tmp%d" % i, [128, 512]) for i in range(2)]
        wb = [S.sb("f_wb%d" % i, [128, KC, 512], BF16) for i in range(3)]
        wo = S.sb("f_wo", [128, FC, D], BF16)
        sg = [S.sb("f_sg%d" % i, [128, 512], BF16) for i in range(2)]
        xc = [S.sb("f_xc%d" % i, [128, NT]) for i in range(2)]
        for q4 in range(2):
            f0, f1 = q4 * 11, (q4 + 1) * 11
            P.dma(wo[:, f0:f1, :], V(g.d["w_out_b"].ap[k, :, f0 * D:f1 * D].rearrange("p (f n) -> p f n", n=D), None))
        rstdh = [rstd, S.sb("f_rstd2", [128, 512])]
        nw = 0
        nx = 0
        nsg = 0

        def load_x(tile, h):
            t0, nt, n = tile
            hw = min(512, nt)
            P.dma(xt[:, :, 0:hw], dview(xTd[:, :, t0 + h * hw:t0 + (h + 1) * hw], "c p t -> p c t"))

        def a1(tile, h):
            load_x(tile, h)
            norm_stats(g, sq, rstdh[h], xt, min(512, tile[1]))

        def a2(tile, h):
            t0, nt, n = tile
            hw = min(512, nt)
            load_x(tile, h)
            norm_apply(g, tmp, rstdh[h], xt, hT[:, :, h * hw:(h + 1) * hw], hw, l, i0, n)

        for h in range(tiles[0][1] // min(512, tiles[0][1])):
            a1(tiles[0], h)
        for h in range(tiles[0][1] // min(512, tiles[0][1])):
            a2(tiles[0], h)
        for ti, (t0, nt, n) in enumerate(tiles):
            hw = min(512, nt)
            nh = nt // hw
            nxt = tiles[ti + 1] if ti + 1 < len(tiles) else None
            nnh = (nxt[1] // min(512, nxt[1])) if nxt else 0
            for jp in range(FC // 2):
                w_ = wb[nw % 3]
                nw += 1
                P.dma(w_, V(g.d["w_in_b"].ap[k, jp].rearrange("p (k n) -> p k n", n=512), None))
                if nxt is not None and jp == 6:
                    a1(nxt, 0)
                if nxt is not None and jp == 8 and nnh > 1:
                    a1(nxt, 1)
                for jj in range(2):
                    j = jp * 2 + jj
                    for h in range(nh):
                        bg_ = nbank(g)
                        bu = nbank(g)
                        for kc in range(KC):
                            P.mm(bg_[:, 0:hw], w_[:, kc, jj * 128:(jj + 1) * 128], hT[:, kc, h * hw:(h + 1) * hw],
                                 start=(kc == 0), stop=(kc == KC - 1))
                        for kc in range(KC):
                            P.mm(bu[:, 0:hw], w_[:, kc, 256 + jj * 128:256 + (jj + 1) * 128], hT[:, kc, h * hw:(h + 1) * hw],
                                 start=(kc == 0), stop=(kc == KC - 1))
                        s_ = sg[nsg % 2]
                        nsg += 1
                        P.act(s_[:, 0:hw], bg_[:, 0:hw], AF.Silu)
                        P.tt(actT[:, j, h * hw:(h + 1) * hw], s_[:, 0:hw], bu[:, 0:hw], ALU.mult)
            for h in range(nnh):
                a2(nxt, h)
            for c in range(KC):
                x_ = xc[nx % 2]
                nx += 1
                P.dma(x_[:, 0:nt], V(xTd.ap[c, :, t0:t0 + nt], None))
                for h in range(nh):
                    b = nbank(g)
                    for f in range(FC):
                        P.mm(b[:, 0:hw], wo[:, f, c * 128:(c + 1) * 128], actT[:, f, h * hw:(h + 1) * hw],
                             start=(f == 0), stop=(f == FC - 1))
                    P.stt(x_[:, h * hw:(h + 1) * hw], b[:, 0:hw], mcol(g.modh, l, i0 + 2, c, n), x_[:, h * hw:(h + 1) * hw],
                          ALU.mult, ALU.add)
                P.dma(V(xTd.ap[c, :, t0:t0 + nt], None), x_[:, 0:nt])


def emit_final(g):
    P = g.P
    xTd = g.d["xT"]
    finals = []
    with P.scope() as S:
        fg = S.sb("fgT", [128, KC])
        rows_to_cols(g, S, fg, dview(g.d["final_gain"], "(c p) -> c p", p=128), KC, "fgst")
        xt = [S.sb("o_xt%d" % i, [128, KC, 512]) for i in range(2)]
        sq = [S.sb("o_sq%d" % i, [128, 512], BF16) for i in range(2)]
        rstd = S.sb("o_rstd", [128, 512])
        yt = S.sb("o_yt", [128, KC, 512])
        ot = [S.sb("o_ot%d" % i, [128, 4, D]) for i in range(2)]
        for t in range(g.nlat // 512):
            t0 = LCTX + t * 512
            x_ = xt[t % 2]
            P.dma(x_, dview(xTd[:, :, t0:t0 + 512], "c p t -> p c t"))
            b = nbank(g)
            for c in range(KC):
                s_ = sq[c % 2]
                P.act(s_, x_[:, c, :], AF.Square)
                P.mm(b, g.onesb, s_, start=(c == 0), stop=(c == KC - 1))
            P.act(rstd, b, AF.Sqrt, scale=1.0 / D, bias=g.epsc[:, 0:1])
            P.recip(rstd, rstd)
            for c in range(KC):
                P.stt(yt[:, c, :], x_[:, c, :], fg[:, c:c + 1], rstd, ALU.mult, ALU.mult)
            o_ = ot[t % 2]
            for bl in range(4):
                for c0 in range(0, KC, 4):
                    b = nbank(g)
                    for c in range(c0, c0 + 4):
                        P.tr(b[:, (c - c0) * 128:(c - c0 + 1) * 128], yt[:, c, bl * 128:(bl + 1) * 128], g.ident)
                    P.copy(o_[:, bl, c0 * 128:(c0 + 4) * 128], b, q=("act" if (c0 // 4) % 2 else "dve"))
            finals.append(P.dma(dview(g.d["out"][t * 512:(t + 1) * 512, :], "(b p) f -> p b f", p=128), o_))
    return finals


HD = 64
EPI_OFF = [3, 5]


def rot(g, key, lo, hi):
    c = g.rotc.get(key, 0)
    g.rotc[key] = c + 1
    return g.pb[lo + c % (hi - lo)]


def load_cast(g, S, st_pool, cnt, dst, src, cols, q="pool"):
    P = g.P
    st = st_pool[cnt[0] % len(st_pool)]
    cnt[0] += 1
    P.dma(st[:, :, 0:cols], src)
    P.copy(dst, st[:, :, 0:cols], q=q)
    return st


def emit_attn_proj(g, l):
    P = g.P
    even = (l % 2 == 0)
    e = l // 2
    xTd = g.d["xT"]
    if even:
        nqc, nkv = 4, 2
        wname, wsname = "even_w_in", "even_qk_sw"
        qcols, kcols, vcol0 = 512, 128, 640
    else:
        nqc, nkv = 8, 4
        wname, wsname = "odd_w_in", "odd_qk_sw"
        qcols, kcols, vcol0 = 1024, 256, 1280
    vcols = kcols
    win = dview(g.d[wname][e], "(kc p) n -> p kc n", p=128)
    wsw = dview(g.d[wsname][e], "(kc p) n -> p kc n", p=128)
    with P.scope() as S:
        wq = S.sb("wq", [128, KC, qcols], BF16)
        wqs = S.sb("wqs", [128, KC, qcols], BF16)
        wk2 = S.sb("wk2", [128, KC, nkv, 128], BF16)
        wks2 = S.sb("wks2", [128, KC, nkv, 128], BF16)
        wv = S.sb("wv", [128, KC, vcols], BF16)
        stp = [S.sb("pst%d" % i, [128, KC, 512]) for i in range(2)]
        cnt = [0]
        for s_ in range(qcols // 512):
            load_cast(g, S, stp, cnt, wq[:, :, s_ * 512:(s_ + 1) * 512], win[:, :, s_ * 512:(s_ + 1) * 512], 512)
            load_cast(g, S, stp, cnt, wqs[:, :, s_ * 512:(s_ + 1) * 512], wsw[:, :, s_ * 512:(s_ + 1) * 512], 512)
        for (dst2, srcw) in ((wk2, win), (wks2, wsw)):
            st = stp[cnt[0] % 2]
            cnt[0] += 1
            P.dma(st[:, :, 0:kcols], srcw[:, :, qcols:qcols + kcols])
            sv = st[:, :, 0:kcols].re("p k (g d) -> p k g d", d=64)
            P.copy(dst2[:, :, :, 0:64], sv, q="pool")
            P.copy(dst2[:, :, :, 64:128], sv, q="pool")
        load_cast(g, S, stp, cnt, wv, win[:, :, vcol0:vcol0 + vcols], vcols)
        if even:
            qg = S.sb("qg", [128, 4])
            for j, nm in enumerate(["q_gain", "q_gain_sw", "k_gain", "k_gain_sw"]):
                for hh in range(2):
                    P.dma(qg[hh * 64:(hh + 1) * 64, j:j + 1], dview(g.d[nm][e], "(d o) -> d o", o=1))
            blk = S.sb("blk", [128, 128], BF16)
            P.memset(blk, 0.0)
            P.memset(blk[0:64, 0:64], 1.0)
            P.memset(blk[64:128, 64:128], 1.0)
            sqb = [S.sb("sqb%d" % i, [128, 512], BF16) for i in range(2)]
            rs = [S.sb("rs%d" % i, [128, 512]) for i in range(2)]
        xt = S.sb("p_xt", [128, KC, 512])
        hT = S.sb("p_hT", [128, KC, 512], BF16)
        sq = [S.sb("p_sq%d" % i, [128, 512], BF16) for i in range(2)]
        rstd = S.sb("p_rstd", [128, 512])
        tmp = [S.sb("p_tmp%d" % i, [128, 512]) for i in range(2)]
        cs = S.sb("p_cos", [128, 512])
        sn = S.sb("p_sin", [128, 512])
        t1 = [S.sb("p_t1%d" % i, [128, 512]) for i in range(2)]
        t2 = [S.sb("p_t2%d" % i, [128, 512]) for i in range(2)]
        qTt = S.sb("p_qTt", [128, nqc, 512], BF16)
        kTt = S.sb("p_kTt", [128, nkv, 512], BF16)
        vt = S.sb("p_vt", [128, 4, nkv, 192], BF16)
        P.memset(vt, 1.0)
        if even:
            g.wrw = S.sb("wrw", [128, KC, 1920], BF16)
            wrw_d = dview(g.d["even_w_in"][e], "(kc p) n -> p kc n", p=128)
            for s_ in range(4):
                c0 = 768 + s_ * 512
                cw = min(512, 2688 - c0)
                load_cast(g, S, stp, cnt, g.wrw[:, :, s_ * 512:s_ * 512 + cw], wrw_d[:, :, c0:c0 + cw], cw)
            g.rwfo = [S.sb("rwfo%d" % i, [128, 512]) for i in range(2)]
        tiles = [(0, LCTX, 1, None)] + [(LCTX + t * 512, 512, 0, t * 512) for t in range(g.nlat // 512)]
        nr = 0
        for (t0, nt, n, a0) in tiles:
            P.dma(xt[:, :, 0:nt], dview(xTd[:, :, t0:t0 + nt], "c p t -> p c t"))
            emit_norm_mod(g, S, (sq, rstd, tmp), xt, hT, nt, l, 3, n)
            lat = a0 is not None
            if lat:
                P.dma(cs, g.d["cosT"][:, a0:a0 + 512])
                P.dma(sn, g.d["sinT"][:, a0:a0 + 512])
            jobs = [(wq[:, :, c * 128:(c + 1) * 128], wqs[:, :, c * 128:(c + 1) * 128], qTt[:, c, 0:nt], 0) for c in range(nqc)]
            jobs += [(wk2[:, :, c, :], wks2[:, :, c, :], kTt[:, c, 0:nt], 2) for c in range(nkv)]
            for (wa, wb_, dst, gi) in jobs:
                ba = rot(g, "pj", 0, 8)
                for kc in range(KC):
                    P.mm(ba[:, 0:nt], wa[:, kc, :], hT[:, kc, 0:nt], start=(kc == 0), stop=(kc == KC - 1))
                if lat:
                    bb = rot(g, "pj", 0, 8)
                    for kc in range(KC):
                        P.mm(bb[:, 0:nt], wb_[:, kc, :], hT[:, kc, 0:nt], start=(kc == 0), stop=(kc == KC - 1))
                if even:
                    s_ = sqb[nr % 2]
                    r_ = rs[nr % 2]
                    P.act(s_[:, 0:nt], ba[:, 0:nt], AF.Square)
                    bn = rot(g, "pj", 0, 8)
                    P.mm(bn[:, 0:nt], blk, s_[:, 0:nt])
                    P.act(r_[:, 0:nt], bn[:, 0:nt], AF.Sqrt, scale=1.0 / HD, bias=g.epsc[:, 0:1])
                    P.recip(r_[:, 0:nt], r_[:, 0:nt])
                a_ = t1[nr % 2]
                b_ = t2[nr % 2]
                nr += 1
                if even:
                    if lat:
                        P.stt(a_[:, 0:nt], ba[:, 0:nt], qg[:, gi:gi + 1], r_[:, 0:nt], ALU.mult, ALU.mult)
                        P.stt(b_[:, 0:nt], bb[:, 0:nt], qg[:, gi + 1:gi + 2], r_[:, 0:nt], ALU.mult, ALU.mult)
                        P.tt(a_[:, 0:nt], a_[:, 0:nt], cs[:, 0:nt], ALU.mult, q="pool")
                        P.tt(b_[:, 0:nt], b_[:, 0:nt], sn[:, 0:nt], ALU.mult, q="pool")
                        P.tt(dst, a_[:, 0:nt], b_[:, 0:nt], ALU.add, q="pool")
                    else:
                        P.stt(dst, ba[:, 0:nt], qg[:, gi:gi + 1], r_[:, 0:nt], ALU.mult, ALU.mult)
                else:
                    if lat:
                        P.tt(a_[:, 0:nt], ba[:, 0:nt], cs[:, 0:nt], ALU.mult)
                        P.tt(b_[:, 0:nt], bb[:, 0:nt], sn[:, 0:nt], ALU.mult)
                        P.tt(dst, a_[:, 0:nt], b_[:, 0:nt], ALU.add, q="pool")
                    else:
                        P.copy(dst, ba[:, 0:nt], q="act")
            for tb in range(nt // 128):
                bv = rot(g, "pj", 0, 8)
                for kc in range(KC):
                    P.mm(bv[:, 0:vcols], hT[:, kc, tb * 128:(tb + 1) * 128], wv[:, kc, :], start=(kc == 0), stop=(kc == KC - 1))
                P.copy(vt[:, tb, :, 64:128], bv[:, 0:vcols].re("p (g d) -> p g d", d=64), q="act")
            P.dma(dview(g.d["qT"][0:nqc, :, t0:t0 + nt], "c p t -> p c t"), qTt[:, :, 0:nt])
            P.dma(dview(g.d["kT2"][0:nkv, :, t0:t0 + nt], "c p t -> p c t"), kTt[:, :, 0:nt])
            P.dma(dview(g.d["Vd"][t0 // 128:(t0 + nt) // 128, :, 0:nkv * 192], "b p f -> p b f"),
                  vt[:, 0:nt // 128].re("p b g f -> p b (g f)"))
            if even:
                emit_rwkv_proj(g, S, l, hT, t0, nt)


def emit_rwkv_proj(g, S, l, hT, t0, nt):
    P = g.P
    for c in range(15):
        b = rot(g, "pj", 0, 8)
        for kc in range(KC):
            P.mm(b[:, 0:nt], g.wrw[:, kc, c * 128:(c + 1) * 128], hT[:, kc, 0:nt], start=(kc == 0), stop=(kc == KC - 1))
        fo = g.rwfo[c % 2]
        P.copy(fo[:, 0:nt], b[:, 0:nt], q=("act" if c % 2 else "dve"))
        P.dma(V(g.d["fT"].ap[c, :, t0:t0 + nt], None), fo[:, 0:nt])


def emit_attn(g, l):
    P = g.P
    even = (l % 2 == 0)
    o = l // 2
    ctx_out = l < g.depth - 1
    if even:
        nqc, nkv = 4, 2
    else:
        nqc, nkv = 8, 4
    NB = g.ntok // 128
    with P.scope() as S:
        kT2 = S.sb("a_kT2", [128, nkv, g.ntok], BF16)
        Va = S.sb("a_V", [128, NB, nkv, 192], BF16)
        for c in range(nkv):
            P.dma(kT2[:, c, :], V(g.d["kT2"].ap[c], None))
        P.dma(Va.re("p b g f -> p b (g f)"), dview(g.d["Vd"][:, :, 0:nkv * 192], "b p f -> p b f"))
        swapM = S.sb("a_swap", [128, 128])
        P.copy(swapM[:, 0:64], g.ident[:, 64:128])
        P.copy(swapM[:, 64:128], g.ident[:, 0:64])
        if not even:
            masks = S.sb("a_masks", [128, 6, 512], BF16)
            P.memset(masks, 1.0, q="pool")
            for r in range(-1, 5):
                ma = masks[:, r + 1, :].ap
                P.op("pool", lambda e, ma=ma, r=r: e.affine_select(ma, ma, [[1, 512]], ALU.is_ge, 0.0,
                                                                   base=-r * 128 + 128, channel_multiplier=-1), [masks], [masks])
                P.op("pool", lambda e, ma=ma, r=r: e.affine_select(ma, ma, [[-1, 512]], ALU.is_ge, 0.0,
                                                                   base=r * 128 + 128, channel_multiplier=1), [masks], [masks])
            eS = S.sb("a_sk", [128, 16])
            P.dma(eS, V(g.d["sink"].ap[o].rearrange("(o h) -> o h", o=1).broadcast_to([128, 16]), None))
            P.act(eS, eS, AF.Exp)
            padi = S.sb("a_padi", [128, 128], mybir.dt.int32)
            pia = padi.ap
            P.op("pool", lambda e: e.iota(pia, [[1, 128]], base=1, channel_multiplier=0), [], [padi])
            padcnt = S.sb("a_padcnt", [128, 128])
            P.copy(padcnt, padi)
        qz = [[S.sb("a_qz%d%d" % (hh, i), [128, nqc, 512], BF16) for i in range(2)] for hh in range(2)]
        for hh in range(2):
            for i in range(2):
                P.memset(qz[hh][i][(1 - hh) * 64:(2 - hh) * 64], 0.0)
        oT = [S.sb("a_oT%d" % i, [128, nqc, 512], BF16) for i in range(2)]
        pT = [S.sb("a_pT%d" % i, [128, 512], BF16) for i in range(8)]
        den = [S.sb("a_den%d" % i, [128, 512]) for i in range(3)]
        rshb = [S.sb("a_rsh%d" % i, [128, 512]) for i in range(3)]
        for i in range(3):
            P.memset(den[i], 1.0)
        tiles = []
        if ctx_out:
            tiles.append((0, LCTX, None))
        tiles += [(LCTX + t * 512, 512, t) for t in range(g.nlat // 512)]
        npT = 0
        nep = 0
        def load_q(ti_):
            t0_, nt_, _ = tiles[ti_]
            for hh_ in range(2):
                P.dma(qz[hh_][ti_ % 2][hh_ * 64:(hh_ + 1) * 64, :, 0:nt_],
                      dview(g.d["qT"][0:nqc, hh_ * 64:(hh_ + 1) * 64, t0_:t0_ + nt_], "c p t -> p c t"))
        load_q(0)
        deferred = []
        for ti, (t0, nt, tq) in enumerate(tiles):
            o_ = oT[ti % 2]
            qq = [qz[0][ti % 2], qz[1][ti % 2]]
            chunks = [(0, 0, nt, None), (1, 0, nt, None)]
            if tq is not None:
                if even:
                    chunks += [(2 + kb, 0, nt, None) for kb in range(g.nlat // 128)]
                else:
                    for r in range(-1, 5):
                        kb = tq * 4 + r
                        if 1 <= kb < g.nlat // 128:
                            chunks.append((2 + kb, max(0, r * 128 - 128), min(512, r * 128 + 256), r + 1))
            items = [(hc, ci, hh) for hc in range(nqc) for ci in range(len(chunks)) for hh in range(2)]
            LA = 4
            pbuf = {}

            def stage_a(idx):
                nonlocal npT
                hc, ci, hh = items[idx]
                kb, c0, c1, mi = chunks[ci]
                gk = hc // 2
                bS = rot(g, "at", 4, 7)
                P.mm(bS[:, c0:c1], kT2[:, gk, kb * 128:(kb + 1) * 128], qq[hh][:, hc, c0:c1])
                p_ = pT[npT % 8]
                npT += 1
                P.act(p_[:, c0:c1], bS[:, c0:c1], AF.Exp, scale=0.125)
                if mi is not None:
                    P.tt(p_[:, c0:c1], p_[:, c0:c1], masks[:, mi, c0:c1], ALU.mult)
                pbuf[idx] = p_

            def stage_b(idx):
                nonlocal nep
                hc, ci, hh = items[idx]
                kb, c0, c1, mi = chunks[ci]
                gk = hc // 2
                bo = g.pb[(hc % 2) * 2 + hh]
                p_ = pbuf.pop(idx)
                vsl = Va[:, kb, gk, 64:192] if hh == 0 else Va[:, kb, gk, 0:128]
                P.mm(bo[:, c0:c1], vsl, p_[:, c0:c1], start=(ci == 0), stop=(ci == len(chunks) - 1))
                if ci == len(chunks) - 1:
                    sr, orow = (1 - hh) * 64, hh * 64
                    h = 2 * hc + hh
                    d_ = den[nep % 3]
                    r_ = rshb[nep % 3]
                    nep += 1
                    if even:
                        P.recip(d_[sr:sr + 64, 0:nt], bo[sr:sr + 64, 0:nt])
                    else:
                        P.ts(d_[sr:sr + 64, 0:nt], bo[sr:sr + 64, 0:nt], eS[sr:sr + 64, h:h + 1], ALU.add)
                        if tq == g.nlat // 512 - 1:
                            P.tt(d_[sr:sr + 64, 384:512], d_[sr:sr + 64, 384:512], padcnt[sr:sr + 64, :], ALU.add)
                        P.recip(d_[sr:sr + 64, 0:nt], d_[sr:sr + 64, 0:nt])
                    st_ = {}

                    def e2a(d_=d_, st_=st_, nt=nt):
                        st_["b"] = g.pb[7]
                        P.mm(st_["b"][:, 0:nt], swapM, d_[:, 0:nt])

                    def e2b(r_=r_, st_=st_, orow=orow, nt=nt):
                        P.copy(r_[orow:orow + 64, 0:nt], st_["b"][orow:orow + 64, 0:nt], q="act")

                    def e3(r_=r_, bo=bo, o_=o_, orow=orow, hc=hc, nt=nt):
                        P.tt(o_[orow:orow + 64, hc, 0:nt], bo[orow:orow + 64, 0:nt], r_[orow:orow + 64, 0:nt], ALU.mult)
                    off = EPI_OFF[hh] if len(chunks) >= 6 else 0
                    deferred.append([off, e2a])
                    deferred.append([off + 2, e2b])
                    deferred.append([off + 6, e3])

            def run_deferred(flush=False):
                keep = []
                for it in deferred:
                    it[0] -= 1
                    if it[0] <= 0 or flush:
                        it[1]()
                    else:
                        keep.append(it)
                deferred[:] = keep

            for idx in range(len(items) + LA):
                g.bg.tick(8)
                if idx < len(items):
                    stage_a(idx)
                if idx >= LA:
                    stage_b(idx - LA)
                run_deferred()
                if idx == len(items) // 2 and ti + 1 < len(tiles):
                    load_q(ti + 1)
            while deferred:
                run_deferred(flush=True)
            P.dma(dview(g.d["oT"][0:nqc, :, t0:t0 + nt], "c p t -> p c t"), o_[:, :, 0:nt])


GN_EPS = 64e-5
CW = 128


def emit_rwkv(g, l):
    P = g.P
    e = l // 2
    NT = g.ntok
    NCH = NT // CW
    RW = g.rwdt
    d = g.d
    with P.scope() as S0:
        gC = S0.sb("rw_gC", [128, 2, 4, NCH])
        rwkv_features(g, l, S0, gC)
        with P.scope() as S:
            rwkv_scan(g, l, S, gC)


def rwkv_features(g, l, S0, gC):
    P = g.P
    e = l // 2
    NT = g.ntok
    RW = g.rwdt
    d = g.d
    with P.scope() as S:
        NR = 30 + 8 + 8 + 4 + 4 + 4
        prow = S.sb("rw_prow", [NR, 128])
        P.dma(prow[0:30, :], dview(d["rwkv_mu"][e], "d (c p) -> (d c) p", p=128))
        P.dma(prow[30:38, :], dview(d["rwkv_w0"][e], "d (c p) -> (d c) p", p=128))
        P.dma(prow[38:46, :], dview(d["rwkv_a0"][e], "d (c p) -> (d c) p", p=128))
        P.dma(prow[46:50, :], dview(d["rwkv_k_k"][e], "(c p) -> c p", p=128))
        P.dma(prow[50:54, :], dview(d["rwkv_k_a"][e], "(c p) -> c p", p=128))
        P.dma(prow[54:58, :], V(d["rwkv_r_k"].ap[e].rearrange("h dk -> (h dk)").rearrange("(c p) -> c p", p=128), None))
        prm = S.sb("rw_prm", [128, NR])
        b = rot(g, "pj", 0, 8)
        P.tr(b[:, 0:NR], prow, g.ident[0:NR, 0:NR])
        P.copy(prm, b[:, 0:NR])
        mu0 = lambda c: prm[:, c:c + 1]
        mu1 = lambda c: prm[:, 15 + c:16 + c]
        w0c = lambda dd, j: prm[:, 30 + dd * 4 + j:31 + dd * 4 + j]
        a0c = lambda dd, j: prm[:, 38 + dd * 4 + j:39 + dd * 4 + j]
        kkc = lambda j: prm[:, 46 + j:47 + j]
        kac = lambda j: prm[:, 50 + j:51 + j]
        rkc = lambda j: prm[:, 54 + j:55 + j]
        mc = S.sb("rw_mc", [128, 15])
        P.tt(mc, prm[:, 0:15], prm[:, 15:30], ALU.add)
        P.ts(mc, mc, -1.0, ALU.mult, 1.0, ALU.add)
        omka = S.sb("rw_omka", [128, 4])
        P.ts(omka, prm[:, 50:54], -1.0, ALU.mult, 1.0, ALU.add)
        w2T = S.sb("rw_w2T", [128, 512])
        a2T = S.sb("rw_a2T", [128, 512])
        g2s = S.sb("rw_g2s", [128, 512])
        P.dma(w2T, dview(d["rwkv_w2"][e], "d r c -> (d r) c"))
        P.dma(a2T, dview(d["rwkv_a2"][e], "d r c -> (d r) c"))
        P.dma(g2s, d["rwkv_g2"][e])
        blkf = S.sb("rw_blkf", [128, 128])
        P.memset(blkf, 0.0)
        P.memset(blkf[0:64, 0:64], 1.0)
        P.memset(blkf[64:128, 64:128], 1.0)
        rst = S.sb("rw_rst", [128, 512])
        P.memset(rst, 1.0)
        P.memset(rst.re("p (c i) -> p c i", i=CW)[:, :, 0:1], 0.0)
        zer = S.sb("rw_zer", [128, 512])
        P.memset(zer, 0.0)
        fb = S.sb("rw_fb", [128, 15, 514])
        fs = S.sb("rw_fs", [128, 15, 512])
        nT = [0]
        tpool = [S.sb("rw_t%d" % i, [128, 512]) for i in range(14)]

        def tmp():
            t = tpool[nT[0] % len(tpool)]
            nT[0] += 1
            return t
        obuf = [S.sb("rw_ob%d" % i, [128, 512], RW) for i in range(4)]
        nO = [0]

        def ob():
            t = obuf[nO[0] % len(obuf)]
            nO[0] += 1
            return t
        g.rw_vt = S.sb("rw_vt", [128, 4, 512], RW)
        g.rw_kt = [S.sb("rw_kt%d" % i, [128, 4, 512], RW) for i in range(2)]
        g.rw_bt = [S.sb("rw_bt%d" % i, [128, 4, 512], RW) for i in range(2)]
        twl = S.sb("rw_twl", [128, 512])
        sgl = S.sb("rw_sgl", [128, 512])
        kk = S.sb("rw_kk", [128, 512])
        gob = [S.sb("rw_go%d" % i, [128, 512], BF16) for i in range(2)]
        identR = g.ident
        if RW != F32:
            identR = S.sb("rw_identb", [128, 128], RW)
            P.copy(identR, g.ident)
        tiles = [(0, LCTX, 0, LCTX)] + [(LCTX + t * 512, 512, LCTX, NT) for t in range(g.nlat // 512)]
        for (t0, nt, s0, s1) in tiles:
            nch = nt // CW
            ch0 = t0 // CW
            lo = 1 if t0 == s0 else 0
            hi = 1 if t0 + nt == s1 else 0
            if lo:
                P.memset(fb[:, :, 0:1], 0.0)
            if hi:
                P.memset(fb[:, :, nt + 1:nt + 2], 0.0)
            P.dma(fb[:, :, lo:nt + 2 - hi], dview(d["fT"][:, :, t0 - 1 + lo:t0 + nt + 1 - hi], "c p t -> p c t"))
            for c in range(15):
                t_ = tmp()
                P.ts(t_[:, 0:nt], fb[:, c, 1:nt + 1], mc[:, c:c + 1], ALU.mult)
                P.stt(t_[:, 0:nt], fb[:, c, 0:nt], mu0(c), t_[:, 0:nt], ALU.mult, ALU.add)
                P.stt(fs[:, c, 0:nt], fb[:, c, 2:nt + 2], mu1(c), t_[:, 0:nt], ALU.mult, ALU.add)
            R_ = lambda j: fs[:, j, 0:nt]
            K_ = lambda j: fs[:, 4 + j, 0:nt]
            V_ = lambda j: fs[:, 8 + j, 0:nt]
            P.act(twl[:, 0:nt], fs[:, 12, 0:nt], AF.Tanh)
            P.act(sgl[:, 0:nt], fs[:, 14, 0:nt], AF.Sigmoid)
            for j in range(4):
                kkr = tmp()
                P.ts(kkr[:, 0:nt], K_(j), kkc(j), ALU.mult)
                sq = tmp()
                P.act(sq[:, 0:nt], kkr[:, 0:nt], AF.Square)
                b = rot(g, "pj", 0, 8)
                P.mm(b[:, 0:nt], blkf, sq[:, 0:nt])
                rn = tmp()
                P.act(rn[:, 0:nt], b[:, 0:nt], AF.Sqrt, bias=g.epsc[:, 0:1])
                P.recip(rn[:, 0:nt], rn[:, 0:nt])
                P.tt(kk[:, 0:nt], kkr[:, 0:nt], rn[:, 0:nt], ALU.mult)
                b = rot(g, "pj", 0, 8)
                P.mm(b[:, 0:nt], g2s[:, j * 128:(j + 1) * 128], sgl[:, 0:nt])
                go = gob[j % 2]
                P.copy(go[:, 0:nt], b[:, 0:nt], q="act")
                P.dma(V(d["gT"].ap[j, :, t0:t0 + nt], None), go[:, 0:nt])
                rk = tmp()
                P.stt(rk[:, 0:nt], R_(j), rkc(j), K_(j), ALU.mult, ALU.mult)
                b = rot(g, "pj", 0, 8)
                P.mm(b[:, 0:nt], blkf, rk[:, 0:nt])
                bon = tmp()
                P.tt(bon[:, 0:nt], b[:, 0:nt], V_(j), ALU.mult)
                P.dma(V(d["bonT"].ap[j, :, t0:t0 + nt], None), bon[:, 0:nt])
                if RW != F32:
                    vr = ob()
                    P.copy(vr[:, 0:nt], V_(j), q="pool")
                    vsrc = vr
                else:
                    vsrc = fs[:, 8 + j, :]
                b = rot(g, "pj", 0, 8)
                bR = b.cast(RW) if RW != F32 else b
                for cb in range(nch):
                    P.tr(bR[:, cb * 128:(cb + 1) * 128], vsrc[:, cb * 128:(cb + 1) * 128], identR)
                vtile = g.rw_vt
                P.copy(vtile[:, 0:nch, j * 128:(j + 1) * 128], bR[:, 0:nt].re("p (c f) -> p c f", f=128), q="act")
                for dd in range(2):
                    p0 = dd * 64
                    b = rot(g, "pj", 0, 8)
                    P.mm(b[:, 0:nt], w2T[p0:p0 + 64, j * 128:(j + 1) * 128], twl[p0:p0 + 64, 0:nt])
                    lw = tmp()
                    P.act(lw[:, 0:nt], b[:, 0:nt], AF.Sigmoid, bias=w0c(dd, j))
                    P.ts(lw[:, 0:nt], lw[:, 0:nt], -0.6065306597126334, ALU.mult)
                    b = rot(g, "pj", 0, 8)
                    P.mm(b[:, 0:nt], a2T[p0:p0 + 64, j * 128:(j + 1) * 128], fs[p0:p0 + 64, 13, 0:nt])
                    a_ = tmp()
                    P.act(a_[:, 0:nt], b[:, 0:nt], AF.Sigmoid, bias=a0c(dd, j))
                    pf = tmp()
                    P.scan(pf[:, 0:nt], rst[:, 0:nt], lw[:, 0:nt], 0.0, ALU.mult, ALU.add)
                    if dd == 1:
                        pb = tmp()
                        P.tt(pb[:, 0:nt], lw[:, 0:nt], pf[:, 0:nt], ALU.subtract)
                        tot = pf[:, 0:nt].re("p (c i) -> p c i", i=CW)[:, :, CW - 1:CW].bc([128, nch, CW])
                        P.tt(pb[:, 0:nt].re("p (c i) -> p c i", i=CW), pb[:, 0:nt].re("p (c i) -> p c i", i=CW), tot, ALU.add)
                        pp = pb
                    else:
                        pp = pf
                    ep = tmp()
                    P.act(ep[:, 0:nt], pp[:, 0:nt], AF.Exp)
                    en = tmp()
                    P.act(en[:, 0:nt], pp[:, 0:nt], AF.Exp, scale=-1.0)
                    pm = tmp()
                    P.tt(pm[:, 0:nt], pp[:, 0:nt], lw[:, 0:nt], ALU.subtract)
                    P.act(pm[:, 0:nt], pm[:, 0:nt], AF.Exp)
                    epv = ep[:, 0:nt].re("p (c i) -> p c i", i=CW)
                    idx = CW - 1 if dd == 0 else 0
                    P.copy(gC[:, dd, j, ch0:ch0 + nch], epv[:, :, idx], q="pool")
                    kkt = ob()
                    P.tt(kkt[:, 0:nt], kk[:, 0:nt], pm[:, 0:nt], ALU.mult)
                    rt = ob()
                    P.tt(rt[:, 0:nt], R_(j), ep[:, 0:nt], ALU.mult)
                    qd = V(d["QR"].ap[dd, j, :, ch0 * 2 * CW:(ch0 + nch) * 2 * CW].rearrange("p (c w i) -> p c w i", w=2, i=CW), None)
                    P.dma(qd[:, :, 0, :], kkt[:, 0:nt].re("p (c i) -> p c i", i=CW))
                    P.dma(qd[:, :, 1, :], rt[:, 0:nt].re("p (c i) -> p c i", i=CW))
                    kd = tmp()
                    P.ts(kd[:, 0:nt], a_[:, 0:nt], kac(j), ALU.mult, omka[:, j:j + 1], ALU.add)
                    P.tt(kd[:, 0:nt], kd[:, 0:nt], K_(j), ALU.mult)
                    kh = ob()
                    P.tt(kh[:, 0:nt], kd[:, 0:nt], en[:, 0:nt], ALU.mult)
                    bd = tmp()
                    P.tt(bd[:, 0:nt], kk[:, 0:nt], a_[:, 0:nt], ALU.mult)
                    bh = ob()
                    P.tt(bh[:, 0:nt], bd[:, 0:nt], en[:, 0:nt], ALU.mult)
                    P.dma(V(d["KH"].ap[dd, j, :, t0:t0 + nt], None), kh[:, 0:nt])
                    P.dma(V(d["BH"].ap[dd, j, :, t0:t0 + nt], None), bh[:, 0:nt])
                    for (src, dst) in ((kh, g.rw_kt[dd]), (bh, g.rw_bt[dd])):
                        b = rot(g, "pj", 0, 8)
                        bR = b.cast(RW) if RW != F32 else b
                        for cb in range(nch):
                            P.tr(bR[:, cb * 128:(cb + 1) * 128], src[:, cb * 128:(cb + 1) * 128], identR)
                        P.copy(dst[:, 0:nch, j * 128:(j + 1) * 128], bR[:, 0:nt].re("p (c f) -> p c f", f=128), q="act")
            P.dma(dview(d["Vt"][ch0:ch0 + nch], "c p f -> p c f"), g.rw_vt[:, 0:nch, :])
            for dd in range(2):
                P.dma(dview(d["KHt"][dd, ch0:ch0 + nch], "c p f -> p c f"), g.rw_kt[dd][:, 0:nch, :])
                P.dma(dview(d["BHt"][dd, ch0:ch0 + nch], "c p f -> p c f"), g.rw_bt[dd][:, 0:nch, :])


def rwkv_scan(g, l, S, gC):
    P = g.P
    NT = g.ntok
    NCH = NT // CW
    RW = g.rwdt
    d = g.d
    mskA = [S.sb("rw_mA%d" % i, [128, 256]) for i in range(2)]
    mskN = [S.sb("rw_mN%d" % i, [128, 128]) for i in range(2)]

    def tri(v, step, cm, op):
        a = v.ap
        P.memset(v, 1.0, q="pool")
        P.op("pool", lambda e_: e_.affine_select(a, a, [[step, 128]], op, 0.0, base=0, channel_multiplier=cm), [v], [v])
    tri(mskA[0][:, 0:128], 1, -1, ALU.is_gt)
    tri(mskA[0][:, 128:256], 1, -1, ALU.is_ge)
    tri(mskN[0], -1, 1, ALU.is_gt)
    tri(mskA[1][:, 0:128], -1, 1, ALU.is_gt)
    tri(mskA[1][:, 128:256], -1, 1, ALU.is_ge)
    tri(mskN[1], 1, -1, ALU.is_gt)
    for dd in range(2):
        P.ts(mskN[dd], mskN[dd], -1.0, ALU.mult)
    Hs = [[S.sb("rw_H%d%d" % (dd, i), [128, 4, 64], RW) for i in range(2)] for dd in range(2)]
    for dd in range(2):
        P.memset(Hs[dd][0], 0.0)
    QRb = [S.sb("rw_QRb%d" % dd, [128, 4, 256], RW) for dd in range(2)]
    KHb = [S.sb("rw_KHb%d" % dd, [128, 4, 128], RW) for dd in range(2)]
    BHb = [S.sb("rw_BHb%d" % dd, [128, 4, 128], RW) for dd in range(2)]
    KHtb = [S.sb("rw_KHtb%d" % dd, [128, 512], RW) for dd in range(2)]
    BHtb = [S.sb("rw_BHtb%d" % dd, [128, 512], RW) for dd in range(2)]
    Vtb = [S.sb("rw_Vtb%d" % dd, [128, 512], RW) for dd in range(2)]
    A1s = [S.sb("rw_A1s%d" % dd, [128, 8, 256], RW) for dd in range(2)]
    A2s = [S.sb("rw_A2s%d" % dd, [128, 8, 256], RW) for dd in range(2)]
    ZR = [[S.sb("rw_ZR%d%d" % (dd, i), [128, 8, 256], RW) for i in range(2)] for dd in range(2)]
    Ys = [[S.sb("rw_Y%d%d" % (dd, i), [128, 8, 128], RW) for i in range(2)] for dd in range(2)]
    Rfin = [None, None]
    Wsb = [S.sb("rw_W%d" % dd, [128, 8, 64], RW) for dd in range(2)]
    Un = [S.sb("rw_Un%d" % dd, [128, 8, 64], RW) for dd in range(2)]
    yo = [S.sb("rw_yo%d" % dd, [128, 4, 128]) for dd in range(2)]
    identb = g.ident.re("p (o f) -> p o f", o=1).bc([128, 4, 128])
    nctx = LCTX // CW
    order = [list(range(NCH)), list(range(nctx - 1, -1, -1)) + list(range(NCH - 1, nctx - 1, -1))]
    ev = [0]

    fr = False

    def rr(v):
        return v.cast(F32R) if fr else v

    def evac(dst, src):
        P.copy(dst, src, q=("act" if ev[0] % 2 else "dve"))
        ev[0] += 1

    def prod8(dd, lhs, rhs, evac_fn):
        for half in range(2):
            b = rot(g, "rw", 0, 8)
            bv = b.re("p (h f) -> p h f", f=128)
            for hh in range(4):
                h = half * 4 + hh
                P.mm(bv[:, hh, :], rr(lhs[:, h, :]), rr(rhs[:, h, :]))
            evac_fn(half, bv)

    for step in range(NCH):
        cur = step % 2
        nxt = 1 - cur
        cs = [order[0][step], order[1][step]]
        for dd in range(2):
            c = cs[dd]
            P.dma(QRb[dd], dview(d["QR"][dd, :, :, c * 256:(c + 1) * 256], "j p f -> p j f"))
            P.dma(KHb[dd], dview(d["KH"][dd, :, :, c * CW:(c + 1) * CW], "j p f -> p j f"))
            P.dma(BHb[dd], dview(d["BH"][dd, :, :, c * CW:(c + 1) * CW], "j p f -> p j f"))
            P.dma(KHtb[dd], V(d["KHt"].ap[dd, c], None))
            P.dma(BHtb[dd], V(d["BHt"].ap[dd, c], None))
            P.dma(Vtb[dd], V(d["Vt"].ap[c], None))
        for dd in range(2):
            for j in range(4):
                b1 = rot(g, "rw", 0, 8)
                b2 = rot(g, "rw", 0, 8)
                b3 = rot(g, "rw", 0, 8)
                b1v = b1.re("p (h f) -> p h f", f=256)
                b2v = b2.re("p (h f) -> p h f", f=256)
                b3v = b3[:, 0:256].re("p (h f) -> p h f", f=128)
                for hp in range(2):
                    p0 = hp * 64
                    P.mm(b1v[:, hp, :], KHb[dd][p0:p0 + 64, j, :], QRb[dd][p0:p0 + 64, j, :])
                    P.mm(b2v[:, hp, :], BHb[dd][p0:p0 + 64, j, :], QRb[dd][p0:p0 + 64, j, :])
                    P.mm(b3v[:, hp, :], QRb[dd][p0:p0 + 64, j, 0:128], BHb[dd][p0:p0 + 64, j, :])
                mA = mskA[dd].re("p (o f) -> p o f", o=1).bc([128, 2, 256])
                mN = mskN[dd].re("p (o f) -> p o f", o=1).bc([128, 2, 128])
                P.tt(A1s[dd][:, 2 * j:2 * j + 2, :], b1v, mA, ALU.mult)
                P.tt(A2s[dd][:, 2 * j:2 * j + 2, :], b2v, mA, ALU.mult)
                P.tt(rr(Ys[dd][0][:, 2 * j:2 * j + 2, :]), b3v, mN, ALU.mult)
            P.ts(ZR[dd][0][:, :, 0:128], A2s[dd][:, :, 0:128], -1.0, ALU.mult)
            for half in range(2):
                P.copy(ZR[dd][0][:, half * 4:half * 4 + 4, 128:256], identb, q="pool")
        nlev = 7
        for lev in range(nlev):
            a, bn = lev % 2, (lev + 1) % 2
            last = (lev == nlev - 1)
            for dd in range(2):
                zr, zn, yz = ZR[dd][a], ZR[dd][bn], Ys[dd][a]
                for hp2 in range(4):
                    b = rot(g, "rw", 0, 8)
                    bv = b.re("p (h f) -> p h f", f=256)
                    for hh in range(2):
                        h = hp2 * 2 + hh
                        if last:
                            P.mm(bv[:, hh, 128:256], yz[:, h, :], zr[:, h, 128:256])
                        else:
                            P.mm(bv[:, hh, :], yz[:, h, :], zr[:, h, :])
                    h0 = hp2 * 2
                    if not last:
                        evac(zn[:, h0:h0 + 2, 0:128], bv[:, :, 0:128])
                    P.tt(zn[:, h0:h0 + 2, 128:256], bv[:, :, 128:256], zr[:, h0:h0 + 2, 128:256], ALU.add)
            if not last:
                for dd in range(2):
                    zn, yn = ZR[dd][bn], Ys[dd][bn]
                    for half in range(2):
                        b = rot(g, "rw", 0, 8)
                        bv = b.re("p (h f) -> p h f", f=128)
                        for hh in range(4):
                            h = half * 4 + hh
                            P.tr(bv[:, hh, :], zn[:, h, 0:128], g.ident)
                        evac(yn[:, half * 4:half * 4 + 4, :], bv)
        for dd in range(2):
            Rfin[dd] = ZR[dd][nlev % 2]
        for dd in range(2):
            H0 = Hs[dd][cur]
            bW = rot(g, "rw", 0, 8)
            bWv = bW.re("p (h f) -> p h f", f=64)
            for h in range(8):
                j, p0 = h // 2, (h % 2) * 64
                P.mm(bWv[:, h, :], A1s[dd][:, h, 0:128], Vtb[dd][:, h * 64:(h + 1) * 64], start=True, stop=False)
                P.mm(bWv[:, h, :], QRb[dd][p0:p0 + 64, j, 0:128], H0[p0:p0 + 64, j, :], start=False, stop=True)
            evac(rr(Wsb[dd]), bWv)
        for dd in range(2):
            bU = rot(g, "rw", 0, 8)
            bUv = bU.re("p (h f) -> p h f", f=64)
            for h in range(8):
                P.mm(bUv[:, h, :], Rfin[dd][:, h, 128:256], Wsb[dd][:, h, :])
            P.act(Un[dd], bUv, AF.Copy, scale=-1.0)
        for dd in range(2):
            c = cs[dd]
            H0 = Hs[dd][cur]
            H1 = Hs[dd][nxt]
            bY = rot(g, "rw", 0, 8)
            bYv = bY.re("p (j f) -> p j f", f=128)
            for h in range(8):
                j, p0 = h // 2, (h % 2) * 64
                P.mm(bYv[p0:p0 + 64, j, :], H0[p0:p0 + 64, j, :], QRb[dd][p0:p0 + 64, j, 128:256], start=True, stop=False)
                P.mm(bYv[p0:p0 + 64, j, :], Vtb[dd][:, h * 64:(h + 1) * 64], A1s[dd][:, h, 128:256], start=False, stop=False)
                P.mm(bYv[p0:p0 + 64, j, :], Un[dd][:, h, :], A2s[dd][:, h, 128:256], start=False, stop=True)
            evac(yo[dd], bYv)
            P.dma(dview(d["yT"][dd, :, :, c * CW:(c + 1) * CW], "j p f -> p j f"), yo[dd])
            bH = rot(g, "rw", 0, 8)
            bHv = bH[:, 0:256].re("p (j f) -> p j f", f=64)
            for h in range(8):
                j, p0 = h // 2, (h % 2) * 64
                P.mm(bHv[p0:p0 + 64, j, :], KHtb[dd][:, h * 64:(h + 1) * 64], Vtb[dd][:, h * 64:(h + 1) * 64], start=True, stop=False)
                P.mm(bHv[p0:p0 + 64, j, :], BHtb[dd][:, h * 64:(h + 1) * 64], Un[dd][:, h, :], start=False, stop=True)
            P.tt(H1, bHv, H0, ALU.add)
            P.tt(H1, H1, gC[:, dd, :, c:c + 1].bc([128, 4, 64]), ALU.mult)


def readout_setup(g, l, S):
    P = g.P
    e = l // 2
    d = g.d
    c = {}
    gn = S.sb("rw_gn", [128, 8])
    rows_to_cols(g, S, gn[:, 0:4], dview(d["rwkv_gn_w"][e], "(c p) -> c p", p=128), 4, "rw_gnws")
    rows_to_cols(g, S, gn[:, 4:8], dview(d["rwkv_gn_b"][e], "(c p) -> c p", p=128), 4, "rw_gnbs")
    blkf = S.sb("rw_blkf2", [128, 128])
    P.memset(blkf, 0.0)
    P.memset(blkf[0:64, 0:64], 1.0)
    P.memset(blkf[64:128, 64:128], 1.0)
    gne = S.sb("rw_gne", [128, 1])
    P.memset(gne, GN_EPS)
    nb = 2
    c.update(gn=gn, blkf=blkf, gne=gne, nb=nb, n=0)
    for nm, dt in (("yf", F32), ("yb", F32), ("bon", F32), ("g", BF16), ("sq", F32), ("mean", F32), ("var", F32)):
        c[nm] = [S.sb("ro_%s%d" % (nm, i), [128, 512], dt) for i in range(nb)]
    return c


def readout_tile(g, l, c, t0, nt, j, o_):
    P = g.P
    d = g.d
    i = c["n"] % c["nb"]
    c["n"] += 1
    y_, y2, b_, g_, s_, m_, v_ = c["yf"][i], c["yb"][i], c["bon"][i], c["g"][i], c["sq"][i], c["mean"][i], c["var"][i]
    gn, blkf, gne = c["gn"], c["blkf"], c["gne"]
    P.dma(y_[:, 0:nt], V(d["yT"].ap[0, j, :, t0:t0 + nt], None))
    P.dma(y2[:, 0:nt], V(d["yT"].ap[1, j, :, t0:t0 + nt], None))
    P.dma(b_[:, 0:nt], V(d["bonT"].ap[j, :, t0:t0 + nt], None))
    P.dma(g_[:, 0:nt], V(d["gT"].ap[j, :, t0:t0 + nt], None))
    P.tt(y_[:, 0:nt], y_[:, 0:nt], y2[:, 0:nt], ALU.add, q="pool")
    P.act(s_[:, 0:nt], y_[:, 0:nt], AF.Square)
    b1 = rot(g, "pj", 0, 8)
    b2 = rot(g, "pj", 0, 8)
    P.mm(b1[:, 0:nt], blkf, y_[:, 0:nt])
    P.mm(b2[:, 0:nt], blkf, s_[:, 0:nt])
    P.act(m_[:, 0:nt], b1[:, 0:nt], AF.Copy, scale=1.0 / HD)
    P.tt(s_[:, 0:nt], m_[:, 0:nt], m_[:, 0:nt], ALU.mult, q="pool")
    P.stt(v_[:, 0:nt], b2[:, 0:nt], 1.0 / HD, s_[:, 0:nt], ALU.mult, ALU.subtract)
    P.act(v_[:, 0:nt], v_[:, 0:nt], AF.Sqrt, bias=gne[:, 0:1])
    P.recip(v_[:, 0:nt], v_[:, 0:nt])
    P.tt(y_[:, 0:nt], y_[:, 0:nt], m_[:, 0:nt], ALU.subtract, q="pool")
    P.tt(y_[:, 0:nt], y_[:, 0:nt], v_[:, 0:nt], ALU.mult)
    P.ts(y_[:, 0:nt], y_[:, 0:nt], gn[:, j:j + 1], ALU.mult, gn[:, 4 + j:5 + j], ALU.add)
    P.tt(y_[:, 0:nt], y_[:, 0:nt], b_[:, 0:nt], ALU.add, q="pool")
    P.tt(o_, y_[:, 0:nt], g_[:, 0:nt], ALU.mult)


def emit_outproj(g, l):
    P = g.P
    even = (l % 2 == 0)
    ctx_out = l < g.depth - 1
    wo_d = dview(g.d["even_w_out" if even else "odd_w_out"][l // 2], "(kc p) n -> p kc n", p=128)
    xTd = g.d["xT"]
    with P.scope() as S:
        wo = S.sb("wo", [128, KC, D], BF16)
        stp = [S.sb("ost%d" % i, [128, KC, 512]) for i in range(2)]
        cnt = [0]
        for s_ in range(2):
            load_cast(g, S, stp, cnt, wo[:, :, s_ * 512:(s_ + 1) * 512], wo_d[:, :, s_ * 512:(s_ + 1) * 512], 512)
        oTt = [S.sb("op_oT%d" % i, [128, KC, 512], BF16) for i in range(2)]
        xt = [S.sb("op_xt%d" % i, [128, KC, 512]) for i in range(2)]
        ro = readout_setup(g, l, S) if even else None
        tiles = []
        if ctx_out:
            tiles.append((0, LCTX, 1))
        tiles += [(LCTX + t * 512, 512, 0) for t in range(g.nlat // 512)]

        def prep(ti):
            t0, nt, n = tiles[ti]
            o_ = oTt[ti % 2]
            x_ = xt[ti % 2]
            na = 4 if even else KC
            P.dma(o_[:, 0:na, 0:nt], dview(g.d["oT"][0:na, :, t0:t0 + nt], "c p t -> p c t"))
            P.dma(x_[:, :, 0:nt], dview(xTd[:, :, t0:t0 + nt], "c p t -> p c t"))
            if even:
                for j in range(4):
                    readout_tile(g, l, ro, t0, nt, j, o_[:, 4 + j, 0:nt])

        prep(0)
        for ti, (t0, nt, n) in enumerate(tiles):
            o_ = oTt[ti % 2]
            x_ = xt[ti % 2]
            if ti + 1 < len(tiles):
                prep(ti + 1)
            for c in range(KC):
                b = rot(g, "op", 0, 8)
                for fc in range(KC):
                    P.mm(b[:, 0:nt], wo[:, fc, c * 128:(c + 1) * 128], o_[:, fc, 0:nt], start=(fc == 0), stop=(fc == KC - 1))
                P.stt(x_[:, c, 0:nt], b[:, 0:nt], mcol(g.modT, l, 5, c, n), x_[:, c, 0:nt], ALU.mult, ALU.add)
            P.dma(dview(xTd[:, :, t0:t0 + nt], "c p t -> p c t"), x_[:, :, 0:nt])


WEIGHT_SPECS = [
    ("w_mod", [4, D, 9 * D]), ("b_mod", [4, 9 * D]), ("ffn_in", [4, 2, D, 2 * DFF]), ("ffn_out", [4, 2, DFF, D]),
    ("final_gain", [D]),
    ("odd_w_in", [2, D, 1536]), ("odd_qk_sw", [2, D, 1280]), ("odd_w_out", [2, D, D]), ("sink", [2, 16]),
    ("even_w_in", [2, D, 2688]), ("even_qk_sw", [2, D, 640]), ("even_w_out", [2, D, D]),
    ("q_gain", [2, 64]), ("q_gain_sw", [2, 64]), ("k_gain", [2, 64]), ("k_gain_sw", [2, 64]),
    ("rwkv_mu", [2, 2, 1920]), ("rwkv_w0", [2, 2, 512]), ("rwkv_w2", [2, 2, 64, 512]), ("rwkv_a0", [2, 2, 512]),
    ("rwkv_a2", [2, 2, 64, 512]), ("rwkv_g2", [2, 128, 512]), ("rwkv_k_k", [2, 512]), ("rwkv_k_a", [2, 512]),
    ("rwkv_r_k", [2, 8, 64]), ("rwkv_gn_w", [2, 512]), ("rwkv_gn_b", [2, 512]),
]


def default_plan(depth):
    plan = []
    for l in range(depth):
        plan += [("ffn1", l), ("mix", l), ("ffn2", l)]
    return plan


def build(nlat=4096, depth=4, plan=None, debug=False, rwdt=F32):
    nc = bass.Bass("TRN2", target_bir_lowering=False)
    es = ExitStack()
    with es:
        g = G()
        g.P = P = Prog(nc, es)
        g.nlat = nlat
        g.depth = depth
        g.ntok = LCTX + nlat
        g.rotc = {}
        g.d = {}
        g.d["x"] = P.dram("x", [nlat, D], kind="ExternalInput")
        g.d["ctx"] = P.dram("ctx", [LCTX, D], kind="ExternalInput")
        g.d["cvec"] = P.dram("cvec", [2, D], kind="ExternalInput")
        g.d["cosT"] = P.dram("cosT", [128, nlat], kind="ExternalInput")
        g.d["sinT"] = P.dram("sinT", [128, nlat], kind="ExternalInput")
        for nm, shp in WEIGHT_SPECS:
            g.d[nm] = P.dram(nm, shp, kind="ExternalInput")
        g.d["out"] = P.dram("out", [nlat, D], kind="ExternalOutput")
        g.d["xT"] = P.dram("xT", [KC, 128, g.ntok], kind="Internal")
        sk = "ExternalOutput" if debug else "Internal"
        g.d["qT"] = P.dram("qT", [KC, 128, g.ntok], BF16, kind=sk)
        g.d["kT2"] = P.dram("kT2", [4, 128, g.ntok], BF16, kind=sk)
        g.d["Vd"] = P.dram("Vd", [g.ntok // 128, 128, 4 * 192], BF16, kind=sk)
        g.d["oT"] = P.dram("oT", [KC, 128, g.ntok], BF16, kind=sk)
        g.rwdt = rwdt
        nch = g.ntok // CW
        g.d["fT"] = P.dram("fT", [15, 128, g.ntok], kind=sk)
        g.d["QR"] = P.dram("QR", [2, 4, 128, nch * 2 * CW], rwdt, kind=sk)
        g.d["KH"] = P.dram("KH", [2, 4, 128, g.ntok], rwdt, kind=sk)
        g.d["BH"] = P.dram("BH", [2, 4, 128, g.ntok], rwdt, kind=sk)
        g.d["KHt"] = P.dram("KHt", [2, nch, 128, 512], rwdt, kind=sk)
        g.d["BHt"] = P.dram("BHt", [2, nch, 128, 512], rwdt, kind=sk)
        g.d["Vt"] = P.dram("Vt", [nch, 128, 512], rwdt, kind=sk)
        g.d["gT"] = P.dram("gT", [4, 128, g.ntok], BF16, kind=sk)
        g.d["bonT"] = P.dram("bonT", [4, 128, g.ntok], kind=sk)
        g.d["yT"] = P.dram("yT", [2, 4, 128, g.ntok], kind=sk)
        g.d["w_in_b"] = P.dram("w_in_b", [2 * depth, FC // 2, 128, KC * 512], BF16, kind="Internal")
        g.d["w_out_b"] = P.dram("w_out_b", [2 * depth, 128, FC * D], BF16, kind="Internal")
        setup_consts(g)
        g.bg = BgPrep(g)
        g.epsc = P.sb("epsc", [128, 1])
        P.memset(g.epsc, EPS)
        emit_mod(g)
        emit_in_transpose(g)
        for (st, l) in (plan if plan is not None else default_plan(depth)):
            if st == "ffn1":
                emit_ffn(g, l, 0)
            elif st == "ffn2":
                emit_ffn(g, l, 1)
            elif st == "mix":
                g.bg.limit_k = 2 * l + 2
                emit_attn_proj(g, l)
                emit_attn(g, l)
                if l % 2 == 0:
                    emit_rwkv(g, l)
                emit_outproj(g, l)
        finals = emit_final(g)
        P.emit(finals)
    return nc


def rope_tables(nlat):
    n = np.arange(nlat)
    row = (n // 64).astype(np.float32)
    col = (n % 64).astype(np.float32)
    nf = 16
    inv = (np.float32(10000.0) ** (-np.arange(nf, dtype=np.float32) / np.float32(nf))).astype(np.float32)
    ang = np.concatenate([row[:, None] * inv, col[:, None] * inv], axis=-1).astype(np.float32)
    cos, sin = np.cos(ang).astype(np.float32), np.sin(ang).astype(np.float32)
    d = np.arange(64)
    cosT = cos[:, d // 2].T
    sgn = np.where(d % 2 == 0, -1.0, 1.0).astype(np.float32)
    sinT = (sin[:, d // 2] * sgn[None, :]).T
    return (np.ascontiguousarray(np.concatenate([cosT, cosT], 0)), np.ascontiguousarray(np.concatenate([sinT, sinT], 0)))


def host_layout(inputs, b, nlat=4096):
    f = lambda a: np.ascontiguousarray(np.asarray(a, dtype=np.float32))
    sw = lambda w, n: f(w[..., (np.arange(n) ^ 1)])
    cosT, sinT = rope_tables(nlat)
    m = {
        "x": f(inputs["x"][b, :nlat]), "ctx": f(inputs["ctx"][b]),
        "cvec": f(np.stack([np.asarray(inputs["c"][b]), np.asarray(inputs["c_ctx"])])),
        "cosT": cosT, "sinT": sinT,
        "odd_qk_sw": sw(np.asarray(inputs["odd_w_in"])[:, :, :1280], 1280),
        "even_qk_sw": sw(np.asarray(inputs["even_w_in"])[:, :, :640], 640),
        "q_gain_sw": sw(np.asarray(inputs["q_gain"]), 64), "k_gain_sw": sw(np.asarray(inputs["k_gain"]), 64),
    }
    for nm, _ in WEIGHT_SPECS:
        if nm not in m:
            m[nm] = f(inputs[nm])
    return m


_NC_CACHE = {}


def kernel(**inputs):
    nlat = int(np.asarray(inputs["x"]).shape[1])
    nb = int(np.asarray(inputs["x"]).shape[0])
    if "nc" not in _NC_CACHE:
        _NC_CACHE["nc"] = build(nlat=nlat, depth=4)
    nc = _NC_CACHE["nc"]
    in_maps = [host_layout(inputs, b, nlat) for b in range(nb)]
    res = run_bass_kernel_spmd(nc, in_maps, core_ids=list(range(nb)))
    return np.stack([np.asarray(r["out"], dtype=np.float32) for r in res.results], axis=0)
```
